# Optimizing a Trainium2 kernel written in Bass

```python
import math
import jax, jax.numpy as jnp
from jax import lax
import numpy as np

D_MODEL = 1024
BATCH = 2
SEQ = 8192
DEPTH = 2

GRID_W = 64
CTX_LEN = 256
N_EVEN = (DEPTH + 1) // 2
N_ODD = DEPTH // 2
NORM_EPS = 1e-6
ROPE_THETA = 10000.0
Q_BLOCK = 128
N_MOD = 6
S5_WIDTH = D_MODEL // 2
S5_GROUP_DIM = 16
S5_GROUPS = S5_WIDTH // S5_GROUP_DIM
S5_STATE = 64
S5_DT_MIN = 1e-3
S5_DT_MAX = 1e-1
MLA_HEADS = 4
MLA_NOPE = 128
MLA_ROPE = 64
MLA_QK_DIM = MLA_NOPE + MLA_ROPE
MLA_V = 128
MLA_Q_RANK = 384
MLA_KV_RANK = 256
MLA_SCALE = MLA_QK_DIM ** -0.5
A_IN_WIDTH = S5_WIDTH + MLA_Q_RANK + MLA_KV_RANK + MLA_ROPE
A_OUT_WIDTH = S5_WIDTH + MLA_HEADS * MLA_V
WIN_HEADS = 16
WIN_KV_HEADS = 4
WIN_GROUP = WIN_HEADS // WIN_KV_HEADS
WIN_HEAD_DIM = 64
WINDOW = 128
WIN_SCALE = WIN_HEAD_DIM ** -0.5
C_IN_WIDTH = (WIN_HEADS + 2 * WIN_KV_HEADS) * WIN_HEAD_DIM
C_OUT_WIDTH = WIN_HEADS * WIN_HEAD_DIM
FFN_HIDDEN = -(-8 * D_MODEL // (3 * 256)) * 256

kernel_name = 'hybrid_s5_mla_window_dit_prefix'


def rms_norm(x, gain):
    xf = x.astype(jnp.float32)
    y = xf * lax.rsqrt(jnp.mean(xf * xf, axis=-1, keepdims=True) + NORM_EPS)
    return (y * gain.astype(jnp.float32)).astype(x.dtype)


def ada_modulation(cond, w, b):
    m = (jax.nn.silu(cond) @ w + b)[..., None, :]
    return jnp.split(m, N_MOD, axis=-1)


def modulate(x, gain, shift, scale):
    return rms_norm(x, gain) * (1 + scale) + shift


def grid_rope_tables(rows, rot_dim):
    row = jnp.repeat(jnp.arange(rows, dtype=jnp.int32), GRID_W)
    col = jnp.tile(jnp.arange(GRID_W, dtype=jnp.int32), rows)
    n_freq = rot_dim // 4
    inv_freq = ROPE_THETA ** (-jnp.arange(n_freq, dtype=jnp.float32) / n_freq)
    ang_r = row.astype(jnp.float32)[:, None] * inv_freq
    ang_c = col.astype(jnp.float32)[:, None] * inv_freq
    ang = jnp.concatenate([ang_r, ang_r, ang_c, ang_c], axis=-1)
    return jnp.cos(ang), jnp.sin(ang)


def apply_axial_rope(x, cos, sin):
    x1, x2, x3, x4 = jnp.split(x, 4, axis=-1)
    rot = jnp.concatenate([-x2, x1, -x4, x3], axis=-1)
    return x * cos[:, None, :].astype(x.dtype) + rot * sin[:, None, :].astype(x.dtype)


def swiglu(h, w_gate, w_up, w_down):
    return (jax.nn.silu(h @ w_gate) * (h @ w_up)) @ w_down


def zoh_discretize(lam_re, lam_im, log_step, b_re, b_im):
    lam = lax.complex(lam_re.astype(jnp.float32), lam_im.astype(jnp.float32))
    step = jnp.exp(log_step.astype(jnp.float32))[:, None]
    lam_bar = jnp.exp(lam * step)
    b = lax.complex(b_re.astype(jnp.float32), b_im.astype(jnp.float32))
    b_bar = ((lam_bar - 1.0) / lam)[..., None] * b
    return lam_bar, b_bar


def diag_scan(lam_bar, bu, h0):
    if h0 is not None:
        bu = bu.at[:, 0].add(lam_bar * h0)
    a = jnp.broadcast_to(lam_bar, (1, bu.shape[1]) + lam_bar.shape)

    def combine(left, right):
        a_l, b_l = left
        a_r, b_r = right
        return a_l * a_r, a_r * b_l + b_r

    _, h = lax.associative_scan(combine, (a, bu), axis=1)
    return h


def s5_bidirectional(u_ctx, u_lat, lam_re, lam_im, log_step, b_re, b_im, c_re, c_im,
                     d_skip, w_glu, b_glu, need_ctx):
    def grouped(u):
        return u.reshape(u.shape[0], u.shape[1], S5_GROUPS, S5_GROUP_DIM)

    uc, ul = grouped(u_ctx), grouped(u_lat)
    d = d_skip.reshape(S5_GROUPS, S5_GROUP_DIM).astype(jnp.float32)
    y_lat = ul.astype(jnp.float32) * d
    y_ctx = uc.astype(jnp.float32) * d if need_ctx else None
    for direction in range(2):
        rev = direction == 1
        lam_bar, b_bar = zoh_discretize(lam_re[direction], lam_im[direction], log_step[direction],
                                        b_re[direction], b_im[direction])
        c_mat = lax.complex(c_re[direction].astype(jnp.float32), c_im[direction].astype(jnp.float32))

        def drive(u):
            u = jnp.flip(u, axis=1) if rev else u
            return jnp.einsum('btgs,gps->btgp', u.astype(jnp.complex64), b_bar)

        def readout(h):
            y = jnp.real(jnp.einsum('btgp,gsp->btgs', h, c_mat))
            return jnp.flip(y, axis=1) if rev else y

        h_c = diag_scan(lam_bar, drive(uc), None)
        h_l = diag_scan(lam_bar, drive(ul), h_c[:, -1])
        y_lat = y_lat + readout(h_l)
        if need_ctx:
            y_ctx = y_ctx + readout(h_c)

    def glu(y):
        y = jax.nn.gelu(y.reshape(y.shape[0], y.shape[1], S5_WIDTH))
        return (y * jax.nn.sigmoid(y @ w_glu + b_glu)).astype(u_lat.dtype)

    return (glu(y_ctx) if need_ctx else None), glu(y_lat)


def mla_queries(cq, qa_norm, w_q_b, q_norm, cos, sin):
    b, t, _ = cq.shape
    q = (rms_norm(cq, qa_norm) @ w_q_b).reshape(b, t, MLA_HEADS, MLA_QK_DIM)
    q = rms_norm(q, q_norm)
    if cos is not None:
        q = jnp.concatenate([q[..., :MLA_NOPE], apply_axial_rope(q[..., MLA_NOPE:], cos, sin)], axis=-1)
    return q


def mla_keys_values(ckv, k_rope, kva_norm, w_kv_b, k_norm, cos, sin):
    b, t, _ = ckv.shape
    kv = (rms_norm(ckv, kva_norm) @ w_kv_b).reshape(b, t, MLA_HEADS, MLA_NOPE + MLA_V)
    k_nope, v = kv[..., :MLA_NOPE], kv[..., MLA_NOPE:]
    k_pe = jnp.broadcast_to(k_rope[:, :, None, :], (b, t, MLA_HEADS, MLA_ROPE))
    k = rms_norm(jnp.concatenate([k_nope, k_pe], axis=-1), k_norm)
    if cos is not None:
        k = jnp.concatenate([k[..., :MLA_NOPE], apply_axial_rope(k[..., MLA_NOPE:], cos, sin)], axis=-1)
    return k, v


def full_attention(q, k, v, scale):
    s = jnp.einsum('bqhd,bkhd->bhqk', q, k).astype(jnp.float32) * scale
    p = jax.nn.softmax(s, axis=-1).astype(v.dtype)
    return jnp.einsum('bhqk,bkhd->bqhd', p, v)


def blocked_dense_attention(q, k, v, scale):
    b, n, h, dq = q.shape
    nb = n // Q_BLOCK
    qb = q.reshape(b, nb, Q_BLOCK, h, dq).swapaxes(0, 1)
    out = lax.map(lambda q_blk: full_attention(q_blk, k, v, scale), qb)
    return out.swapaxes(0, 1).reshape(b, n, h * v.shape[-1])


def ssm_mla_mixer(h_ctx, h_lat, w_in, w_out, lam_re, lam_im, log_step, b_re, b_im, c_re, c_im,
                  d_skip, w_glu, b_glu, qa_norm, w_q_b, kva_norm, w_kv_b, q_norm, k_norm,
                  cos, sin, need_ctx):
    cuts = [S5_WIDTH, S5_WIDTH + MLA_Q_RANK, S5_WIDTH + MLA_Q_RANK + MLA_KV_RANK]
    u_c, cq_c, ckv_c, kr_c = jnp.split(h_ctx @ w_in, cuts, axis=-1)
    u_l, cq_l, ckv_l, kr_l = jnp.split(h_lat @ w_in, cuts, axis=-1)
    b, n, _ = h_lat.shape
    s5_c, s5_l = s5_bidirectional(u_c, u_l, lam_re, lam_im, log_step, b_re, b_im, c_re, c_im,
                                  d_skip, w_glu, b_glu, need_ctx)
    k_c, v_c = mla_keys_values(ckv_c, kr_c, kva_norm, w_kv_b, k_norm, None, None)
    k_l, v_l = mla_keys_values(ckv_l, kr_l, kva_norm, w_kv_b, k_norm, cos, sin)
    q_l = mla_queries(cq_l, qa_norm, w_q_b, q_norm, cos, sin)
    o_l = blocked_dense_attention(q_l, jnp.concatenate([k_c, k_l], axis=1),
                                  jnp.concatenate([v_c, v_l], axis=1), MLA_SCALE)
    out_l = jnp.concatenate([s5_l, o_l], axis=-1) @ w_out
    out_c = None
    if need_ctx:
        q_c = mla_queries(cq_c, qa_norm, w_q_b, q_norm, None, None)
        o_c = full_attention(q_c, k_c, v_c, MLA_SCALE)
        o_c = o_c.reshape(b, o_c.shape[1], MLA_HEADS * MLA_V)
        out_c = jnp.concatenate([s5_c, o_c], axis=-1) @ w_out
    return out_c, out_l


def sink_softmax(s, sink_logit):
    m = jnp.maximum(jnp.max(s, axis=-1, keepdims=True), sink_logit)
    p = jnp.exp(s - m)
    return p / (jnp.sum(p, axis=-1, keepdims=True) + jnp.exp(sink_logit - m))


def context_sink_attention(q, k, v, sink_l):
    s = jnp.einsum('bqkgd,bjkd->bkgqj', q, k).astype(jnp.float32) * WIN_SCALE
    w = sink_softmax(s, sink_l).astype(v.dtype)
    return jnp.einsum('bkgqj,bjkd->bqkgd', w, v)


def banded_sink_attention(q, k, v, k_ctx, v_ctx, sink_l):
    b, n = q.shape[0], q.shape[1]
    nb = n // Q_BLOCK
    band = 3 * Q_BLOCK
    qb = q.reshape(b, nb, Q_BLOCK, WIN_KV_HEADS, WIN_GROUP, WIN_HEAD_DIM).swapaxes(0, 1)
    pad = ((0, 0), (Q_BLOCK, Q_BLOCK), (0, 0), (0, 0))
    k_pad, v_pad = jnp.pad(k, pad), jnp.pad(v, pad)
    rel = (jnp.arange(band)[None, :] - Q_BLOCK) - jnp.arange(Q_BLOCK)[:, None]
    in_window = jnp.abs(rel) <= WINDOW

    def block(args):
        i, q_blk = args
        start = i * Q_BLOCK
        k_blk = lax.dynamic_slice_in_dim(k_pad, start, band, axis=1)
        v_blk = lax.dynamic_slice_in_dim(v_pad, start, band, axis=1)
        key_pos = start - Q_BLOCK + jnp.arange(band)
        valid = in_window & ((key_pos >= 0) & (key_pos < n))[None, :]
        s_loc = jnp.einsum('bqkgd,bjkd->bkgqj', q_blk, k_blk).astype(jnp.float32) * WIN_SCALE
        s_loc = jnp.where(valid, s_loc, -jnp.inf)
        s_ctx = jnp.einsum('bqkgd,bjkd->bkgqj', q_blk, k_ctx).astype(jnp.float32) * WIN_SCALE
        w = sink_softmax(jnp.concatenate([s_ctx, s_loc], axis=-1), sink_l).astype(v.dtype)
        return jnp.einsum('bkgqj,bjkd->bqkgd', w, jnp.concatenate([v_ctx, v_blk], axis=1))

    out = lax.map(block, (jnp.arange(nb), qb))
    return out.swapaxes(0, 1).reshape(b, n, C_OUT_WIDTH)


def window_gqa_mixer(h_ctx, h_lat, w_in, w_out, q_norm, k_norm, sink, cos, sin, need_ctx):
    q_w = WIN_HEADS * WIN_HEAD_DIM
    kv_w = WIN_KV_HEADS * WIN_HEAD_DIM
    sink_l = sink.astype(jnp.float32).reshape(WIN_KV_HEADS, WIN_GROUP)[None, :, :, None, None]

    def project(h, rope_cos, rope_sin):
        bb, t, _ = h.shape
        q, k, v = jnp.split(h @ w_in, [q_w, q_w + kv_w], axis=-1)
        k = rms_norm(k.reshape(bb, t, WIN_KV_HEADS, WIN_HEAD_DIM), k_norm)
        v = v.reshape(bb, t, WIN_KV_HEADS, WIN_HEAD_DIM)
        if rope_cos is not None:
            k = apply_axial_rope(k, rope_cos, rope_sin)
        return q, k, v

    def prep_queries(q, rope_cos, rope_sin):
        bb, t, _ = q.shape
        q = rms_norm(q.reshape(bb, t, WIN_HEADS, WIN_HEAD_DIM), q_norm)
        if rope_cos is not None:
            q = apply_axial_rope(q, rope_cos, rope_sin)
        return q.reshape(bb, t, WIN_KV_HEADS, WIN_GROUP, WIN_HEAD_DIM)

    q_c, k_c, v_c = project(h_ctx, None, None)
    q_l, k_l, v_l = project(h_lat, cos, sin)
    o_l = banded_sink_attention(prep_queries(q_l, cos, sin), k_l, v_l, k_c, v_c, sink_l)
    out_l = o_l @ w_out
    out_c = None
    if need_ctx:
        o_c = context_sink_attention(prep_queries(q_c, None, None), k_c, v_c, sink_l)
        out_c = o_c.reshape(o_c.shape[0], o_c.shape[1], C_OUT_WIDTH) @ w_out
    return out_c, out_l


def setup_inputs(seed: int = 0) -> dict:
    key = jax.random.key(seed)
    ks = list(jax.random.split(key, 48))

    def nrm(shape, scale):
        return jax.random.normal(ks.pop(), shape, jnp.float32) * scale

    def gain(shape):
        return 1.0 + nrm(shape, 0.05)

    G, P, S, W = S5_GROUPS, S5_STATE, S5_GROUP_DIM, S5_WIDTH
    lam_im_base = jnp.pi * jnp.arange(P, dtype=jnp.float32)
    return {
        'x': nrm((BATCH, SEQ, D_MODEL), 1.0),
        'c': nrm((BATCH, D_MODEL), 1.0),
        'ctx': nrm((BATCH, CTX_LEN, D_MODEL), 1.0),
        'c_ctx': nrm((D_MODEL,), 1.0),
        'ada_w': nrm((DEPTH, D_MODEL, N_MOD * D_MODEL), 0.5 * D_MODEL ** -0.5),
        'ada_b': nrm((DEPTH, N_MOD * D_MODEL), 0.02),
        'norm_mix': gain((DEPTH, D_MODEL)),
        'norm_ffn': gain((DEPTH, D_MODEL)),
        'ffn_w_gate': nrm((DEPTH, D_MODEL, FFN_HIDDEN), D_MODEL ** -0.5),
        'ffn_w_up': nrm((DEPTH, D_MODEL, FFN_HIDDEN), D_MODEL ** -0.5),
        'ffn_w_down': nrm((DEPTH, FFN_HIDDEN, D_MODEL), FFN_HIDDEN ** -0.5),
        'a_w_in': nrm((N_EVEN, D_MODEL, A_IN_WIDTH), D_MODEL ** -0.5),
        'a_w_out': nrm((N_EVEN, A_OUT_WIDTH, D_MODEL), A_OUT_WIDTH ** -0.5),
        's5_lam_re': -0.5 + nrm((N_EVEN, 2, G, P), 0.01),
        's5_lam_im': lam_im_base + nrm((N_EVEN, 2, G, P), 0.01),
        's5_log_step': jax.random.uniform(ks.pop(), (N_EVEN, 2, G), jnp.float32,
                                          math.log(S5_DT_MIN), math.log(S5_DT_MAX)),
        's5_b_re': nrm((N_EVEN, 2, G, P, S), (2 * S) ** -0.5),
        's5_b_im': nrm((N_EVEN, 2, G, P, S), (2 * S) ** -0.5),
        's5_c_re': nrm((N_EVEN, 2, G, S, P), (2 * P) ** -0.5),
        's5_c_im': nrm((N_EVEN, 2, G, S, P), (2 * P) ** -0.5),
        's5_d': nrm((N_EVEN, W), 1.0),
        's5_w_glu': nrm((N_EVEN, W, W), W ** -0.5),
        's5_b_glu': nrm((N_EVEN, W), 0.02),
        'mla_qa_norm': gain((N_EVEN, MLA_Q_RANK)),
        'mla_w_q_b': nrm((N_EVEN, MLA_Q_RANK, MLA_HEADS * MLA_QK_DIM), MLA_Q_RANK ** -0.5),
        'mla_kva_norm': gain((N_EVEN, MLA_KV_RANK)),
        'mla_w_kv_b': nrm((N_EVEN, MLA_KV_RANK, MLA_HEADS * (MLA_NOPE + MLA_V)), MLA_KV_RANK ** -0.5),
        'mla_q_norm': gain((N_EVEN, MLA_QK_DIM)),
        'mla_k_norm': gain((N_EVEN, MLA_QK_DIM)),
        'c_w_in': nrm((N_ODD, D_MODEL, C_IN_WIDTH), D_MODEL ** -0.5),
        'c_w_out': nrm((N_ODD, C_OUT_WIDTH, D_MODEL), C_OUT_WIDTH ** -0.5),
        'c_q_norm': gain((N_ODD, WIN_HEAD_DIM)),
        'c_k_norm': gain((N_ODD, WIN_HEAD_DIM)),
        'c_sink': nrm((N_ODD, WIN_HEADS), 0.5),
    }


def reference(x, c, ctx, c_ctx, ada_w, ada_b, norm_mix, norm_ffn, ffn_w_gate, ffn_w_up, ffn_w_down,
              a_w_in, a_w_out, s5_lam_re, s5_lam_im, s5_log_step, s5_b_re, s5_b_im, s5_c_re, s5_c_im,
              s5_d, s5_w_glu, s5_b_glu, mla_qa_norm, mla_w_q_b, mla_kva_norm, mla_w_kv_b,
              mla_q_norm, mla_k_norm, c_w_in, c_w_out, c_q_norm, c_k_norm, c_sink):
    rows = x.shape[1] // GRID_W
    cos_a, sin_a = grid_rope_tables(rows, MLA_ROPE)
    cos_c, sin_c = grid_rope_tables(rows, WIN_HEAD_DIM)
    h_ctx, h_lat = ctx, x
    for i in range(DEPTH):
        need_ctx = i < DEPTH - 1
        j = i // 2
        sh_l, sc_l, g_l, sh2_l, sc2_l, g2_l = ada_modulation(c, ada_w[i], ada_b[i])
        sh_c, sc_c, g_c, sh2_c, sc2_c, g2_c = ada_modulation(c_ctx, ada_w[i], ada_b[i])
        a_l = modulate(h_lat, norm_mix[i], sh_l, sc_l)
        a_c = modulate(h_ctx, norm_mix[i], sh_c, sc_c)
        if i % 2 == 0:
            o_c, o_l = ssm_mla_mixer(a_c, a_l, a_w_in[j], a_w_out[j], s5_lam_re[j], s5_lam_im[j],
                                     s5_log_step[j], s5_b_re[j], s5_b_im[j], s5_c_re[j], s5_c_im[j],
                                     s5_d[j], s5_w_glu[j], s5_b_glu[j], mla_qa_norm[j], mla_w_q_b[j],
                                     mla_kva_norm[j], mla_w_kv_b[j], mla_q_norm[j], mla_k_norm[j],
                                     cos_a, sin_a, need_ctx)
        else:
            o_c, o_l = window_gqa_mixer(a_c, a_l, c_w_in[j], c_w_out[j], c_q_norm[j], c_k_norm[j],
                                        c_sink[j], cos_c, sin_c, need_ctx)
        h_lat = h_lat + g_l * o_l
        h_lat = h_lat + g2_l * swiglu(modulate(h_lat, norm_ffn[i], sh2_l, sc2_l),
                                      ffn_w_gate[i], ffn_w_up[i], ffn_w_down[i])
        if need_ctx:
            h_ctx = h_ctx + g_c * o_c
            h_ctx = h_ctx + g2_c * swiglu(modulate(h_ctx, norm_ffn[i], sh2_c, sc2_c),
                                          ffn_w_gate[i], ffn_w_up[i], ffn_w_down[i])
    return h_lat
```

```python
import math
import numpy as np
import concourse.bass as bass
import concourse.mybir as mybir
from concourse.bass_utils import run_bass_kernel_spmd

F32 = mybir.dt.float32
BF = mybir.dt.bfloat16
I32 = mybir.dt.int32
AF = mybir.ActivationFunctionType
ALU = mybir.AluOpType
AX = mybir.AxisListType

D = 1024; KC = 8; SEQ = 8192; CTX = 256; NF = SEQ + CTX; NTF = NF // 128
OWN_LAT = 2304; NOWN = CTX + OWN_LAT; NTO = NOWN // 128
HID = 2816; HC = HID // 128
EPS = 1e-6
NCH = NF // 8
PE, ACT, DVE, POOL, SP = "pe", "act", "dve", "pool", "sp"
ENGS = (PE, ACT, DVE, POOL, SP)
NSLOT = {SP: 12, POOL: 12, ACT: 12}
SAME_SYNC = True
NB1 = 3
P1_SKEW = 9
P4_STAGED = True
P4_DIV = 2


class Op:
    __slots__ = ("idx", "eng", "fn", "deps", "need", "signal", "count", "is_dma", "slot", "val")


class Prog:
    def __init__(self, nc):
        self.nc = nc
        self.ops = []
        self.lastw = {}
        self.rd_eng = {}
        self.rd_dma = {}
        self.esem = {e: nc.alloc_semaphore("es_" + e) for e in ENGS}
        self.slots = {q: [nc.alloc_semaphore("ds_%s%d" % (q, i)) for i in range(n)] for q, n in NSLOT.items()}
        self.slot_n = {q: 0 for q in NSLOT}
        self.slot_last = {q: [None] * n for q, n in NSLOT.items()}
        self.cnt = {e: 0 for e in ENGS}
        self.seen_slot = {e: {} for e in ENGS}
        self.seen_idx = {e: {} for e in ENGS}
        self.emitted = 0
        self.epoch = 0
        self.last_op = {e: None for e in ENGS}

    capturing = None

    def capture(self, body):
        self.capturing = {"stages": [[]], "w": {}, "r": {}, "n": 0}
        body()
        st = [x for x in self.capturing["stages"] if x]
        self.capturing = None
        return st

    def _cap_add(self, eng, fn, r, w, dma):
        cap = self.capturing
        cap["n"] += 1
        eid = ("dma", eng, cap["n"]) if dma else eng
        conflict = False
        for k in list(r) + list(w):
            if k in cap["w"] and cap["w"][k] != eid:
                conflict = True
        for k in w:
            if k in cap["r"] and (cap["r"][k] - {eid}):
                conflict = True
        if conflict:
            cap["stages"].append([]); cap["w"] = {}; cap["r"] = {}
        cap["stages"][-1].append((eng, fn, tuple(r), tuple(w), dma))
        for k in w:
            cap["w"][k] = eid
        for k in r:
            cap["r"].setdefault(k, set()).add(eid)
        return None

    def run_staged(self, tiles, skew=1):
        info = []
        for st in tiles:
            lastw, lastt = {}, {}
            for si, stage in enumerate(st):
                for (eng, fn, r, w, dma) in stage:
                    for k in r:
                        lastt[k] = si
                    for k in w:
                        lastt[k] = si; lastw[k] = si
            info.append((lastw, lastt))
        active = []
        i = 0; step = 0
        while i < len(tiles) or active:
            if i < len(tiles) and step % skew == 0:
                active.append([i, 0]); i += 1
            progressed = False
            for ai, a in enumerate(active):
                t, si = a
                stage = tiles[t][si]
                ok = True
                for (eng, fn, r, w, dma) in stage:
                    for o in active[:ai]:
                        ow, ot = info[o[0]]
                        for k in w:
                            if ot.get(k, -1) >= o[1]:
                                ok = False
                        for k in r:
                            if ow.get(k, -1) >= o[1]:
                                ok = False
                    if not ok:
                        break
                if not ok:
                    continue
                for (eng, fn, r, w, dma) in stage:
                    self.add(eng, fn, r, w, dma)
                a[1] += 1
                progressed = True
            active = [a for a in active if a[1] < len(tiles[a[0]])]
            assert progressed or not active
            step += 1

    def add(self, eng, fn, r=(), w=(), dma=False):
        if self.capturing is not None:
            return self._cap_add(eng, fn, r, w, dma)
        op = Op()
        op.idx = len(self.ops); op.eng = eng; op.fn = fn; op.is_dma = dma
        op.signal = False; op.count = None; op.need = None; op.slot = None; op.val = None
        deps = {}
        for k in r:
            d = self.lastw.get(k)
            if d is not None:
                deps[d.idx] = d
        for k in w:
            d = self.lastw.get(k)
            if d is not None:
                deps[d.idx] = d
            for d in self.rd_eng.get(k, {}).values():
                deps[d.idx] = d
            for d in self.rd_dma.get(k, ()):
                deps[d.idx] = d
        if dma:
            q = eng
            n = self.slot_n[q]; self.slot_n[q] = n + 1
            si = n % NSLOT[q]
            op.slot = si; op.val = 16 * (n // NSLOT[q] + 1)
            prev = self.slot_last[q][si]
            if prev is not None:
                deps[prev.idx] = prev
            self.slot_last[q][si] = op
        op.deps = list(deps.values())
        for k in r:
            if dma:
                self.rd_dma.setdefault(k, []).append(op)
            else:
                self.rd_eng.setdefault(k, {})[eng] = op
        for k in w:
            self.lastw[k] = op
            self.rd_eng[k] = {}
            self.rd_dma[k] = []
        self.ops.append(op)
        if not dma:
            self.last_op[eng] = op
        return op

    def pe(self, fn, r=(), w=()): return self.add(PE, fn, r, w)
    def act(self, fn, r=(), w=()): return self.add(ACT, fn, r, w)
    def dve(self, fn, r=(), w=()): return self.add(DVE, fn, r, w)
    def pool(self, fn, r=(), w=()): return self.add(POOL, fn, r, w)

    def dma(self, q, out, in_, r=(), w=(), **kw):
        return self.add(q, lambda e: e.dma_start(out=out, in_=in_, **kw), r, w, dma=True)

    def barrier(self):
        deps = [o for o in self.last_op.values() if o is not None and o.idx >= self.epoch]
        for q in NSLOT:
            for o in self.slot_last[q]:
                if o is not None and o.idx >= self.epoch:
                    deps.append(o)
        saved = dict(self.last_op)
        for e in ENGS:
            op = self.add(e, None)
            op.deps = [d for d in deps]
        self.last_op = saved

    def emit(self):
        self.barrier()
        new = self.ops[self.emitted:]
        for op in new:
            need = []
            for d in sorted(op.deps, key=lambda o: o.idx):
                if d.idx < self.epoch:
                    continue
                if d.is_dma:
                    key = (d.eng, d.slot)
                    if self.seen_slot[op.eng].get(key, 0) >= d.val:
                        continue
                    self.seen_slot[op.eng][key] = d.val
                    need.append(d)
                else:
                    if d.eng == op.eng and not op.is_dma and op.fn is not None:
                        if op.eng == PE or not SAME_SYNC:
                            continue
                    if self.seen_idx[op.eng].get(d.eng, -1) >= d.idx:
                        continue
                    self.seen_idx[op.eng][d.eng] = d.idx
                    d.signal = True
                    need.append(d)
            op.need = need
        for op in new:
            if not op.is_dma and op.signal:
                self.cnt[op.eng] += 1
                op.count = self.cnt[op.eng]
        per = {e: [o for o in new if o.eng == e] for e in ENGS}
        esem, slots = self.esem, self.slots

        def run(e, lst):
            for op in lst:
                for d in op.need:
                    if d.is_dma:
                        e.wait_ge(slots[d.eng][d.slot], d.val)
                    else:
                        e.wait_ge(esem[d.eng], d.count)
                if op.fn is None:
                    continue
                ins = op.fn(e)
                if op.is_dma:
                    ins.then_inc(slots[op.eng][op.slot], 16)
                elif op.signal:
                    ins.then_inc(esem[op.eng], 1)

        with self.nc.Block() as blk:
            blk.tensor(lambda e: run(e, per[PE]))
            blk.scalar(lambda e: run(e, per[ACT]))
            blk.vector(lambda e: run(e, per[DVE]))
            blk.gpsimd(lambda e: run(e, per[POOL]))
            blk.sync(lambda e: run(e, per[SP]))
        self.emitted = len(self.ops)
        self.epoch = len(self.ops)


def bcast(ap, shape):
    return ap.to_broadcast(list(shape))


class Ctx:
    pass


def build(stage=99, debug=False):
    nc = bass.Bass("TRN2", target_bir_lowering=False)
    P = Prog(nc)
    K = Ctx()

    def din(name, shape, dt=F32):
        return nc.dram_tensor(name, list(shape), dt, kind="ExternalInput").ap()

    def dscr(name, shape, dt):
        return nc.dram_tensor(name, list(shape), dt, kind=("ExternalOutput" if debug else "Internal")).ap()

    I = {}
    I["xf"] = din("xf", [NF, D]); I["xo"] = din("xo", [NOWN, D]); I["oidx"] = din("oidx", [NOWN, 1], I32)
    I["cs"] = din("cs", [128, KC, 2])
    I["ropeA_f"] = din("ropeA_f", [NF, 2, 64]); I["ropeA_o"] = din("ropeA_o", [NOWN, 2, 64])
    I["ropeC_o"] = din("ropeC_o", [NOWN, 2, 64])
    I["ada_w"] = din("ada_w", [2, D, 6 * D]); I["ada_b"] = din("ada_b", [2, 6 * D])
    I["norm_mix"] = din("norm_mix", [2, D]); I["norm_ffn"] = din("norm_ffn", [2, D])
    I["ffn_w_gate"] = din("ffn_w_gate", [2, D, HID]); I["ffn_w_up"] = din("ffn_w_up", [2, D, HID])
    I["ffn_w_down"] = din("ffn_w_down", [2, HID, D])
    I["a_w_in"] = din("a_w_in", [D, 1216]); I["a_w_out"] = din("a_w_out", [D, D])
    I["s5t"] = din("s5t", [128, 32, 3]); I["s5b"] = din("s5b", [128, 32, 2, 16]); I["s5c"] = din("s5c", [128, 32, 2, 16])
    I["s5dd"] = din("s5dd", [128, 32]); I["s5_w_glu"] = din("s5_w_glu", [512, 512]); I["s5bg"] = din("s5bg", [128, 4])
    I["mla_qa_norm"] = din("mla_qa_norm", [1, 384]); I["mla_w_q_b"] = din("mla_w_q_b", [384, 768])
    I["mla_kva_norm"] = din("mla_kva_norm", [1, 256]); I["mla_w_kv_b"] = din("mla_w_kv_b", [256, 1024])
    I["mla_q_norm"] = din("mla_q_norm", [1, 192]); I["mla_k_norm"] = din("mla_k_norm", [1, 192])
    I["c_w_in"] = din("c_w_in", [D, 1536]); I["c_w_out"] = din("c_w_out", [D, D])
    I["c_q_norm"] = din("c_q_norm", [1, 64]); I["c_k_norm"] = din("c_k_norm", [1, 64]); I["c_sink"] = din("c_sink", [1, 16])
    out = nc.dram_tensor("out", [OWN_LAT, D], F32, kind="ExternalOutput").ap()

    S = {}
    S["mod"] = dscr("modsc", [2, 128, 2, 6 * D], F32)
    S["KTa"] = dscr("KTa", [128, 4, NF], BF); S["KTb"] = dscr("KTb", [64, 4, NF], BF)
    S["V"] = dscr("Vsc", [NF, 512], BF); S["U"] = dscr("Usc", [32, 128, NCH], BF)
    S["YT"] = dscr("YTsc", [512, 8, NCH], BF); S["S5"] = dscr("S5sc", [NF, 512], BF)
    S["H1"] = dscr("H1sc", [NOWN, D], F32); S["H2"] = dscr("H2sc", [NOWN, D], F32)
    S["H3"] = dscr("H3sc", [OWN_LAT, D], F32)

    ident = nc.alloc_sbuf_tensor("ident", [128, 128], BF)
    identf = nc.alloc_sbuf_tensor("identf", [128, 128], F32)
    ones_bf = nc.alloc_sbuf_tensor("ones_bf", [128, 128], BF)
    P.pool(lambda e: e.memset(identf[:], 0.0), w=["identf"])
    P.pool(lambda e: e.affine_select(out=identf[:], in_=identf[:], pattern=[[-1, 128]], compare_op=ALU.not_equal,
                                     fill=1.0, base=0, channel_multiplier=1), r=["identf"], w=["identf"])
    P.pool(lambda e: e.tensor_copy(out=ident[:], in_=identf[:]), r=["identf"], w=["ident"])
    P.pool(lambda e: e.memset(ones_bf[:], 1.0), w=["ones_bf"])
    K.ident, K.identf, K.ones_bf = ident, identf, ones_bf

    phase0_mod(nc, P, I, S)
    if stage >= 1:
        phase1_full(nc, P, I, S, K)
    if stage >= 2:
        phase2_s5(nc, P, I, S, K)
    if stage >= 3:
        phase3_mla(nc, P, I, S, K)
        phase_ffn(nc, P, I, S, K, 0, S["H1"], S["H2"], 2, NTO)
    if stage >= 4:
        phase4_win(nc, P, I, S, K)
        phase_ffn(nc, P, I, S, K, 1, S["H3"], out, 0, OWN_LAT // 128)
    else:
        pass
    return nc


def phase0_mod(nc, P, I, S):
    with nc.sbuf_tensor("cst", [128, KC, 2], F32) as cst, nc.sbuf_tensor("sl", [128, KC, 2], F32) as sl, \
            nc.sbuf_tensor("SL", [128, 2, KC, 128], BF) as SL, \
            nc.sbuf_tensor("wb0", [128, KC, 512], BF) as wb0, nc.sbuf_tensor("wb1", [128, KC, 512], BF) as wb1, \
            nc.sbuf_tensor("bb0", [128, 512], F32) as bb0, nc.sbuf_tensor("bb1", [128, 512], F32) as bb1, \
            nc.sbuf_tensor("gmix", [128, D], F32) as gmix, nc.sbuf_tensor("gffn", [128, D], F32) as gffn, \
            nc.sbuf_tensor("modt", [128, 2, 6 * D], F32) as modt, \
            nc.psum_tensor("pm0", [128, 512], F32) as pm0, nc.psum_tensor("pm1", [128, 512], F32) as pm1:
        wb = [wb0, wb1]; bb = [bb0, bb1]; pm = [pm0, pm1]
        P.dma(SP, cst[:], I["cs"][:, :, :], w=["cst"])
        P.act(lambda e: e.activation(out=sl[:], in_=cst[:], func=AF.Silu), r=["cst"], w=["sl"])
        for t in range(2):
            for kc in range(KC):
                P.dve(lambda e, t=t, kc=kc: e.tensor_copy(out=SL[:, t, kc, :], in_=bcast(sl[:, kc, t:t + 1], [128, 128])),
                      r=["sl"], w=["SL"])
        n = 0
        for layer in range(2):
            P.dma(SP, gmix[:], bcast(I["norm_mix"][layer:layer + 1, :], [128, D]), w=["gmix"])
            P.dma(SP, gffn[:], bcast(I["norm_ffn"][layer:layer + 1, :], [128, D]), w=["gffn"])
            wv = I["ada_w"][layer].rearrange("(kc p) n -> p kc n", p=128)
            for j in range(12):
                b = n % 2; n += 1
                P.dma(POOL, wb[b][:], wv[:, :, j * 512:(j + 1) * 512], w=["wb%d" % b])
                P.dma(SP, bb[b][:], bcast(I["ada_b"][layer:layer + 1, j * 512:(j + 1) * 512], [128, 512]), w=["bb%d" % b])
                for t in range(2):
                    for kc in range(KC):
                        P.pe(lambda e, t=t, kc=kc, b=b: e.matmul(pm[t][:], lhsT=SL[:, t, kc, :], rhs=wb[b][:, kc, :],
                                                                  start=(kc == 0), stop=(kc == KC - 1)),
                             r=["SL", "wb%d" % b], w=["pm%d" % t])
                    P.dve(lambda e, t=t, b=b, j=j: e.tensor_tensor(out=modt[:, t, j * 512:(j + 1) * 512], in0=pm[t][:],
                                                                    in1=bb[b][:], op=ALU.add),
                          r=["pm%d" % t, "bb%d" % b], w=["modt"])
            for t in range(2):
                P.dve(lambda e, t=t: e.scalar_tensor_tensor(out=modt[:, t, D:2 * D], in0=modt[:, t, D:2 * D], scalar=1.0,
                                                             in1=gmix[:], op0=ALU.add, op1=ALU.mult),
                      r=["modt", "gmix"], w=["modt"])
                P.dve(lambda e, t=t: e.scalar_tensor_tensor(out=modt[:, t, 4 * D:5 * D], in0=modt[:, t, 4 * D:5 * D], scalar=1.0,
                                                             in1=gffn[:], op0=ALU.add, op1=ALU.mult),
                      r=["modt", "gffn"], w=["modt"])
            P.dma(ACT, S["mod"][layer], modt[:], r=["modt"], w=["modsc%d" % layer])
        P.emit()


def rstd_from_ssq(P, ssq, rstd, n, width, tag, sfx=""):
    P.dve(lambda e: e.tensor_scalar(out=rstd, in0=ssq, scalar1=1.0 / width, scalar2=EPS, op0=ALU.mult, op1=ALU.add),
          r=[tag + "ssq" + sfx], w=[tag + "rstd" + sfx])
    P.act(lambda e: e.activation(out=rstd, in_=rstd, func=AF.Sqrt), r=[tag + "rstd" + sfx], w=[tag + "rstd" + sfx])
    P.dve(lambda e: e.reciprocal(out=rstd, in_=rstd), r=[tag + "rstd" + sfx], w=[tag + "rstd" + sfx])


def norm_mod(P, x_ap, xkey, A_ap, B_ap, modkey, junk, ssq, rstd, tmp, a_bf, tag):
    P.act(lambda e: e.activation(out=junk, in_=x_ap, func=AF.Square), r=[xkey], w=[tag + "tmp"])
    P.dve(lambda e: e.reduce_sum(out=ssq, in_=junk, axis=AX.X), r=[tag + "tmp"], w=[tag + "ssq"])
    rstd_from_ssq(P, ssq, rstd, 1, D, tag)
    P.dve(lambda e: e.scalar_tensor_tensor(out=tmp, in0=x_ap, scalar=rstd, in1=A_ap, op0=ALU.mult, op1=ALU.mult),
          r=[xkey, tag + "rstd", modkey], w=[tag + "tmp"])
    P.pool(lambda e: e.tensor_tensor(out=a_bf, in0=tmp, in1=B_ap, op=ALU.add), r=[tag + "tmp", modkey], w=[tag + "a_bf"])


def transpose8(P, K, a_bf, pT, aT_out, tag, aTkey, pTkey=None):
    pTkey = pTkey or (tag + "pT")
    for kc in range(KC):
        P.pe(lambda e, kc=kc: e.transpose(pT[:, kc, :], a_bf[:, kc * 128:(kc + 1) * 128], K.ident[:]),
             r=[tag + "a_bf", "ident"], w=[pTkey])
    P.act(lambda e: e.activation(out=aT_out, in_=pT[:, :, :], func=AF.Copy), r=[pTkey], w=[aTkey])


def norm_mod_T(P, K, x_ap, xkey, A_ap, B_ap, modkey, junk, ssq, rstd, tmp, a_bf, pT, aT_out, tag, aTkey, pTkey=None):
    norm_mod(P, x_ap, xkey, A_ap, B_ap, modkey, junk, ssq, rstd, tmp, a_bf, tag)
    transpose8(P, K, a_bf, pT, aT_out, tag, aTkey, pTkey)


def run_pipeline(gens):
    active = []
    it = iter(gens)
    more = True
    while more or active:
        nxt = next(it, None) if more else None
        if nxt is None:
            more = False
        for g in list(active):
            try:
                next(g)
            except StopIteration:
                active.remove(g)
        if nxt is not None:
            try:
                next(nxt)
                active.append(nxt)
            except StopIteration:
                pass


def rope_inplace(P, eng_a, eng_b, x4, rp, t1, nh, rkeys, xkey, tkey):
    cosb = bcast(rp[:, 0:1, :], [128, nh, 64])
    for a in range(2):
        sl_ = slice(a * 32, (a + 1) * 32)
        sw = x4[:, :, sl_].rearrange("p h (w q) -> p h w q", w=2)[:, :, ::-1, :]
        t1v = t1[:, :, sl_].rearrange("p h (w q) -> p h w q", w=2)
        sinb = bcast(rp[:, 1:2, sl_], [128, nh, 32]).rearrange("p h (w q) -> p h w q", w=2)
        eng_a(lambda e, sw=sw, t1v=t1v, sinb=sinb: e.tensor_tensor(out=t1v, in0=sw, in1=sinb, op=ALU.mult),
              r=[xkey] + rkeys, w=[tkey])
    eng_b(lambda e: e.tensor_tensor(out=x4, in0=x4, in1=cosb, op=ALU.mult), r=[xkey, tkey] + rkeys, w=[xkey])
    eng_b(lambda e: e.tensor_tensor(out=x4, in0=x4, in1=t1, op=ALU.add), r=[xkey, tkey], w=[xkey])


def phase1_full(nc, P, I, S, K):
    from contextlib import ExitStack
    with ExitStack() as es:
        def sb(name, shape, dt):
            return es.enter_context(nc.sbuf_tensor(name, list(shape), dt))

        def ps(name, shape, dt=F32):
            return es.enter_context(nc.psum_tensor(name, list(shape), dt))
        Win = sb("Win", [128, KC, 832], BF); Wkvb = sb("Wkvb", [128, 2, 1024], BF)
        modv = sb("modv", [128, 2, 2 * D], F32)
        gkva = sb("gkva", [128, 256], F32); gk = sb("gk", [128, 192], F32)
        xt = [sb("xt%d" % i, [128, D], F32) for i in range(NB1)]
        rp = [sb("rp%d" % i, [128, 2, 64], F32) for i in range(NB1)]
        junk2 = [sb("junk%d" % i, [128, D], F32) for i in range(NB1)]; tmp2 = junk2
        a_bf2 = [sb("a_bf%d" % i, [128, D], BF) for i in range(NB1)]
        aT4 = [sb("aT4_%d" % i, [128, KC, 512], BF) for i in range(2)]
        UT = sb("UT", [128, 4, 8, NCH], BF)
        st2 = [sb("st%d" % i, [128, 16], F32) for i in range(NB1)]
        c_bf2 = [sb("c_bf%d" % i, [128, 256], BF) for i in range(NB1)]; cT2 = [sb("cT%d" % i, [128, 2, 128], BF) for i in range(NB1)]
        kf2 = [sb("kf%d" % i, [128, 4, 192], F32) for i in range(NB1)]; ksq2 = [sb("ksq%d" % i, [128, 4, 192], F32) for i in range(NB1)]
        t12 = [sb("t1_%d" % i, [128, 4, 64], F32) for i in range(NB1)]
        K_bf2 = [sb("K_bf%d" % i, [128, 4, 192], BF) for i in range(NB1)]; KTs = [sb("KTs%d" % i, [128, 8, 128], BF) for i in range(NB1)]
        V_bf = [sb("V_bf%d" % i, [128, 4, 128], BF) for i in range(NB1)]
        pTa = ps("pTa", [128, 8, 128], BF); pTk = ps("pTk", [128, 8, 128], BF); pcT = ps("pcT", [128, 8, 128], BF)
        pKV = ps("pKV", [128, 512]); pU = [ps("pU%d" % i, [128, 512]) for i in range(2)]
        pKVB = ps("pKVB", [128, 1024])

        wv = I["a_w_in"].rearrange("(kc p) n -> p kc n", p=128)
        P.dma(POOL, Win[:, :, 0:512], wv[:, :, 0:512], w=["Win"])
        P.dma(POOL, Win[:, :, 512:832], wv[:, :, 896:1216], w=["Win"])
        P.dma(POOL, Wkvb[:], I["mla_w_kv_b"].rearrange("(kc p) n -> p kc n", p=128), w=["Wkvb"])
        P.dma(SP, modv[:], S["mod"][0][:, :, 0:2 * D], r=["modsc0"], w=["modv"])
        P.dma(SP, gkva[:], bcast(I["mla_kva_norm"][0:1, :], [128, 256]), w=["gkva"])
        P.dma(SP, gk[:], bcast(I["mla_k_norm"][0:1, :], [128, 192]), w=["gk"])
        ropev = I["ropeA_f"]
        def p1_tile(tg, ti, b, aT, aTkey, typ, nb, cbase):
            rows = slice(tg * 128, (tg + 1) * 128)
            rb = b
            junk, tmp, a_bf, st, c_bf, cT, kf, ksq, t1, K_bf = (junk2[b], tmp2[b], a_bf2[b], st2[b], c_bf2[b], cT2[b], kf2[b],
                                                                  ksq2[b], t12[b], K_bf2[b])
            sfx = "_%d" % b
            P.dma(SP, xt[b][:], I["xf"][rows, :], w=["xt%d" % b])
            P.dma(SP, rp[rb][:], ropev[rows, :, :], w=["rp%d" % rb])
            norm_mod(P, xt[b][:], "xt%d" % b, modv[:, typ, D:2 * D], modv[:, typ, 0:D], "modv",
                     junk[:], st[:, 0:1], st[:, 1:2], tmp[:], a_bf[:], "p1" + sfx)
            transpose8(P, K, a_bf[:], pTa, aT[:, :, ti * 128:(ti + 1) * 128], "p1" + sfx, aTkey, pTkey="pTa")
            for kc in range(KC):
                P.pe(lambda e, kc=kc, aT=aT, ti=ti: e.matmul(pKV[:, 0:320], lhsT=aT[:, kc, ti * 128:(ti + 1) * 128],
                                                            rhs=Win[:, kc, 512:832], start=(kc == 0), stop=(kc == KC - 1)),
                     r=[aTkey, "Win"], w=["pKV"])
            P.act(lambda e: e.activation(out=junk[:, 0:256], in_=pKV[:, 0:256], func=AF.Square), r=["pKV"], w=["p1" + sfx + "tmp"])
            P.dve(lambda e: e.reduce_sum(out=st[:, 2:3], in_=junk[:, 0:256], axis=AX.X), r=["p1" + sfx + "tmp"], w=["cssq" + sfx])
            rstd_from_ssq(P, st[:, 2:3], st[:, 3:4], 1, 256, "c", sfx)
            P.dve(lambda e: e.scalar_tensor_tensor(out=c_bf[:], in0=pKV[:, 0:256], scalar=st[:, 3:4], in1=gkva[:],
                                                   op0=ALU.mult, op1=ALU.mult), r=["pKV", "crstd" + sfx, "gkva"], w=["c_bf" + sfx])
            P.act(lambda e: e.activation(out=kf[:, :, 128:192], in_=bcast(pKV[:, 256:320].unsqueeze(1), [128, 4, 64]),
                                         func=AF.Copy), r=["pKV"], w=["kf" + sfx])
            for k2 in range(2):
                P.pe(lambda e, k2=k2: e.transpose(pcT[:, k2, :], c_bf[:, k2 * 128:(k2 + 1) * 128], K.ident[:]),
                     r=["c_bf" + sfx, "ident"], w=["pcT"])
            P.dve(lambda e: e.tensor_copy(out=cT[:], in_=pcT[:, 0:2, :]), r=["pcT"], w=["cT" + sfx])
            for half in range(2):
                for k2 in range(2):
                    P.pe(lambda e, half=half, k2=k2: e.matmul(pKVB[:, half * 512:(half + 1) * 512], lhsT=cT[:, k2, :],
                                                              rhs=Wkvb[:, k2, half * 512:(half + 1) * 512],
                                                              start=(k2 == 0), stop=(k2 == 1)),
                         r=["cT" + sfx, "Wkvb"], w=["pKVB"])
            kvv = pKVB[:].rearrange("p (h c) -> p h c", h=4)
            P.act(lambda e, kvv=kvv: e.activation(out=kf[:, :, 0:128], in_=kvv[:, :, 0:128], func=AF.Copy), r=["pKVB"], w=["kf" + sfx])
            P.act(lambda e, kvv=kvv, b=b: e.activation(out=V_bf[b][:], in_=kvv[:, :, 128:256], func=AF.Copy),
                  r=["pKVB"], w=["V_bf%d" % b])
            P.dma(ACT, S["V"][rows, :], V_bf[b][:].rearrange("p h d -> p (h d)"), r=["V_bf%d" % b], w=["Vsc"])
            P.dve(lambda e: e.tensor_tensor(out=ksq[:], in0=kf[:], in1=kf[:], op=ALU.mult), r=["kf" + sfx], w=["ksq" + sfx])
            P.dve(lambda e: e.reduce_sum(out=st[:, 4:8], in_=ksq[:], axis=AX.X), r=["ksq" + sfx], w=["kssq" + sfx])
            rstd_from_ssq(P, st[:, 4:8], st[:, 8:12], 4, 192, "k", sfx)
            P.dve(lambda e: e.tensor_tensor(out=kf[:], in0=kf[:], in1=bcast(st[:, 8:12].unsqueeze(2), [128, 4, 192]),
                                            op=ALU.mult), r=["kf" + sfx, "krstd" + sfx], w=["kf" + sfx])
            P.pool(lambda e: e.tensor_tensor(out=kf[:], in0=kf[:], in1=bcast(gk[:].unsqueeze(1), [128, 4, 192]),
                                             op=ALU.mult), r=["kf" + sfx, "gk"], w=["kf" + sfx])
            rope_inplace(P, P.pool, P.dve, kf[:, :, 128:192], rp[rb], t1[:], 4, ["rp%d" % rb], "kf" + sfx, "t1" + sfx)
            P.act(lambda e: e.activation(out=K_bf[:], in_=kf[:], func=AF.Copy), r=["kf" + sfx], w=["K_bf" + sfx])
            for h in range(4):
                P.pe(lambda e, h=h: e.transpose(pTk[:, 2 * h, :], K_bf[:, h, 0:128], K.ident[:]), r=["K_bf" + sfx, "ident"], w=["pTk"])
                P.pe(lambda e, h=h: e.transpose(pTk[0:64, 2 * h + 1, :], K_bf[:, h, 128:192], K.ident[:]),
                     r=["K_bf" + sfx, "ident"], w=["pTk"])
            pv = pTk[:].rearrange("p (h two) t -> p h two t", two=2)
            kv_ = KTs[b][:].rearrange("p (h two) t -> p h two t", two=2)
            P.dve(lambda e, pv=pv, kv_=kv_: e.tensor_copy(out=kv_[:, :, 0, :], in_=pv[:, :, 0, :]), r=["pTk"], w=["KTs%d" % b])
            P.dve(lambda e, pv=pv, kv_=kv_: e.tensor_copy(out=kv_[0:64, :, 1, :], in_=pv[0:64, :, 1, :]), r=["pTk"], w=["KTs%d" % b])
            P.dma(ACT, S["KTa"][:, :, rows], kv_[:, :, 0, :], r=["KTs%d" % b], w=["KTa"])
            P.dma(ACT, S["KTb"][:, :, rows], kv_[0:64, :, 1, :], r=["KTs%d" % b], w=["KTb"])
            if ti == nb - 1:
                p1_u(aT, aTkey, nb, cbase)

        def p1_u(aT, aTkey, nb, cbase):
            ncol = nb * 16
            for fc in range(4):
                pu = pU[fc % 2]
                for kc in range(KC):
                    P.pe(lambda e, fc=fc, kc=kc, pu=pu, aT=aT, nb=nb: e.matmul(
                        pu[:, 0:nb * 128], lhsT=Win[:, kc, fc * 128:(fc + 1) * 128],
                        rhs=aT[:, kc, 0:nb * 128].rearrange("p (c j) -> p j c", j=8), start=(kc == 0), stop=(kc == KC - 1)),
                        r=[aTkey, "Win"], w=["pU%d" % (fc % 2)])
                P.act(lambda e, fc=fc, pu=pu, nb=nb, cbase=cbase, ncol=ncol: e.activation(
                    out=UT[:, fc, :, cbase:cbase + ncol], in_=pu[:, 0:nb * 128].rearrange("p (j c) -> p j c", j=8), func=AF.Copy),
                    r=["pU%d" % (fc % 2)], w=["UT"])

        batches = [(0, 2, 0, 1)] + [(2 + 4 * k, 4, 32 + 64 * k, 0) for k in range(16)]
        tiles = []
        nt = 0
        for bi, (t0, nb, cbase, typ) in enumerate(batches):
            aT = aT4[bi % 2]; aTkey = "aT4_%d" % (bi % 2)
            for ti in range(nb):
                tiles.append(P.capture(lambda t0=t0, ti=ti, nt=nt, aT=aT, aTkey=aTkey, typ=typ, nb=nb, cbase=cbase:
                                       p1_tile(t0 + ti, ti, nt % NB1, aT, aTkey, typ, nb, cbase))); nt += 1
        P.run_staged(tiles, skew=P1_SKEW)
        n = 0
        for fc in range(4):
            for gi in range(8):
                g = fc * 8 + gi
                q = SP if n % 2 == 0 else ACT; n += 1
                P.dma(q, S["U"][g].rearrange("(j s) c -> s j c", s=16), UT[gi * 16:(gi + 1) * 16, fc, :, :], r=["UT"], w=["Usc"])
        P.emit()


def _rope_tables(n_tok_lat):
    n = np.arange(n_tok_lat)
    row = (n // 64).astype(np.float32); col = (n % 64).astype(np.float32)
    inv = (10000.0 ** (-np.arange(16, dtype=np.float32) / 16)).astype(np.float32)
    ar = row[:, None] * inv; ac = col[:, None] * inv
    ang = np.concatenate([ar, ar, ac, ac], axis=-1).astype(np.float32)
    cos = np.cos(ang).astype(np.float32); sin = np.sin(ang).astype(np.float32)
    sgn = np.concatenate([-np.ones(16), np.ones(16), -np.ones(16), np.ones(16)]).astype(np.float32)
    return np.stack([cos, sin * sgn], axis=1)


def own_start(j):
    return min(max(2048 * j - 128, 0), SEQ - OWN_LAT)


def prep_core(inp, core, shared):
    b, j = core // 4, core % 4
    s = own_start(j)
    f = lambda a: np.ascontiguousarray(a, dtype=np.float32)
    m = dict(shared)
    m["xf"] = f(np.concatenate([inp["ctx"][b], inp["x"][b]], 0))
    m["xo"] = f(np.concatenate([inp["ctx"][b], inp["x"][b, s:s + OWN_LAT]], 0))
    m["oidx"] = np.concatenate([np.arange(CTX), CTX + s + np.arange(OWN_LAT)]).astype(np.int32).reshape(-1, 1)
    cs = np.stack([inp["c"][b].reshape(KC, 128).T, inp["c_ctx"].reshape(KC, 128).T], axis=-1)
    m["cs"] = f(cs)
    rl = shared["_rope_lat"]
    idr = np.zeros((CTX, 2, 64), np.float32); idr[:, 0, :] = 1.0
    m["ropeA_f"] = f(np.concatenate([idr, rl], 0))
    m["ropeA_o"] = f(np.concatenate([idr, rl[s:s + OWN_LAT]], 0))
    m["ropeC_o"] = m["ropeA_o"]
    del m["_rope_lat"]
    return m


def prep_shared(inp):
    f = lambda a: np.ascontiguousarray(a, dtype=np.float32)
    m = {}
    for k in ("ada_w", "ada_b", "norm_mix", "norm_ffn", "ffn_w_gate", "ffn_w_up", "ffn_w_down"):
        m[k] = f(inp[k])
    for k in ("a_w_in", "a_w_out", "s5_w_glu", "mla_w_q_b", "mla_w_kv_b", "c_w_in", "c_w_out"):
        m[k] = f(inp[k][0])
    for k in ("mla_qa_norm", "mla_kva_norm", "mla_q_norm", "mla_k_norm", "c_q_norm", "c_k_norm", "c_sink"):
        m[k] = f(inp[k][0].reshape(1, -1))

    def unit(a):
        a = np.asarray(a)
        d_, g_, p_ = a.shape[:3]
        a = a.reshape((d_, g_ // 2, 2, p_) + a.shape[3:])
        a = np.moveaxis(a, (2, 3), (0, 1))
        return a.reshape((2 * p_, d_ * (g_ // 2)) + a.shape[4:])
    lre, lim, ls = inp["s5_lam_re"][0], inp["s5_lam_im"][0], inp["s5_log_step"][0]
    lsb = np.broadcast_to(ls[:, :, None], lre.shape)
    m["s5t"] = f(np.stack([unit(lre), unit(lim), unit(lsb)], -1))
    m["s5b"] = f(np.stack([unit(inp["s5_b_re"][0]), unit(inp["s5_b_im"][0])], 2))
    cre = np.swapaxes(inp["s5_c_re"][0], 2, 3); cim = np.swapaxes(inp["s5_c_im"][0], 2, 3)
    m["s5c"] = f(np.stack([unit(cre), unit(cim)], 2))
    dd = np.asarray(inp["s5_d"][0]).reshape(32, 16)
    m["s5dd"] = f(np.broadcast_to(dd.T[None, :, :], (8, 16, 32)).reshape(128, 32))
    m["s5bg"] = f(np.asarray(inp["s5_b_glu"][0]).reshape(4, 128).T)
    m["_rope_lat"] = _rope_tables(SEQ)
    return m


_NC_CACHE = {}


def kernel(**inputs):
    inp = {k: np.asarray(v) for k, v in inputs.items()}
    if "nc" not in _NC_CACHE:
        _NC_CACHE["nc"] = build()
    nc = _NC_CACHE["nc"]
    shared = prep_shared(inp)
    in_maps = [prep_core(inp, c, shared) for c in range(8)]
    res = run_bass_kernel_spmd(nc, in_maps, core_ids=list(range(8)))
    outp = np.empty((2, SEQ, D), np.float32)
    for c in range(8):
        b, j = c // 4, c % 4
        off = 2048 * j - own_start(j)
        outp[b, 2048 * j:2048 * (j + 1)] = res.results[c]["out"][off:off + 2048]
    return outp


def cmul(eng, o_r, o_i, a_r, a_i, b_r, b_i, t1, t2, keys):
    eng(lambda e: e.tensor_tensor(out=t1, in0=a_r, in1=b_r, op=ALU.mult), r=keys, w=keys)
    eng(lambda e: e.tensor_tensor(out=t2, in0=a_i, in1=b_i, op=ALU.mult), r=keys, w=keys)
    eng(lambda e: e.tensor_tensor(out=o_r, in0=t1, in1=t2, op=ALU.subtract), r=keys, w=keys)
    eng(lambda e: e.tensor_tensor(out=t1, in0=a_r, in1=b_i, op=ALU.mult), r=keys, w=keys)
    eng(lambda e: e.tensor_tensor(out=t2, in0=a_i, in1=b_r, op=ALU.mult), r=keys, w=keys)
    eng(lambda e: e.tensor_tensor(out=o_i, in0=t1, in1=t2, op=ALU.add), r=keys, w=keys)


def s5_scan(eng, vr, vi, n, L, MU, NMI, us, seed, tot, tmp, key, tkey, fused):
    s1, s2, b1, b2, b3, b4 = tmp
    kk = [key, tkey]
    u0 = us.start

    def op2(out, in0, in1, op):
        eng(lambda e: e.tensor_tensor(out=out, in0=in0, in1=in1, op=op), r=kk, w=kk)

    def cp(out, in_):
        eng(lambda e: e.tensor_copy(out=out, in_=in_), r=kk, w=kk)

    def fma(out, in0, sc, in1):
        eng(lambda e: e.scalar_tensor_tensor(out=out, in0=in0, scalar=sc, in1=in1, op0=ALU.mult, op1=ALU.add), r=kk, w=kk)

    def groups(cnt):
        if cnt >= 64:
            return [(slice(0, 2), 2), (slice(2, 4), 2)]
        return [(slice(0, 4), 4)]

    def mub(c, l, usl, cnt):
        uu = slice(u0 + usl.start, u0 + usl.stop)
        return bcast(MU[:, c, l, uu].unsqueeze(2), [128, usl.stop - usl.start, cnt])
    for l in range(L):
        st = 1 << l; cnt = n >> (l + 1)
        ar, ai = vr[:, :, st - 1::2 * st], vi[:, :, st - 1::2 * st]
        br, bi = vr[:, :, 2 * st - 1::2 * st], vi[:, :, 2 * st - 1::2 * st]
        for (usl, nu) in groups(cnt):
            if fused and cnt >= 64:
                for q in range(usl.start, usl.stop):
                    mr = MU[:, 0, l, u0 + q:u0 + q + 1]; mi = MU[:, 1, l, u0 + q:u0 + q + 1]; nmi = NMI[:, l, u0 + q:u0 + q + 1]
                    fma(br[:, q, :], ar[:, q, :], mr, br[:, q, :])
                    fma(br[:, q, :], ai[:, q, :], nmi, br[:, q, :])
                    fma(bi[:, q, :], ar[:, q, :], mi, bi[:, q, :])
                    fma(bi[:, q, :], ai[:, q, :], mr, bi[:, q, :])
                continue
            p1 = (b1 if nu == 2 else s1)[:, :, 0:cnt]; p2 = (b2 if nu == 2 else s2)[:, :, 0:cnt]
            mr, mi = mub(0, l, usl, cnt), mub(1, l, usl, cnt)
            op2(p1, ar[:, usl, :], mr, ALU.mult); op2(p2, ai[:, usl, :], mi, ALU.mult)
            op2(br[:, usl, :], br[:, usl, :], p1, ALU.add); op2(br[:, usl, :], br[:, usl, :], p2, ALU.subtract)
            op2(p1, ar[:, usl, :], mi, ALU.mult); op2(p2, ai[:, usl, :], mr, ALU.mult)
            op2(bi[:, usl, :], bi[:, usl, :], p1, ALU.add); op2(bi[:, usl, :], bi[:, usl, :], p2, ALU.add)
    lr, li = vr[:, :, n - 1:n], vi[:, :, n - 1:n]
    if tot is not None:
        cp(tot[0], lr); cp(tot[1], li)
    if seed is None:
        eng(lambda e: e.memset(lr, 0.0), r=kk, w=kk)
        eng(lambda e: e.memset(li, 0.0), r=kk, w=kk)
    else:
        cp(lr, seed[0]); cp(li, seed[1])
    for l in reversed(range(L)):
        st = 1 << l; cnt = n >> (l + 1)
        ar, ai = vr[:, :, st - 1::2 * st], vi[:, :, st - 1::2 * st]
        br, bi = vr[:, :, 2 * st - 1::2 * st], vi[:, :, 2 * st - 1::2 * st]
        for (usl, nu) in groups(cnt):
            if nu == 2:
                tr, ti = b3[:, :, 0:cnt], b4[:, :, 0:cnt]
            else:
                tr, ti = s1[:, :, 32:32 + cnt], s2[:, :, 32:32 + cnt]
            cp(tr, br[:, usl, :]); cp(ti, bi[:, usl, :])
            if fused and cnt >= 64:
                for q2, q in enumerate(range(usl.start, usl.stop)):
                    mr = MU[:, 0, l, u0 + q:u0 + q + 1]; mi = MU[:, 1, l, u0 + q:u0 + q + 1]; nmi = NMI[:, l, u0 + q:u0 + q + 1]
                    fma(br[:, q, :], br[:, q, :], mr, ar[:, q, :])
                    fma(br[:, q, :], ti[:, q2, :], nmi, br[:, q, :])
                    fma(bi[:, q, :], bi[:, q, :], mr, ai[:, q, :])
                    fma(bi[:, q, :], tr[:, q2, :], mi, bi[:, q, :])
            else:
                p1 = (b1 if nu == 2 else s1)[:, :, 0:cnt]; p2 = (b2 if nu == 2 else s2)[:, :, 0:cnt]
                mr, mi = mub(0, l, usl, cnt), mub(1, l, usl, cnt)
                op2(p1, tr, mr, ALU.mult); op2(p2, ti, mi, ALU.mult)
                op2(p1, p1, p2, ALU.subtract); op2(br[:, usl, :], p1, ar[:, usl, :], ALU.add)
                op2(p1, tr, mi, ALU.mult); op2(p2, ti, mr, ALU.mult)
                op2(p1, p1, p2, ALU.add); op2(bi[:, usl, :], p1, ai[:, usl, :], ALU.add)
            cp(ar[:, usl, :], tr); cp(ai[:, usl, :], ti)


def phase2_s5(nc, P, I, S, K):
    from contextlib import ExitStack
    TWO_PI = 2.0 * math.pi
    with ExitStack() as es0:
        def sbp(name, shape, dt):
            return es0.enter_context(nc.sbuf_tensor(name, list(shape), dt))
        TOEP = sbp("TOEP", [128, 32, 128], BF)
        WSTAB = sbp("WSTAB", [128, 32, 2, 2, 128], BF)
        WOUT = sbp("WOUT", [128, 32, 2, 128], BF)
        MU = sbp("MU", [128, 2, 10, 32], F32)
        NMI = sbp("NMI", [128, 10, 32], F32)
        with ExitStack() as es:
            def sb(name, shape, dt=F32):
                return es.enter_context(nc.sbuf_tensor(name, list(shape), dt))
            T3 = sb("T3", [128, 32, 3]); B4 = sb("B4", [128, 32, 2, 16]); C4 = sb("C4", [128, 32, 2, 16])
            dd = sb("dd", [128, 32])
            sm = {n: sb("sm_" + n, [128, 32]) for n in
                  ("step", "rho", "th", "tq", "tf", "r", "s1", "hs", "c1", "sn", "cs", "ar", "ai", "den", "nr", "ni",
                   "cr", "ci", "ir", "ii", "tA", "tB", "am1")}
            tiq = sb("tiq", [128, 32], I32)
            PW = sb("PW", [128, 2, 32, 9]); NW = sb("NW", [128, 2, 32, 9])
            QA = sb("QA", [128, 2, 32, 9]); QB = sb("QB", [128, 2, 32, 9])
            bbar = sb("bbar", [128, 2, 32, 16])
            XB = sb("XB", [128, 2, 32, 8, 16]); XC = sb("XC", [128, 2, 32, 9, 16])
            WST = sb("WST", [128, 2, 32, 128], BF)
            big1 = sb("big1", [128, 32, 9, 16]); big2 = sb("big2", [128, 32, 9, 16])
            maskF = sb("maskF", [128, 128]); maskB = sb("maskB", [128, 128])
            tg = sb("tg", [128, 128]); tb = sb("tb", [128, 128])
            ptr = es.enter_context(nc.psum_tensor("ptr", [128, 2, 128], BF))
            ptf2 = es.enter_context(nc.psum_tensor("ptf2", [128, 128], F32))
            ptb2 = es.enter_context(nc.psum_tensor("ptb2", [128, 128], F32))
            ptf = es.enter_context(nc.psum_tensor("ptf", [128, 128], F32))
            ptb = es.enter_context(nc.psum_tensor("ptb", [128, 128], F32))
            kk = ["tab"]

            def dv(fn): P.dve(fn, r=kk, w=kk)

            def ac(fn): P.act(fn, r=kk, w=kk)
            P.dma(SP, T3[:], I["s5t"][:, :, :], w=kk); P.dma(SP, B4[:], I["s5b"][:, :, :, :], w=kk)
            P.dma(SP, C4[:], I["s5c"][:, :, :, :], w=kk); P.dma(SP, dd[:], I["s5dd"][:, :], w=kk)
            P.pool(lambda e: e.memset(maskF[:], 1.0), r=kk, w=kk)
            P.pool(lambda e: e.affine_select(out=maskF[:].rearrange("p (t s) -> p t s", s=16), in_=maskF[:].rearrange("p (t s) -> p t s", s=16),
                                             pattern=[[16, 8], [0, 16]], compare_op=ALU.is_ge, fill=0.0, base=15,
                                             channel_multiplier=-1), r=kk, w=kk)
            P.pool(lambda e: e.memset(maskB[:], 1.0), r=kk, w=kk)
            P.pool(lambda e: e.affine_select(out=maskB[:].rearrange("p (t s) -> p t s", s=16), in_=maskB[:].rearrange("p (t s) -> p t s", s=16),
                                             pattern=[[-16, 8], [0, 16]], compare_op=ALU.is_ge, fill=0.0, base=0,
                                             channel_multiplier=1), r=kk, w=kk)
            Lr, Li, Ls = T3[:, :, 0], T3[:, :, 1], T3[:, :, 2]
            s_ = {k: v[:] for k, v in sm.items()}

            def tt(o, a, b, op): dv(lambda e: e.tensor_tensor(out=o, in0=a, in1=b, op=op))

            def ts(o, a, m, add): dv(lambda e: e.tensor_scalar(out=o, in0=a, scalar1=m, scalar2=add, op0=ALU.mult, op1=ALU.add))
            ac(lambda e: e.activation(out=s_["step"], in_=Ls, func=AF.Exp))
            tt(s_["tA"], Lr, s_["step"], ALU.mult)
            ac(lambda e: e.activation(out=s_["rho"], in_=s_["tA"], func=AF.Exp))
            tt(s_["th"], Li, s_["step"], ALU.mult)
            ts(s_["tq"], s_["th"], 1.0 / TWO_PI, 0.0)
            dv(lambda e: e.tensor_copy(out=tiq[:], in_=s_["tq"]))
            dv(lambda e: e.tensor_copy(out=s_["tf"], in_=tiq[:]))
            tt(s_["r"], s_["tq"], s_["tf"], ALU.subtract)
            ac(lambda e: e.activation(out=s_["s1"], in_=s_["r"], func=AF.Sin, scale=math.pi))
            ac(lambda e: e.activation(out=s_["hs"], in_=s_["r"], func=AF.Sin, scale=math.pi / 2))
            tt(s_["tA"], s_["hs"], s_["hs"], ALU.mult); ts(s_["c1"], s_["tA"], -2.0, 1.0)
            tt(s_["tA"], s_["s1"], s_["c1"], ALU.mult); ts(s_["sn"], s_["tA"], 2.0, 0.0)
            tt(s_["tA"], s_["s1"], s_["s1"], ALU.mult); ts(s_["cs"], s_["tA"], -2.0, 1.0)
            tt(s_["ar"], s_["rho"], s_["cs"], ALU.mult); tt(s_["ai"], s_["rho"], s_["sn"], ALU.mult)
            tt(s_["tA"], Lr, Lr, ALU.mult); tt(s_["tB"], Li, Li, ALU.mult); tt(s_["den"], s_["tA"], s_["tB"], ALU.add)
            dv(lambda e: e.reciprocal(out=s_["den"], in_=s_["den"]))
            ts(s_["am1"], s_["ar"], 1.0, -1.0)
            tt(s_["tA"], s_["am1"], Lr, ALU.mult); tt(s_["tB"], s_["ai"], Li, ALU.mult); tt(s_["nr"], s_["tA"], s_["tB"], ALU.add)
            tt(s_["tA"], s_["ai"], Lr, ALU.mult); tt(s_["tB"], s_["am1"], Li, ALU.mult); tt(s_["ni"], s_["tA"], s_["tB"], ALU.subtract)
            tt(s_["cr"], s_["nr"], s_["den"], ALU.mult); tt(s_["ci"], s_["ni"], s_["den"], ALU.mult)
            tt(s_["tA"], s_["rho"], s_["rho"], ALU.mult)
            dv(lambda e: e.reciprocal(out=s_["tA"], in_=s_["tA"]))
            tt(s_["ir"], s_["ar"], s_["tA"], ALU.mult); tt(s_["tB"], s_["ai"], s_["tA"], ALU.mult); ts(s_["ii"], s_["tB"], -1.0, 0.0)
            for (W_, xr, xi) in ((PW, s_["ar"], s_["ai"]), (NW, s_["ir"], s_["ii"])):
                dv(lambda e, W_=W_: e.memset(W_[:, 0, :, 0], 1.0)); dv(lambda e, W_=W_: e.memset(W_[:, 1, :, 0], 0.0))
                for k in range(1, 9):
                    cmul(P.dve, W_[:, 0, :, k], W_[:, 1, :, k], W_[:, 0, :, k - 1], W_[:, 1, :, k - 1], xr, xi, s_["tA"], s_["tB"], kk)
            dv(lambda e: e.tensor_copy(out=MU[:, 0, 0, :], in_=PW[:, 0, :, 8])); dv(lambda e: e.tensor_copy(out=MU[:, 1, 0, :], in_=PW[:, 1, :, 8]))
            for l in range(1, 10):
                cmul(P.dve, MU[:, 0, l, :], MU[:, 1, l, :], MU[:, 0, l - 1, :], MU[:, 1, l - 1, :], MU[:, 0, l - 1, :], MU[:, 1, l - 1, :],
                     s_["tA"], s_["tB"], kk)
            dv(lambda e: e.tensor_scalar(out=NMI[:], in0=MU[:, 1], scalar1=-1.0, scalar2=None, op0=ALU.mult))
            b1 = big1[:, :, 0, :]; b2 = big2[:, :, 0, :]
            cmul(P.dve, bbar[:, 0], bbar[:, 1], bcast(s_["cr"].unsqueeze(2), [128, 32, 16]), bcast(s_["ci"].unsqueeze(2), [128, 32, 16]),
                 B4[:, :, 0, :], B4[:, :, 1, :], b1, b2, kk)
            for c in range(2):
                dv(lambda e, c=c: e.tensor_copy(out=QA[:, c, 0:16, :], in_=NW[:, c, 0:16, :]))
                dv(lambda e, c=c: e.tensor_copy(out=QA[:, c, 16:32, :], in_=PW[:, c, 16:32, :]))
                dv(lambda e, c=c: e.tensor_copy(out=QB[:, c, 0:16, :], in_=PW[:, c, 0:16, :]))
                dv(lambda e, c=c: e.tensor_copy(out=QB[:, c, 16:32, :], in_=NW[:, c, 16:32, :]))
            sh8 = [128, 32, 8, 16]; sh9 = [128, 32, 9, 16]
            cmul(P.dve, XB[:, 0], XB[:, 1], bcast(QA[:, 0, :, 0:8].unsqueeze(3), sh8), bcast(QA[:, 1, :, 0:8].unsqueeze(3), sh8),
                 bcast(bbar[:, 0].unsqueeze(2), sh8), bcast(bbar[:, 1].unsqueeze(2), sh8), big1[:, :, 0:8, :], big2[:, :, 0:8, :], kk)
            cmul(P.dve, XC[:, 0], XC[:, 1], bcast(QB[:, 0, :, :].unsqueeze(3), sh9), bcast(QB[:, 1, :, :].unsqueeze(3), sh9),
                 bcast(C4[:, :, 0, :].unsqueeze(2), sh9), bcast(C4[:, :, 1, :].unsqueeze(2), sh9), big1[:], big2[:], kk)
            for c in range(2):
                dv(lambda e, c=c: e.tensor_copy(out=WST[:, c, 16:32, :].rearrange("p u (k s) -> p u k s", s=16), in_=XB[:, c, 16:32]))
            sh16 = [128, 16, 8, 16]
            cmul(P.dve, WST[:, 0, 0:16, :].rearrange("p u (k s) -> p u k s", s=16), WST[:, 1, 0:16, :].rearrange("p u (k s) -> p u k s", s=16),
                 bcast(PW[:, 0, 0:16, 7:8].unsqueeze(3), sh16), bcast(PW[:, 1, 0:16, 7:8].unsqueeze(3), sh16),
                 XB[:, 0, 0:16], XB[:, 1, 0:16], big1[:, 0:16, 0:8, :], big2[:, 0:16, 0:8, :], kk)
            dv(lambda e: e.tensor_copy(out=WOUT[:, 0:16, 0, :].rearrange("p u (k s) -> p u k s", s=16), in_=XC[:, 0, 0:16, 1:9, :]))
            dv(lambda e: e.tensor_scalar(out=WOUT[:, 0:16, 1, :].rearrange("p u (k s) -> p u k s", s=16), in0=XC[:, 1, 0:16, 1:9, :], scalar1=-1.0, scalar2=None, op0=ALU.mult))
            wob_r = WOUT[:, 16:32, 0, :].rearrange("p u (k s) -> p u k s", s=16)
            wob_i = WOUT[:, 16:32, 1, :].rearrange("p u (k s) -> p u k s", s=16)
            cmul(P.dve, wob_r, wob_i, bcast(PW[:, 0, 16:32, 8:9].unsqueeze(3), sh16), bcast(PW[:, 1, 16:32, 8:9].unsqueeze(3), sh16),
                 XC[:, 0, 16:32, 0:8, :], XC[:, 1, 16:32, 0:8, :], big1[:, 0:16, 0:8, :], big2[:, 0:16, 0:8, :], kk)
            dv(lambda e: e.tensor_scalar(out=wob_i, in0=wob_i, scalar1=-1.0, scalar2=None, op0=ALU.mult))
            P.pool(lambda e: e.memset(WSTAB[:], 0.0), r=kk, w=kk)
            for u in range(32):
                for c in range(2):
                    P.pe(lambda e, u=u, c=c: e.transpose(ptr[:, c, :], WST[:, c, u, :], K.ident[:]), r=kk + ["ident"], w=kk)
                dv(lambda e, u=u: e.tensor_copy(out=WSTAB[:, u, :, 0, 0:64], in_=ptr[:, :, 0:64]))
                dv(lambda e, u=u: e.tensor_copy(out=WSTAB[:, u, :, 1, 64:128], in_=ptr[:, :, 64:128]))
            for g in range(32):
                gp, g2 = g // 2, g % 2
                rows = slice(g2 * 64, g2 * 64 + 64)
                for (u, pt_, pt2_) in ((gp, ptf, ptf2), (16 + gp, ptb, ptb2)):
                    P.pe(lambda e, u=u, pt_=pt_, rows=rows: e.matmul(pt_[:], lhsT=XB[rows, 0, u].rearrange("p k s -> p (k s)"),
                                                                    rhs=XC[rows, 0, u, 0:8, :].rearrange("p k s -> p (k s)"),
                                                                    start=True, stop=True), r=kk, w=kk)
                    P.pe(lambda e, u=u, pt2_=pt2_, rows=rows: e.matmul(pt2_[:], lhsT=XB[rows, 1, u].rearrange("p k s -> p (k s)"),
                                                                      rhs=XC[rows, 1, u, 0:8, :].rearrange("p k s -> p (k s)"),
                                                                      start=True, stop=True), r=kk, w=kk)
                dv(lambda e: e.tensor_copy(out=tg[:], in_=ptf[:]))
                tt(tg[:], tg[:], ptf2[:], ALU.subtract)
                tt(tg[:], tg[:], maskF[:], ALU.mult)
                dv(lambda e: e.tensor_copy(out=tb[:], in_=ptb[:]))
                tt(tb[:], tb[:], ptb2[:], ALU.subtract)
                tt(tb[:], tb[:], maskB[:], ALU.mult)
                tt(tg[:], tg[:], tb[:], ALU.add)
                dv(lambda e, g=g: e.scalar_tensor_tensor(out=TOEP[:, g, :], in0=K.identf[:], scalar=dd[:, g:g + 1], in1=tg[:],
                                                         op0=ALU.mult, op1=ALU.add))
            P.emit()
        s5_main(nc, P, I, S, K, TOEP, WSTAB, WOUT, MU, NMI)


def s5_main(nc, P, I, S, K, TOEP, WSTAB, WOUT, MU, NMI):
    from contextlib import ExitStack
    blocks = [(0, 32), (32, 544), (544, 1056)]
    with ExitStack() as es:
        def sb(name, shape, dt=F32):
            return es.enter_context(nc.sbuf_tensor(name, list(shape), dt))
        Ug2 = [sb("Ug%d" % i, [128, 8, NCH], BF) for i in range(2)]
        SCr = sb("SCr", [128, 8, NCH]); SCi = sb("SCi", [128, 8, NCH])
        Hr = sb("Hr", [128, 8, NCH], BF); Hi = sb("Hi", [128, 8, NCH], BF)
        tmpall = sb("stmpall", [128, 2 * 1024])
        tsm = [sb("stsm%d" % i, [128, 4, 64]) for i in range(2)]
        YGs = [sb("YGs%d" % i, [128, NCH], BF) for i in range(2)]

        def bigt(i):
            return tmpall[:, 1024 * i:1024 * (i + 1)].rearrange("p (a b) -> p a b", a=2)
        tmp = [tsm[0][:], tsm[1][:], None, None, bigt(0), bigt(1)]
        tot = sb("tot", [128, 2, 2, 4, 1])
        pS = [es.enter_context(nc.psum_tensor("pS%d" % i, [128, 512], F32)) for i in range(2)]
        pY = [es.enter_context(nc.psum_tensor("pY%d" % i, [128, 512], F32)) for i in range(2)]
        cnt = {"n": 0, "yg": 0}

        def load_u(gb):
            P.dma(SP, Ug2[gb % 2][:], S["U"][8 * gb:8 * gb + 8].rearrange("g p c -> p g c"), r=["Usc"], w=["Ug%d" % (gb % 2)])

        def s_job(gb, d_):
            Ug = Ug2[gb % 2]; uk = "Ug%d" % (gb % 2)
            for i in range(4):
                q = d_ * 4 + i; u = d_ * 16 + 4 * gb + i
                for c in range(2):
                    SC = SCr if c == 0 else SCi
                    for (c0, c1) in blocks:
                        n = cnt["n"]; cnt["n"] += 1
                        p_ = pS[n % 2]; pk = "pS%d" % (n % 2)
                        P.pe(lambda e, p_=p_, u=u, c=c, i=i, c0=c0, c1=c1, Ug=Ug: e.matmul(
                            p_[:, 0:c1 - c0], lhsT=WSTAB[:, u, c, 0, :], rhs=Ug[:, 2 * i, c0:c1], start=True, stop=False),
                            r=[uk], w=[pk])
                        P.pe(lambda e, p_=p_, u=u, c=c, i=i, c0=c0, c1=c1, Ug=Ug: e.matmul(
                            p_[:, 0:c1 - c0], lhsT=WSTAB[:, u, c, 1, :], rhs=Ug[:, 2 * i + 1, c0:c1], start=False, stop=True),
                            r=[uk], w=[pk])
                        P.act(lambda e, p_=p_, SC=SC, q=q, c0=c0, c1=c1: e.activation(out=SC[:, q, c0:c1], in_=p_[:, 0:c1 - c0],
                                                                                  func=AF.Copy), r=[pk], w=["sc%d" % d_])

        def scan_job(gb, d_):
            us = slice(d_ * 16 + 4 * gb, d_ * 16 + 4 * gb + 4)
            qs = slice(d_ * 4, d_ * 4 + 4)
            vrc, vic = SCr[:, qs, 0:32], SCi[:, qs, 0:32]
            vrl, vil = SCr[:, qs, 32:NCH], SCi[:, qs, 32:NCH]
            if d_ == 1:
                vrc, vic, vrl, vil = vrc[:, :, ::-1], vic[:, :, ::-1], vrl[:, :, ::-1], vil[:, :, ::-1]
            tt_ = (tot[:, d_, 0], tot[:, d_, 1])
            s5_scan(P.dve, vrc, vic, 32, 5, MU, NMI, us, None, tt_, tmp, "sc%d" % d_, "stmp0", True)
            s5_scan(P.dve, vrl, vil, 1024, 10, MU, NMI, us, tt_, None, tmp, "sc%d" % d_, "stmp0", True)
            P.act(lambda e, qs=qs: e.activation(out=Hr[:, qs, :], in_=SCr[:, qs, :], func=AF.Copy), r=["sc%d" % d_], w=["Hr%d" % d_])
            P.pool(lambda e, qs=qs: e.tensor_copy(out=Hi[:, qs, :], in_=SCi[:, qs, :]), r=["sc%d" % d_], w=["Hi%d" % d_])

        def y_job(gb):
            Ug = Ug2[gb % 2]; uk = "Ug%d" % (gb % 2)
            for i8 in range(8):
                g = 8 * gb + i8; gpl = i8 // 2; g2 = i8 % 2
                rows = slice(g2 * 64, g2 * 64 + 64)
                yb = cnt["yg"] % 2; cnt["yg"] += 1
                yg = YGs[yb]; yk = "YGs%d" % yb
                for (c0, c1) in blocks:
                    n = cnt["n"]; cnt["n"] += 1
                    p_ = pY[n % 2]; pk = "pY%d" % (n % 2)
                    w_ = c1 - c0
                    P.pe(lambda e, p_=p_, g=g, i8=i8, c0=c0, c1=c1, w_=w_, Ug=Ug: e.matmul(p_[:, 0:w_], lhsT=TOEP[:, g, :], rhs=Ug[:, i8, c0:c1],
                                                                                        start=True, stop=False), r=[uk], w=[pk])
                    for d_ in range(2):
                        q = d_ * 4 + gpl; u = d_ * 16 + 4 * gb + gpl
                        P.pe(lambda e, p_=p_, u=u, q=q, c0=c0, c1=c1, w_=w_, rows=rows: e.matmul(
                            p_[:, 0:w_], lhsT=WOUT[rows, u, 0, :], rhs=Hr[rows, q, c0:c1], start=False, stop=False), r=["Hr%d" % d_], w=[pk])
                        P.pe(lambda e, p_=p_, u=u, q=q, c0=c0, c1=c1, w_=w_, rows=rows, d_=d_: e.matmul(
                            p_[:, 0:w_], lhsT=WOUT[rows, u, 1, :], rhs=Hi[rows, q, c0:c1], start=False, stop=(d_ == 1)), r=["Hi%d" % d_], w=[pk])
                    P.act(lambda e, p_=p_, yg=yg, c0=c0, c1=c1, w_=w_: e.activation(out=yg[:, c0:c1], in_=p_[:, 0:w_], func=AF.Gelu),
                          r=[pk], w=[yk])
                for t in range(8):
                    q_ = SP if t % 2 == 0 else ACT
                    P.dma(q_, S["YT"][g * 16:(g + 1) * 16, t, :], yg[t * 16:(t + 1) * 16, :], r=[yk], w=["YTsc"])

        load_u(0)
        s_job(0, 0); s_job(0, 1)
        for gb in range(4):
            if gb + 1 < 4:
                load_u(gb + 1)
            scan_job(gb, 0)
            if gb + 1 < 4:
                s_job(gb + 1, 0)
            scan_job(gb, 1)
            if gb + 1 < 4:
                s_job(gb + 1, 1)
            y_job(gb)
        P.emit()
    s5_glu(nc, P, I, S, K)


def s5_glu(nc, P, I, S, K):
    from contextlib import ExitStack
    CW = 256
    with ExitStack() as es:
        def sb(name, shape, dt=F32):
            return es.enter_context(nc.sbuf_tensor(name, list(shape), dt))
        YT = sb("YT", [128, 4, 8, NCH], BF)
        Wg = sb("Wglu", [128, 4, 512], BF); bg = sb("bglu", [128, 4])
        sg = [sb("sg%d" % i, [128, 4, CW], BF) for i in range(2)]
        so = [sb("so%d" % i, [128, 4, CW], BF) for i in range(2)]
        stm = [sb("stm%d" % i, [128, 512], BF) for i in range(3)]
        pz = [es.enter_context(nc.psum_tensor("pz%d" % i, [128, 4, CW], F32)) for i in range(2)]
        pt = [es.enter_context(nc.psum_tensor("ptg%d" % i, [128, 4, 128], BF)) for i in range(2)]
        for fc in range(4):
            P.dma(SP, YT[:, fc], S["YT"][fc * 128:(fc + 1) * 128], r=["YTsc"], w=["YT"])
        P.dma(POOL, Wg[:], I["s5_w_glu"].rearrange("(kc p) n -> p kc n", p=128), w=["Wglu"])
        P.dma(SP, bg[:], I["s5bg"][:, :], w=["bglu"])
        s5v = S["S5"].rearrange("(c t) f -> t c f", t=8)
        n = 0; nt_ = 0
        cblocks = [(0, 32)] + [(32 + CW * k, 32 + CW * (k + 1)) for k in range(1024 // CW)]
        for t in range(8):
            for (c0, c1) in cblocks:
                b = n % 2; n += 1
                w_ = c1 - c0
                for fo in range(4):
                    for fi in range(4):
                        P.pe(lambda e, b=b, fo=fo, fi=fi, t=t, c0=c0, c1=c1, w_=w_: e.matmul(
                            pz[b][:, fo, 0:w_], lhsT=Wg[:, fi, fo * 128:(fo + 1) * 128], rhs=YT[:, fi, t, c0:c1],
                            start=(fi == 0), stop=(fi == 3)), r=["YT", "Wglu"], w=["pz%d" % b])
                    P.act(lambda e, b=b, fo=fo, w_=w_: e.activation(out=sg[b][:, fo, 0:w_], in_=pz[b][:, fo, 0:w_], func=AF.Sigmoid,
                                                                    bias=bg[:, fo:fo + 1], scale=1.0), r=["pz%d" % b, "bglu"], w=["sg%d" % b])
                P.dve(lambda e, b=b, t=t, c0=c0, c1=c1, w_=w_: e.tensor_tensor(out=so[b][:, :, 0:w_], in0=YT[:, :, t, c0:c1],
                                                                              in1=sg[b][:, :, 0:w_], op=ALU.mult),
                      r=["YT", "sg%d" % b], w=["so%d" % b])
                for s0 in range(0, w_, 128):
                    sw = min(128, w_ - s0)
                    tb = nt_ % 2; sbi = nt_ % 3; nt_ += 1
                    for fo in range(4):
                        P.pe(lambda e, b=b, fo=fo, s0=s0, sw=sw, tb=tb: e.transpose(pt[tb][0:sw, fo, :], so[b][:, fo, s0:s0 + sw], K.ident[:]),
                             r=["so%d" % b, "ident"], w=["ptg%d" % tb])
                    P.dve(lambda e, sw=sw, tb=tb, sbi=sbi: e.tensor_copy(out=stm[sbi][0:sw, :], in_=pt[tb][0:sw].rearrange("p a f -> p (a f)")),
                          r=["ptg%d" % tb], w=["stm%d" % sbi])
                    P.dma(ACT, s5v[t, c0 + s0:c0 + s0 + sw, :], stm[sbi][0:sw, :], r=["stm%d" % sbi], w=["S5sc"])
        P.emit()


MLA_SCALE = 192 ** -0.5
WIN_SCALE = 64 ** -0.5


def phase3_mla(nc, P, I, S, K):
    from contextlib import ExitStack
    with ExitStack() as es0:
        def sbp(name, shape, dt):
            return es0.enter_context(nc.sbuf_tensor(name, list(shape), dt))
        QTa = sbp("QTa", [128, 4, NOWN], BF); QTb = sbp("QTb", [128, 4, NOWN], BF); OT = sbp("OT", [128, 4, NOWN], BF)
        modv = sbp("modv3", [128, 2, 3 * D], F32)
        P.dma(SP, modv[:], S["mod"][0][:, :, 0:3 * D], r=["modsc0"], w=["modv3"])
        with ExitStack() as es:
            def sb(name, shape, dt=F32):
                return es.enter_context(nc.sbuf_tensor(name, list(shape), dt))

            def ps(name, shape, dt=F32):
                return es.enter_context(nc.psum_tensor(name, list(shape), dt))
            Wqi = sb("Wqi", [128, KC, 384], BF); Wqb = sb("Wqb", [128, 3, 768], BF)
            gqa = sb("gqa", [128, 384]); gq = sb("gq", [128, 192])
            xt = [sb("xq%d" % i, [128, D]) for i in range(2)]; rp = [sb("rq%d" % i, [128, 2, 64]) for i in range(2)]
            tmp2 = [sb("tmpq%d" % i, [128, D]) for i in range(2)]; a_bf2 = [sb("a_bfq%d" % i, [128, D], BF) for i in range(2)]
            aT = [sb("aTq%d" % i, [128, KC, 128], BF) for i in range(2)]
            st2 = [sb("stq%d" % i, [128, 16]) for i in range(2)]
            cq_bf2 = [sb("cq_bf%d" % i, [128, 384], BF) for i in range(2)]; cqT2 = [sb("cqT%d" % i, [128, 3, 128], BF) for i in range(2)]
            qf2 = [sb("qf%d" % i, [128, 4, 192]) for i in range(2)]; qsq2 = [sb("qsq%d" % i, [128, 4, 192]) for i in range(2)]
            t12 = [sb("t1q%d" % i, [128, 4, 64]) for i in range(2)]
            Q_bf2 = [sb("Q_bf%d" % i, [128, 4, 192], BF) for i in range(2)]
            pTa = ps("pTaq", [128, 8, 128], BF); pQ = ps("pQ", [128, 384]); pcq = ps("pcq", [128, 3, 128], BF)
            pQB = ps("pQB", [128, 1024]); pTq = ps("pTq", [128, 8, 128], BF)
            wv = I["a_w_in"].rearrange("(kc p) n -> p kc n", p=128)
            P.dma(POOL, Wqi[:], wv[:, :, 512:896], w=["Wqi"])
            P.dma(POOL, Wqb[:], I["mla_w_q_b"].rearrange("(kc p) n -> p kc n", p=128), w=["Wqb"])
            P.dma(SP, gqa[:], bcast(I["mla_qa_norm"][0:1, :], [128, 384]), w=["gqa"])
            P.dma(SP, gq[:], bcast(I["mla_q_norm"][0:1, :], [128, 192]), w=["gq"])
            P.dve(lambda e: e.tensor_scalar(out=gq[:], in0=gq[:], scalar1=MLA_SCALE, scalar2=None, op0=ALU.mult), r=["gq"], w=["gq"])
            def q_tile(t):
                b = t % 2; typ = 1 if t < 2 else 0
                sfx = "_%d" % b
                tmp, a_bf, st, cq_bf, cqT, qf, qsq, t1, Q_bf = (tmp2[b], a_bf2[b], st2[b], cq_bf2[b], cqT2[b], qf2[b], qsq2[b], t12[b], Q_bf2[b])
                rows = slice(t * 128, (t + 1) * 128)
                P.dma(SP, xt[b][:], I["xo"][rows, :], w=["xq%d" % b])
                P.dma(SP, rp[b][:], I["ropeA_o"][rows, :, :], w=["rq%d" % b])
                norm_mod_T(P, K, xt[b][:], "xq%d" % b, modv[:, typ, D:2 * D], modv[:, typ, 0:D], "modv3",
                           tmp[:], st[:, 0:1], st[:, 1:2], tmp[:], a_bf[:], pTa, aT[b][:], "p3" + sfx, "aTq%d" % b, pTkey="p3pT")
                for kc in range(KC):
                    P.pe(lambda e, kc=kc, b=b: e.matmul(pQ[:], lhsT=aT[b][:, kc, :], rhs=Wqi[:, kc, :], start=(kc == 0), stop=(kc == KC - 1)),
                         r=["aTq%d" % b, "Wqi"], w=["pQ"])
                P.act(lambda e: e.activation(out=tmp[:, 0:384], in_=pQ[:], func=AF.Square), r=["pQ"], w=["p3" + sfx + "tmp"])
                P.dve(lambda e: e.reduce_sum(out=st[:, 2:3], in_=tmp[:, 0:384], axis=AX.X), r=["p3" + sfx + "tmp"], w=["qassq" + sfx])
                rstd_from_ssq(P, st[:, 2:3], st[:, 3:4], 1, 384, "qa", sfx)
                P.dve(lambda e: e.scalar_tensor_tensor(out=cq_bf[:], in0=pQ[:], scalar=st[:, 3:4], in1=gqa[:], op0=ALU.mult, op1=ALU.mult),
                      r=["pQ", "qarstd" + sfx, "gqa"], w=["cq_bf" + sfx])
                for k3 in range(3):
                    P.pe(lambda e, k3=k3: e.transpose(pcq[:, k3, :], cq_bf[:, k3 * 128:(k3 + 1) * 128], K.ident[:]), r=["cq_bf" + sfx, "ident"], w=["pcq"])
                P.dve(lambda e: e.tensor_copy(out=cqT[:], in_=pcq[:]), r=["pcq"], w=["cqT" + sfx])
                for (c0, c1) in ((0, 512), (512, 768)):
                    for k3 in range(3):
                        P.pe(lambda e, k3=k3, c0=c0, c1=c1: e.matmul(pQB[:, c0:c1], lhsT=cqT[:, k3, :], rhs=Wqb[:, k3, c0:c1],
                                                                     start=(k3 == 0), stop=(k3 == 2)), r=["cqT" + sfx, "Wqb"], w=["pQB"])
                P.act(lambda e: e.activation(out=qf[:], in_=pQB[:, 0:768].rearrange("p (h c) -> p h c", h=4), func=AF.Copy), r=["pQB"], w=["qf" + sfx])
                P.dve(lambda e: e.tensor_tensor(out=qsq[:], in0=qf[:], in1=qf[:], op=ALU.mult), r=["qf" + sfx], w=["qsq" + sfx])
                P.dve(lambda e: e.reduce_sum(out=st[:, 4:8], in_=qsq[:], axis=AX.X), r=["qsq" + sfx], w=["qssq" + sfx])
                rstd_from_ssq(P, st[:, 4:8], st[:, 8:12], 4, 192, "q", sfx)
                P.dve(lambda e: e.tensor_tensor(out=qf[:], in0=qf[:], in1=bcast(st[:, 8:12].unsqueeze(2), [128, 4, 192]), op=ALU.mult),
                      r=["qf" + sfx, "qrstd" + sfx], w=["qf" + sfx])
                P.pool(lambda e: e.tensor_tensor(out=qf[:], in0=qf[:], in1=bcast(gq[:].unsqueeze(1), [128, 4, 192]), op=ALU.mult),
                       r=["qf" + sfx, "gq"], w=["qf" + sfx])
                rope_inplace(P, P.pool, P.dve, qf[:, :, 128:192], rp[b], t1[:], 4, ["rq%d" % b], "qf" + sfx, "t1q" + sfx)
                P.act(lambda e: e.activation(out=Q_bf[:], in_=qf[:], func=AF.Copy), r=["qf" + sfx], w=["Q_bf" + sfx])
                for h in range(4):
                    P.pe(lambda e, h=h: e.transpose(pTq[:, 2 * h, :], Q_bf[:, h, 0:128], K.ident[:]), r=["Q_bf" + sfx, "ident"], w=["pTq"])
                    P.pe(lambda e, h=h: e.transpose(pTq[0:64, 2 * h + 1, :], Q_bf[:, h, 128:192], K.ident[:]), r=["Q_bf" + sfx, "ident"], w=["pTq"])
                pv = pTq[:].rearrange("p (h two) t -> p h two t", two=2)
                P.dve(lambda e, pv=pv, rows=rows: e.tensor_copy(out=QTa[:, :, rows], in_=pv[:, :, 0, :]), r=["pTq"], w=["QTa_%d" % t])
                P.dve(lambda e, pv=pv, rows=rows: e.tensor_copy(out=QTb[0:64, :, rows], in_=pv[0:64, :, 1, :]), r=["pTq"], w=["QTb_%d" % t])
            tiles = [P.capture(lambda t=t: q_tile(t)) for t in range(NTO)]
            K.q_stages = len(tiles[2])
            P.run_staged(tiles, skew=max(1, (len(tiles[2]) + 1) // 2))
            P.emit()
        with ExitStack() as es:
            def sb(name, shape, dt=F32):
                return es.enter_context(nc.sbuf_tensor(name, list(shape), dt))

            def ps(name, shape, dt=F32):
                return es.enter_context(nc.psum_tensor(name, list(shape), dt))
            KA = [sb("KA%d" % i, [128, NF], BF) for i in range(2)]; KB = [sb("KB%d" % i, [128, NF], BF) for i in range(2)]
            VH = [sb("VH%d" % i, [128, NTF, 128], BF) for i in range(2)]
            PT = [sb("PT%d" % i, [128, 512], BF) for i in range(4)]
            rec = sb("rec", [128, 512])
            pS = [ps("pSa%d" % i, [128, 512]) for i in range(2)]; pO = ps("pO", [128, 512]); pDen = ps("pDen", [128, 512])
            vsv = S["V"].rearrange("(kt p) (h d) -> p kt h d", p=128, h=4)
            blocks = [(0, 256, 2)] + [(256 + 512 * k, 256 + 512 * (k + 1), NTF) for k in range(4)] + [(2304, 2560, NTF)]
            npt = 0; nps = 0
            for h in range(4):
                hb = h % 2
                P.dma(SP, KA[hb][:], S["KTa"][:, h, :], r=["KTa"], w=["KA%d" % hb])
                P.dma(SP, KB[hb][0:64, :], S["KTb"][:, h, :], r=["KTb"], w=["KB%d" % hb])
                P.dma(POOL, VH[hb][:], vsv[:, :, h, :], r=["Vsc"], w=["VH%d" % hb])
                for (q0, q1, nkt) in blocks:
                    w_ = q1 - q0
                    pend = None
                    for kt in range(nkt + 1):
                        if kt < nkt:
                            sbi = nps % 2; nps += 1
                            ks = slice(kt * 128, (kt + 1) * 128)
                            P.pe(lambda e, sbi=sbi, hb=hb, ks=ks, q0=q0, q1=q1, w_=w_, h=h: e.matmul(
                                pS[sbi][:, 0:w_], lhsT=KA[hb][:, ks], rhs=QTa[:, h, q0:q1], start=True, stop=False),
                                r=["KA%d" % hb, "QTa"], w=["pSa%d" % sbi])
                            P.pe(lambda e, sbi=sbi, hb=hb, ks=ks, q0=q0, q1=q1, w_=w_, h=h: e.matmul(
                                pS[sbi][:, 0:w_], lhsT=KB[hb][0:64, ks], rhs=QTb[0:64, h, q0:q1], start=False, stop=True),
                                r=["KB%d" % hb, "QTb"], w=["pSa%d" % sbi])
                            pi = npt % 4; npt += 1
                            P.act(lambda e, sbi=sbi, pi=pi, w_=w_: e.activation(out=PT[pi][:, 0:w_], in_=pS[sbi][:, 0:w_], func=AF.Exp),
                                  r=["pSa%d" % sbi], w=["PT%d" % pi])
                            cur = (kt, pi)
                        else:
                            cur = None
                        if pend is not None:
                            k0, p0 = pend
                            P.pe(lambda e, k0=k0, p0=p0, hb=hb, w_=w_, nkt=nkt: e.matmul(
                                pO[:, 0:w_], lhsT=VH[hb][:, k0, :], rhs=PT[p0][:, 0:w_], start=(k0 == 0), stop=(k0 == nkt - 1)),
                                r=["VH%d" % hb, "PT%d" % p0], w=["pO"])
                            P.pe(lambda e, k0=k0, p0=p0, w_=w_, nkt=nkt: e.matmul(
                                pDen[:, 0:w_], lhsT=K.ones_bf[:], rhs=PT[p0][:, 0:w_], start=(k0 == 0), stop=(k0 == nkt - 1)),
                                r=["ones_bf", "PT%d" % p0], w=["pDen"])
                        pend = cur
                    P.dve(lambda e, w_=w_: e.reciprocal(out=rec[:, 0:w_], in_=pDen[:, 0:w_]), r=["pDen"], w=["rec"])
                    P.dve(lambda e, w_=w_, h=h, q0=q0, q1=q1: e.tensor_tensor(out=OT[:, h, q0:q1], in0=pO[:, 0:w_], in1=rec[:, 0:w_], op=ALU.mult),
                          r=["pO", "rec"], w=["OT"])
            P.emit()
        with ExitStack() as es:
            def sb(name, shape, dt=F32):
                return es.enter_context(nc.sbuf_tensor(name, list(shape), dt))

            def ps(name, shape, dt=F32):
                return es.enter_context(nc.psum_tensor(name, list(shape), dt))
            Wo = sb("Wo", [128, KC, D], BF)
            oi = [sb("oi%d" % i, [128, 1], I32) for i in range(2)]
            s5g = [sb("s5g%d" % i, [128, 512], BF) for i in range(2)]
            s5T = [sb("s5T%d" % i, [128, 4, 128], BF) for i in range(2)]
            xt = [sb("xo%d" % i, [128, D]) for i in range(2)]; tmp = sb("tmpo", [128, D]); h1 = [sb("h1_%d" % i, [128, D]) for i in range(2)]
            pT = ps("pTo", [128, 4, 128], BF); pOut = [ps("pOut%d" % i, [128, D]) for i in range(2)]
            P.dma(POOL, Wo[:], I["a_w_out"].rearrange("(kc p) n -> p kc n", p=128), w=["Wo"])
            for t in range(NTO):
                b = t % 2; typ = 1 if t < 2 else 0
                rows = slice(t * 128, (t + 1) * 128)
                P.dma(SP, xt[b][:], I["xo"][rows, :], w=["xo%d" % b])
                P.dma(SP, oi[b][:], I["oidx"][rows, :], w=["oi%d" % b])
                P.add(POOL, lambda e, b=b: e.indirect_dma_start(out=s5g[b][:], out_offset=None, in_=S["S5"][:, :],
                                                                in_offset=bass.IndirectOffsetOnAxis(ap=oi[b][:, 0:1], axis=0)),
                      r=["oi%d" % b, "S5sc"], w=["s5g%d" % b], dma=True)
                for fc in range(4):
                    P.pe(lambda e, b=b, fc=fc: e.transpose(pT[:, fc, :], s5g[b][:, fc * 128:(fc + 1) * 128], K.ident[:]),
                         r=["s5g%d" % b, "ident"], w=["pTo"])
                P.act(lambda e, b=b: e.activation(out=s5T[b][:], in_=pT[:], func=AF.Copy), r=["pTo"], w=["s5T%d" % b])
                for half in range(2):
                    hs = slice(half * 512, (half + 1) * 512)
                    for ch in range(8):
                        lhs = s5T[b][:, ch, :] if ch < 4 else OT[:, ch - 4, rows]
                        P.pe(lambda e, b=b, ch=ch, hs=hs, lhs=lhs: e.matmul(pOut[b][:, hs], lhsT=lhs, rhs=Wo[:, ch, hs], start=(ch == 0), stop=(ch == 7)),
                             r=["s5T%d" % b, "OT", "Wo"], w=["pOut%d" % b])
                P.dve(lambda e, b=b, typ=typ: e.tensor_tensor(out=tmp[:], in0=pOut[b][:], in1=modv[:, typ, 2 * D:3 * D], op=ALU.mult),
                      r=["pOut%d" % b, "modv3"], w=["tmpo"])
                P.pool(lambda e, b=b: e.tensor_tensor(out=h1[b][:], in0=tmp[:], in1=xt[b][:], op=ALU.add), r=["tmpo", "xo%d" % b], w=["h1_%d" % b])
                P.dma(ACT, S["H1"][rows, :], h1[b][:], r=["h1_%d" % b], w=["H1sc"])
            P.emit()


def phase_ffn(nc, P, I, S, K, layer, Hin, Hout, n_ctx_tiles, ntiles):
    from contextlib import ExitStack
    tag = "f%d" % layer
    NBT = 3
    WCH = [(0, 6), (6, 12), (12, 17), (17, 22)]
    with ExitStack() as es:
        def sb(name, shape, dt=F32):
            return es.enter_context(nc.sbuf_tensor(tag + name, list(shape), dt))

        def ps(name, shape, dt=F32):
            return es.enter_context(nc.psum_tensor(tag + name, list(shape), dt))
        Wg = sb("Wg", [128, KC, HID], BF); Wu = sb("Wu", [128, KC, HID], BF); Wd = sb("Wd", [128, HC, D], BF)
        modv = sb("modv", [128, 2, 3 * D])
        xn = [sb("xn%d" % i, [128, D]) for i in range(2)]; xr = sb("xr", [128, D]); tmp = sb("tmp", [128, D]); a_bf = sb("a_bf", [128, D], BF)
        aT4 = [sb("aT4_%d" % i, [128, KC, NBT * 128], BF) for i in range(2)]; actT = sb("actT", [128, HC, NBT * 128], BF)
        sg = [sb("sg%d" % i, [128, NBT * 128]) for i in range(2)]
        st = sb("st", [128, 4])
        pTa = ps("pTa", [128, 8, 128], BF)
        pG = [ps("pG%d" % i, [128, 512]) for i in range(2)]; pU = [ps("pU%d" % i, [128, 512]) for i in range(2)]
        pD = ps("pD", [128, D])
        P.dma(SP, modv[:], S["mod"][layer][:, :, 3 * D:6 * D], r=["modsc%d" % layer], w=[tag + "modv"])
        wgv = I["ffn_w_gate"][layer].rearrange("(kc p) n -> p kc n", p=128)
        wuv = I["ffn_w_up"][layer].rearrange("(kc p) n -> p kc n", p=128)
        wdv = I["ffn_w_down"][layer].rearrange("(j p) n -> p j n", p=128)
        for ci, (j0, j1) in enumerate(WCH):
            cs_ = slice(j0 * 128, j1 * 128)
            P.dma(POOL, Wg[:, :, cs_], wgv[:, :, cs_], w=[tag + "Wg%d" % ci])
            P.dma(POOL, Wu[:, :, cs_], wuv[:, :, cs_], w=[tag + "Wu%d" % ci])
        for ci, (j0, j1) in enumerate(WCH):
            P.dma(POOL, Wd[:, j0:j1, :], wdv[:, j0:j1, :], w=[tag + "Wd%d" % ci])
        wch_of = {}
        for ci, (j0, j1) in enumerate(WCH):
            for j in range(j0, j1):
                wch_of[j] = ci
        batches = []
        t0 = 0
        while t0 < ntiles:
            nb = min(NBT, ntiles - t0); batches.append((t0, nb)); t0 += nb

        def norm_tile(bi, ti):
            t0, nb = batches[bi]
            pb = bi % 2
            t = t0 + ti; typ = 1 if t < n_ctx_tiles else 0
            rows = slice(t * 128, (t + 1) * 128)
            xb_ = t % 2
            xk = tag + "xn%d" % xb_
            P.dma(SP, xn[xb_][:], Hin[rows, :], w=[xk])
            norm_mod_T(P, K, xn[xb_][:], xk, modv[:, typ, D:2 * D], modv[:, typ, 0:D], tag + "modv",
                       tmp[:], st[:, 0:1], st[:, 1:2], tmp[:], a_bf[:], pTa, aT4[pb][:, :, ti * 128:(ti + 1) * 128], tag,
                       tag + "aT4_%d" % pb)
        for ti in range(batches[0][1]):
            norm_tile(0, ti)
        for bi, (t0, nb) in enumerate(batches):
            pb = bi % 2
            aT = aT4[pb]; aTk = tag + "aT4_%d" % pb
            ncol = nb * 128
            nxt = list(range(batches[bi + 1][1])) if bi + 1 < len(batches) else []
            for j in range(HC):
                b = j % 2
                js = slice(j * 128, (j + 1) * 128)
                ci = wch_of[j]
                for kc in range(KC):
                    P.pe(lambda e, b=b, kc=kc, js=js, ncol=ncol, aT=aT: e.matmul(pG[b][:, 0:ncol], lhsT=Wg[:, kc, js], rhs=aT[:, kc, 0:ncol],
                                                                                 start=(kc == 0), stop=(kc == KC - 1)),
                         r=[tag + "Wg%d" % ci, aTk], w=[tag + "pG%d" % b])
                for kc in range(KC):
                    P.pe(lambda e, b=b, kc=kc, js=js, ncol=ncol, aT=aT: e.matmul(pU[b][:, 0:ncol], lhsT=Wu[:, kc, js], rhs=aT[:, kc, 0:ncol],
                                                                                 start=(kc == 0), stop=(kc == KC - 1)),
                         r=[tag + "Wu%d" % ci, aTk], w=[tag + "pU%d" % b])
                P.act(lambda e, b=b, ncol=ncol: e.activation(out=sg[b][:, 0:ncol], in_=pG[b][:, 0:ncol], func=AF.Silu),
                      r=[tag + "pG%d" % b], w=[tag + "sg%d" % b])
                P.dve(lambda e, b=b, j=j, ncol=ncol: e.tensor_tensor(out=actT[:, j, 0:ncol], in0=sg[b][:, 0:ncol], in1=pU[b][:, 0:ncol], op=ALU.mult),
                      r=[tag + "sg%d" % b, tag + "pU%d" % b], w=[tag + "actT"])
                if nxt and j in (2, 9, 16):
                    norm_tile(bi + 1, nxt.pop(0))
            while nxt:
                norm_tile(bi + 1, nxt.pop(0))
            for ti in range(nb):
                t = t0 + ti; typ = 1 if t < n_ctx_tiles else 0
                rows = slice(t * 128, (t + 1) * 128)
                P.dma(SP, xr[:], Hin[rows, :], w=[tag + "xr"])
                for half in range(2):
                    hs = slice(half * 512, (half + 1) * 512)
                    for j in range(HC):
                        P.pe(lambda e, j=j, ti=ti, hs=hs: e.matmul(pD[:, hs], lhsT=actT[:, j, ti * 128:(ti + 1) * 128], rhs=Wd[:, j, hs],
                                                                   start=(j == 0), stop=(j == HC - 1)),
                             r=[tag + "actT", tag + "Wd%d" % wch_of[j]], w=[tag + "pD"])
                P.dve(lambda e, typ=typ: e.tensor_tensor(out=tmp[:], in0=pD[:], in1=modv[:, typ, 2 * D:3 * D], op=ALU.mult),
                      r=[tag + "pD", tag + "modv"], w=[tag + "tmp"])
                P.pool(lambda e: e.tensor_tensor(out=xr[:], in0=xr[:], in1=tmp[:], op=ALU.add),
                       r=[tag + "tmp", tag + "xr"], w=[tag + "xr"])
                P.dma(ACT, Hout[rows, :], xr[:], r=[tag + "xr"], w=[tag + "hout"])
        P.emit()


def phase4_win(nc, P, I, S, K):
    from contextlib import ExitStack
    with ExitStack() as es:
        def sb(name, shape, dt=F32):
            return es.enter_context(nc.sbuf_tensor("w_" + name, list(shape), dt))

        def ps(name, shape, dt=F32):
            return es.enter_context(nc.psum_tensor("w_" + name, list(shape), dt))

        def two(name, shape, dt=F32):
            return [sb(name + str(i), shape, dt) for i in range(2)]
        Wi = sb("Wi", [128, KC, 1536], BF); Wo = sb("Wo", [128, 16, D], BF)
        modv = sb("modv", [128, 2, 3 * D])
        gq = sb("gq", [128, 64]); gk = sb("gk", [128, 64]); esk = sb("esk", [128, 16])
        mtmp = sb("mtmp", [128, 128]); maskP = sb("maskP", [128, 128], BF); maskN = sb("maskN", [128, 128], BF)
        KT1 = sb("KT1", [128, 4, NOWN], BF); V1 = sb("V1", [128, NTO, 256], BF)
        xt = [sb("xt%d" % i, [128, D]) for i in range(3)]; rp = [sb("rp%d" % i, [128, 2, 64]) for i in range(2)]
        tmp2 = two("tmp", [128, D]); a_bf2 = two("a_bf", [128, D], BF)
        aT = [sb("aT%d" % i, [128, KC, 128], BF) for i in range(2)]
        st2 = two("st", [128, 64])
        qf2 = two("qf", [128, 16, 64]); qsq2 = two("qsq", [128, 16, 64]); t12 = two("t1", [128, 16, 64])
        Q_bf2 = two("Q_bf", [128, 16, 64], BF); QT = [sb("QT%d" % i, [128, 16, 128], BF) for i in range(3)]
        kf2 = two("kf", [128, 4, 64]); ksq2 = two("ksq", [128, 4, 64]); K_bf2 = two("K_bf", [128, 4, 64], BF)
        PT = [sb("PT%d" % i, [128, 512], BF) for i in range(3)]
        o_bf2 = two("o_bf", [128, 16, 128], BF); den2 = two("den", [128, 512])
        pBig = ps("pBig", [128, 4, 512]); pKVx = ps("pKVx", [128, 512]); pT1 = ps("pT1", [128, 8, 128], BF)
        pS = [ps("pS%d" % i, [128, 512]) for i in range(2)]

        P.dma(POOL, Wi[:], I["c_w_in"].rearrange("(kc p) n -> p kc n", p=128), w=["w_Wi"])
        P.dma(POOL, Wo[0:64, :, :], I["c_w_out"].rearrange("(h d) n -> d h n", d=64), w=["w_Wo"])
        P.dma(SP, modv[:], S["mod"][1][:, :, 0:3 * D], r=["modsc1"], w=["w_modv"])
        P.dma(SP, gq[:], bcast(I["c_q_norm"][0:1, :], [128, 64]), w=["w_gq"])
        P.dma(SP, gk[:], bcast(I["c_k_norm"][0:1, :], [128, 64]), w=["w_gk"])
        P.dma(SP, esk[:], bcast(I["c_sink"][0:1, :], [128, 16]), w=["w_esk"])
        P.dve(lambda e: e.tensor_scalar(out=gq[:], in0=gq[:], scalar1=WIN_SCALE, scalar2=None, op0=ALU.mult), r=["w_gq"], w=["w_gq"])
        P.act(lambda e: e.activation(out=esk[:], in_=esk[:], func=AF.Exp), r=["w_esk"], w=["w_esk"])
        for (mk, pat, cm) in ((maskP, [[-1, 128]], 1), (maskN, [[1, 128]], -1)):
            P.pool(lambda e: e.memset(mtmp[:], 1.0), r=["w_mtmp"], w=["w_mtmp"])
            P.pool(lambda e, pat=pat, cm=cm: e.affine_select(out=mtmp[:], in_=mtmp[:], pattern=pat, compare_op=ALU.is_ge, fill=0.0,
                                                             base=0, channel_multiplier=cm), r=["w_mtmp"], w=["w_mtmp"])
            P.pool(lambda e, mk=mk: e.tensor_copy(out=mk[:], in_=mtmp[:]), r=["w_mtmp"], w=["w_mask"])
        cnt = {"ps": 0, "pt": 0}

        def qkv_tile(t):
            b = t % 2; xb = t % 3; typ = 1 if t < 2 else 0
            sfx = "_%d" % b
            tmp, a_bf, st, qf, qsq, t1, Q_bf, kf, ksq, K_bf = (tmp2[b], a_bf2[b], st2[b], qf2[b], qsq2[b], t12[b], Q_bf2[b],
                                                              kf2[b], ksq2[b], K_bf2[b])
            rows = slice(t * 128, (t + 1) * 128)
            xk = "w_xt%d" % xb
            P.dma(SP, xt[xb][:], S["H2"][rows, :], w=[xk])
            P.dma(SP, rp[b][:], I["ropeC_o"][rows, :, :], w=["w_rp%d" % b])
            norm_mod_T(P, K, xt[xb][:], xk, modv[:, typ, D:2 * D], modv[:, typ, 0:D], "w_modv",
                       tmp[:], st[:, 0:1], st[:, 1:2], tmp[:], a_bf[:], pT1, aT[b][:], "w_" + sfx, "w_aT%d" % b, pTkey="w_pT")
            for kc in range(KC):
                P.pe(lambda e, kc=kc, b=b: e.matmul(pKVx[:], lhsT=aT[b][:, kc, :], rhs=Wi[:, kc, 1024:1536],
                                                    start=(kc == 0), stop=(kc == KC - 1)), r=["w_aT%d" % b, "w_Wi"], w=["w_pKV"])
            if t >= 2:
                for cb in range(2):
                    for kc in range(KC):
                        P.pe(lambda e, cb=cb, kc=kc, b=b: e.matmul(pBig[:, cb, :], lhsT=aT[b][:, kc, :], rhs=Wi[:, kc, cb * 512:(cb + 1) * 512],
                                                                   start=(kc == 0), stop=(kc == KC - 1)), r=["w_aT%d" % b, "w_Wi"], w=["w_pA"])
            P.act(lambda e: e.activation(out=kf[:], in_=pKVx[:, 0:256].rearrange("p (h d) -> p h d", h=4), func=AF.Copy),
                  r=["w_pKV"], w=["w_kf" + sfx])
            P.act(lambda e, t=t: e.activation(out=V1[:, t, :], in_=pKVx[:, 256:512], func=AF.Copy), r=["w_pKV"], w=["w_V1_%d" % t])
            if t >= 2:
                P.act(lambda e: e.activation(out=qf[:], in_=pBig[:, 0:2, :].rearrange("p a (h d) -> p (a h) d", d=64), func=AF.Copy),
                      r=["w_pA"], w=["w_qf" + sfx])
            P.dve(lambda e: e.tensor_tensor(out=ksq[:], in0=kf[:], in1=kf[:], op=ALU.mult), r=["w_kf" + sfx], w=["w_ksq" + sfx])
            P.dve(lambda e: e.reduce_sum(out=st[:, 4:8], in_=ksq[:], axis=AX.X), r=["w_ksq" + sfx], w=["w_kssq" + sfx])
            rstd_from_ssq(P, st[:, 4:8], st[:, 8:12], 4, 64, "w_k", sfx)
            P.dve(lambda e: e.tensor_tensor(out=kf[:], in0=kf[:], in1=bcast(st[:, 8:12].unsqueeze(2), [128, 4, 64]), op=ALU.mult),
                  r=["w_kf" + sfx, "w_krstd" + sfx], w=["w_kf" + sfx])
            P.pool(lambda e: e.tensor_tensor(out=kf[:], in0=kf[:], in1=bcast(gk[:].unsqueeze(1), [128, 4, 64]), op=ALU.mult),
                   r=["w_kf" + sfx, "w_gk"], w=["w_kf" + sfx])
            rope_inplace(P, P.pool, P.dve, kf[:], rp[b], t1[:, 0:4, :], 4, ["w_rp%d" % b], "w_kf" + sfx, "w_t1" + sfx)
            P.act(lambda e: e.activation(out=K_bf[:], in_=kf[:], func=AF.Copy), r=["w_kf" + sfx], w=["w_K_bf" + sfx])
            for h in range(4):
                P.pe(lambda e, h=h: e.transpose(pT1[0:64, h, :], K_bf[:, h, :], K.ident[:]), r=["w_K_bf" + sfx, "ident"], w=["w_pT"])
            P.dve(lambda e, rows=rows: e.tensor_copy(out=KT1[0:64, :, rows], in_=pT1[0:64, 0:4, :]), r=["w_pT"], w=["w_KT1_%d" % t])
            if t < 2:
                return
            P.dve(lambda e: e.tensor_tensor(out=qsq[:], in0=qf[:], in1=qf[:], op=ALU.mult), r=["w_qf" + sfx], w=["w_qsq" + sfx])
            P.dve(lambda e: e.reduce_sum(out=st[:, 16:32], in_=qsq[:], axis=AX.X), r=["w_qsq" + sfx], w=["w_qssq" + sfx])
            rstd_from_ssq(P, st[:, 16:32], st[:, 32:48], 16, 64, "w_q", sfx)
            P.dve(lambda e: e.tensor_tensor(out=qf[:], in0=qf[:], in1=bcast(st[:, 32:48].unsqueeze(2), [128, 16, 64]), op=ALU.mult),
                  r=["w_qf" + sfx, "w_qrstd" + sfx], w=["w_qf" + sfx])
            P.pool(lambda e: e.tensor_tensor(out=qf[:], in0=qf[:], in1=bcast(gq[:].unsqueeze(1), [128, 16, 64]), op=ALU.mult),
                   r=["w_qf" + sfx, "w_gq"], w=["w_qf" + sfx])
            rope_inplace(P, P.pool, P.dve, qf[:], rp[b], t1[:], 16, ["w_rp%d" % b], "w_qf" + sfx, "w_t1" + sfx)
            P.act(lambda e: e.activation(out=Q_bf[:], in_=qf[:], func=AF.Copy), r=["w_qf" + sfx], w=["w_Q_bf" + sfx])
            for hh in range(2):
                for h in range(8):
                    P.pe(lambda e, h=h, hh=hh: e.transpose(pT1[0:64, h, :], Q_bf[:, 8 * hh + h, :], K.ident[:]),
                         r=["w_Q_bf" + sfx, "ident"], w=["w_pT"])
                P.dve(lambda e, t=t, hh=hh: e.tensor_copy(out=QT[t % 3][0:64, 8 * hh:8 * hh + 8, :], in_=pT1[0:64, :, :]),
                      r=["w_pT"], w=["w_QT%d" % (t % 3)])

        def attn_tile(t):
            i = t - 2
            b = t % 2
            sfx = "_%d" % b
            o_bf, den, tmp = o_bf2[b], den2[b], tmp2[b]
            qt = QT[t % 3]; qk = "w_QT%d" % (t % 3)
            keyt = [(0, None), (1, None)]
            if i >= 1:
                keyt.append((t - 1, maskP))
            keyt.append((t, None))
            if i <= NTO - 4:
                keyt.append((t + 1, maskN))
            nk = len(keyt)
            pO = pBig[0:64, 2, :]; pDen = pBig[0:64, 3, :]
            for kh in range(4):
                pend = None
                for n_ in range(nk + 1):
                    cur = None
                    if n_ < nk:
                        kt, mk = keyt[n_]
                        sbi = cnt["ps"] % 2; cnt["ps"] += 1
                        pi = cnt["pt"] % 3; cnt["pt"] += 1
                        P.pe(lambda e, sbi=sbi, kt=kt, kh=kh: e.matmul(pS[sbi][:], lhsT=KT1[0:64, kh, kt * 128:(kt + 1) * 128],
                                                                        rhs=qt[0:64, 4 * kh:4 * kh + 4, :], start=True, stop=True),
                             r=["w_KT1_%d" % kt, qk], w=["w_pS%d" % sbi])
                        P.act(lambda e, sbi=sbi, pi=pi: e.activation(out=PT[pi][:], in_=pS[sbi][:], func=AF.Exp),
                              r=["w_pS%d" % sbi], w=["w_PT%d" % pi])
                        if mk is not None:
                            P.dve(lambda e, pi=pi, mk=mk: e.tensor_tensor(out=PT[pi][:].rearrange("p (g q) -> p g q", g=4),
                                                                          in0=PT[pi][:].rearrange("p (g q) -> p g q", g=4),
                                                                          in1=bcast(mk[:].unsqueeze(1), [128, 4, 128]), op=ALU.mult),
                                  r=["w_PT%d" % pi, "w_mask"], w=["w_PT%d" % pi])
                        cur = (n_, kt, pi)
                    if pend is not None:
                        n0, k0, p0 = pend
                        P.pe(lambda e, n0=n0, k0=k0, p0=p0, kh=kh: e.matmul(pO, lhsT=V1[:, k0, kh * 64:(kh + 1) * 64], rhs=PT[p0][:],
                                                                            start=(n0 == 0), stop=(n0 == nk - 1)),
                             r=["w_V1_%d" % k0, "w_PT%d" % p0], w=["w_pB"])
                        P.pe(lambda e, n0=n0, p0=p0: e.matmul(pDen, lhsT=K.ones_bf[:, 0:64], rhs=PT[p0][:], start=(n0 == 0), stop=(n0 == nk - 1)),
                             r=["ones_bf", "w_PT%d" % p0], w=["w_pC"])
                    pend = cur
                P.dve(lambda e, kh=kh: e.tensor_tensor(out=den[0:64, :].rearrange("p (g q) -> p g q", g=4),
                                                       in0=pDen.rearrange("p (g q) -> p g q", g=4),
                                                       in1=bcast(esk[0:64, 4 * kh:4 * kh + 4].unsqueeze(2), [64, 4, 128]), op=ALU.add),
                      r=["w_pC", "w_esk"], w=["w_den" + sfx])
                P.dve(lambda e: e.reciprocal(out=den[0:64, :], in_=den[0:64, :]), r=["w_den" + sfx], w=["w_den" + sfx])
                P.dve(lambda e, kh=kh: e.tensor_tensor(out=o_bf[0:64, 4 * kh:4 * kh + 4, :].rearrange("p g q -> p (g q)"), in0=pO,
                                                       in1=den[0:64, :], op=ALU.mult), r=["w_pB", "w_den" + sfx], w=["w_o_bf" + sfx])
            for half in range(2):
                for h in range(16):
                    P.pe(lambda e, half=half, h=h: e.matmul(pBig[:, half, :], lhsT=o_bf[0:64, h, :], rhs=Wo[0:64, h, half * 512:(half + 1) * 512],
                                                            start=(h == 0), stop=(h == 15)), r=["w_o_bf" + sfx, "w_Wo"], w=["w_pA"])
            hb = i % 2
            P.dve(lambda e: e.tensor_tensor(out=tmp[:], in0=pBig[:, 0:2, :].rearrange("p a c -> p (a c)"), in1=modv[:, 0, 2 * D:3 * D], op=ALU.mult),
                  r=["w_pA", "w_modv"], w=["w_" + sfx + "tmp"])
            P.pool(lambda e, t=t: e.tensor_tensor(out=tmp[:], in0=tmp[:], in1=xt[t % 3][:], op=ALU.add),
                   r=["w_" + sfx + "tmp", "w_xt%d" % (t % 3)], w=["w_" + sfx + "tmp"])
            P.dma(ACT, S["H3"][i * 128:(i + 1) * 128, :], tmp[:], r=["w_" + sfx + "tmp"], w=["H3sc"])

        order = []
        for t in range(NTO):
            order.append(("q", t))
            if t - 1 >= 2:
                order.append(("a", t - 1))
        order.append(("a", NTO - 1))
        if P4_STAGED:
            tiles = [P.capture((lambda t=t: qkv_tile(t)) if k == "q" else (lambda t=t: attn_tile(t))) for (k, t) in order]
            nst = sorted(len(x) for x in tiles)[len(tiles) // 2]
            K.p4_info = (nst, [len(x) for x in tiles[:8]])
            P.run_staged(tiles, skew=max(1, nst // P4_DIV))
        else:
            for (k, t) in order:
                (qkv_tile if k == "q" else attn_tile)(t)
        P.emit()
```

```python
import math
import numpy as np
import concourse.bass as bass
import concourse.mybir as mybir
from concourse.bass_utils import run_bass_kernel_spmd

F32 = mybir.dt.float32
BF = mybir.dt.bfloat16
I32 = mybir.dt.int32
AF = mybir.ActivationFunctionType
ALU = mybir.AluOpType
AX = mybir.AxisListType

D = 1024; KC = 8; SEQ = 8192; CTX = 256; NF = SEQ + CTX; NTF = NF // 128
OWN_LAT = 2304; NOWN = CTX + OWN_LAT; NTO = NOWN // 128
HID = 2816; HC = HID // 128
EPS = 1e-6
NCH = NF // 8
PE, ACT, DVE, POOL, SP = "pe", "act", "dve", "pool", "sp"
ENGS = (PE, ACT, DVE, POOL, SP)
NSLOT = {SP: 12, POOL: 12, ACT: 12}
SAME_SYNC = True
NB1 = 3
P1_SKEW = 9
P4_STAGED = True
P4_DIV = 2


class Op:
    __slots__ = ("idx", "eng", "fn", "deps", "need", "signal", "count", "is_dma", "slot", "val")


class Prog:
    def __init__(self, nc):
        self.nc = nc
        self.ops = []
        self.lastw = {}
        self.rd_eng = {}
        self.rd_dma = {}
        self.esem = {e: nc.alloc_semaphore("es_" + e) for e in ENGS}
        self.slots = {q: [nc.alloc_semaphore("ds_%s%d" % (q, i)) for i in range(n)] for q, n in NSLOT.items()}
        self.slot_n = {q: 0 for q in NSLOT}
        self.slot_last = {q: [None] * n for q, n in NSLOT.items()}
        self.cnt = {e: 0 for e in ENGS}
        self.seen_slot = {e: {} for e in ENGS}
        self.seen_idx = {e: {} for e in ENGS}
        self.emitted = 0
        self.epoch = 0
        self.last_op = {e: None for e in ENGS}

    capturing = None

    def capture(self, body):
        self.capturing = {"stages": [[]], "w": {}, "r": {}, "n": 0}
        body()
        st = [x for x in self.capturing["stages"] if x]
        self.capturing = None
        return st

    def _cap_add(self, eng, fn, r, w, dma):
        cap = self.capturing
        cap["n"] += 1
        eid = ("dma", eng, cap["n"]) if dma else eng
        conflict = False
        for k in list(r) + list(w):
            if k in cap["w"] and cap["w"][k] != eid:
                conflict = True
        for k in w:
            if k in cap["r"] and (cap["r"][k] - {eid}):
                conflict = True
        if conflict:
            cap["stages"].append([]); cap["w"] = {}; cap["r"] = {}
        cap["stages"][-1].append((eng, fn, tuple(r), tuple(w), dma))
        for k in w:
            cap["w"][k] = eid
        for k in r:
            cap["r"].setdefault(k, set()).add(eid)
        return None

    def run_staged(self, tiles, skew=1):
        info = []
        for st in tiles:
            lastw, lastt = {}, {}
            for si, stage in enumerate(st):
                for (eng, fn, r, w, dma) in stage:
                    for k in r:
                        lastt[k] = si
                    for k in w:
                        lastt[k] = si; lastw[k] = si
            info.append((lastw, lastt))
        active = []
        i = 0; step = 0
        while i < len(tiles) or active:
            if i < len(tiles) and step % skew == 0:
                active.append([i, 0]); i += 1
            progressed = False
            for ai, a in enumerate(active):
                t, si = a
                stage = tiles[t][si]
                ok = True
                for (eng, fn, r, w, dma) in stage:
                    for o in active[:ai]:
                        ow, ot = info[o[0]]
                        for k in w:
                            if ot.get(k, -1) >= o[1]:
                                ok = False
                        for k in r:
                            if ow.get(k, -1) >= o[1]:
                                ok = False
                    if not ok:
                        break
                if not ok:
                    continue
                for (eng, fn, r, w, dma) in stage:
                    self.add(eng, fn, r, w, dma)
                a[1] += 1
                progressed = True
            active = [a for a in active if a[1] < len(tiles[a[0]])]
            assert progressed or not active
            step += 1

    def add(self, eng, fn, r=(), w=(), dma=False):
        if self.capturing is not None:
            return self._cap_add(eng, fn, r, w, dma)
        op = Op()
        op.idx = len(self.ops); op.eng = eng; op.fn = fn; op.is_dma = dma
        op.signal = False; op.count = None; op.need = None; op.slot = None; op.val = None
        deps = {}
        for k in r:
            d = self.lastw.get(k)
            if d is not None:
                deps[d.idx] = d
        for k in w:
            d = self.lastw.get(k)
            if d is not None:
                deps[d.idx] = d
            for d in self.rd_eng.get(k, {}).values():
                deps[d.idx] = d
            for d in self.rd_dma.get(k, ()):
                deps[d.idx] = d
        if dma:
            q = eng
            n = self.slot_n[q]; self.slot_n[q] = n + 1
            si = n % NSLOT[q]
            op.slot = si; op.val = 16 * (n // NSLOT[q] + 1)
            prev = self.slot_last[q][si]
            if prev is not None:
                deps[prev.idx] = prev
            self.slot_last[q][si] = op
        op.deps = list(deps.values())
        for k in r:
            if dma:
                self.rd_dma.setdefault(k, []).append(op)
            else:
                self.rd_eng.setdefault(k, {})[eng] = op
        for k in w:
            self.lastw[k] = op
            self.rd_eng[k] = {}
            self.rd_dma[k] = []
        self.ops.append(op)
        if not dma:
            self.last_op[eng] = op
        return op

    def pe(self, fn, r=(), w=()): return self.add(PE, fn, r, w)
    def act(self, fn, r=(), w=()): return self.add(ACT, fn, r, w)
    def dve(self, fn, r=(), w=()): return self.add(DVE, fn, r, w)
    def pool(self, fn, r=(), w=()): return self.add(POOL, fn, r, w)

    def dma(self, q, out, in_, r=(), w=(), **kw):
        return self.add(q, lambda e: e.dma_start(out=out, in_=in_, **kw), r, w, dma=True)

    def barrier(self):
        deps = [o for o in self.last_op.values() if o is not None and o.idx >= self.epoch]
        for q in NSLOT:
            for o in self.slot_last[q]:
                if o is not None and o.idx >= self.epoch:
                    deps.append(o)
        saved = dict(self.last_op)
        for e in ENGS:
            op = self.add(e, None)
            op.deps = [d for d in deps]
        self.last_op = saved

    def emit(self):
        self.barrier()
        new = self.ops[self.emitted:]
        for op in new:
            need = []
            for d in sorted(op.deps, key=lambda o: o.idx):
                if d.idx < self.epoch:
                    continue
                if d.is_dma:
                    key = (d.eng, d.slot)
                    if self.seen_slot[op.eng].get(key, 0) >= d.val:
                        continue
                    self.seen_slot[op.eng][key] = d.val
                    need.append(d)
                else:
                    if d.eng == op.eng and not op.is_dma and op.fn is not None:
                        if op.eng == PE or not SAME_SYNC:
                            continue
                    if self.seen_idx[op.eng].get(d.eng, -1) >= d.idx:
                        continue
                    self.seen_idx[op.eng][d.eng] = d.idx
                    d.signal = True
                    need.append(d)
            op.need = need
        for op in new:
            if not op.is_dma and op.signal:
                self.cnt[op.eng] += 1
                op.count = self.cnt[op.eng]
        per = {e: [o for o in new if o.eng == e] for e in ENGS}
        esem, slots = self.esem, self.slots

        def run(e, lst):
            for op in lst:
                for d in op.need:
                    if d.is_dma:
                        e.wait_ge(slots[d.eng][d.slot], d.val)
                    else:
                        e.wait_ge(esem[d.eng], d.count)
                if op.fn is None:
                    continue
                ins = op.fn(e)
                if op.is_dma:
                    ins.then_inc(slots[op.eng][op.slot], 16)
                elif op.signal:
                    ins.then_inc(esem[op.eng], 1)

        with self.nc.Block() as blk:
            blk.tensor(lambda e: run(e, per[PE]))
            blk.scalar(lambda e: run(e, per[ACT]))
            blk.vector(lambda e: run(e, per[DVE]))
            blk.gpsimd(lambda e: run(e, per[POOL]))
            blk.sync(lambda e: run(e, per[SP]))
        self.emitted = len(self.ops)
        self.epoch = len(self.ops)


def bcast(ap, shape):
    return ap.to_broadcast(list(shape))


class Ctx:
    pass


def build(stage=99, debug=False):
    nc = bass.Bass("TRN2", target_bir_lowering=False)
    P = Prog(nc)
    K = Ctx()

    def din(name, shape, dt=F32):
        return nc.dram_tensor(name, list(shape), dt, kind="ExternalInput").ap()

    def dscr(name, shape, dt):
        return nc.dram_tensor(name, list(shape), dt, kind=("ExternalOutput" if debug else "Internal")).ap()

    I = {}
    I["xf"] = din("xf", [NF, D]); I["xo"] = din("xo", [NOWN, D]); I["oidx"] = din("oidx", [NOWN, 1], I32)
    I["cs"] = din("cs", [128, KC, 2])
    I["ropeA_f"] = din("ropeA_f", [NF, 2, 64]); I["ropeA_o"] = din("ropeA_o", [NOWN, 2, 64])
    I["ropeC_o"] = din("ropeC_o", [NOWN, 2, 64])
    I["ada_w"] = din("ada_w", [2, D, 6 * D]); I["ada_b"] = din("ada_b", [2, 6 * D])
    I["norm_mix"] = din("norm_mix", [2, D]); I["norm_ffn"] = din("norm_ffn", [2, D])
    I["ffn_w_gate"] = din("ffn_w_gate", [2, D, HID]); I["ffn_w_up"] = din("ffn_w_up", [2, D, HID])
    I["ffn_w_down"] = din("ffn_w_down", [2, HID, D])
    I["a_w_in"] = din("a_w_in", [D, 1216]); I["a_w_out"] = din("a_w_out", [D, D])
    I["s5t"] = din("s5t", [128, 32, 3]); I["s5b"] = din("s5b", [128, 32, 2, 16]); I["s5c"] = din("s5c", [128, 32, 2, 16])
    I["s5dd"] = din("s5dd", [128, 32]); I["s5_w_glu"] = din("s5_w_glu", [512, 512]); I["s5bg"] = din("s5bg", [128, 4])
    I["mla_qa_norm"] = din("mla_qa_norm", [1, 384]); I["mla_w_q_b"] = din("mla_w_q_b", [384, 768])
    I["mla_kva_norm"] = din("mla_kva_norm", [1, 256]); I["mla_w_kv_b"] = din("mla_w_kv_b", [256, 1024])
    I["mla_q_norm"] = din("mla_q_norm", [1, 192]); I["mla_k_norm"] = din("mla_k_norm", [1, 192])
    I["c_w_in"] = din("c_w_in", [D, 1536]); I["c_w_out"] = din("c_w_out", [D, D])
    I["c_q_norm"] = din("c_q_norm", [1, 64]); I["c_k_norm"] = din("c_k_norm", [1, 64]); I["c_sink"] = din("c_sink", [1, 16])
    out = nc.dram_tensor("out", [OWN_LAT, D], F32, kind="ExternalOutput").ap()

    S = {}
    S["mod"] = dscr("modsc", [2, 128, 2, 6 * D], F32)
    S["KTa"] = dscr("KTa", [128, 4, NF], BF); S["KTb"] = dscr("KTb", [64, 4, NF], BF)
    S["V"] = dscr("Vsc", [NF, 512], BF); S["U"] = dscr("Usc", [32, 128, NCH], BF)
    S["YT"] = dscr("YTsc", [512, 8, NCH], BF); S["S5"] = dscr("S5sc", [NF, 512], BF)
    S["H1"] = dscr("H1sc", [NOWN, D], F32); S["H2"] = dscr("H2sc", [NOWN, D], F32)
    S["H3"] = dscr("H3sc", [OWN_LAT, D], F32)

    ident = nc.alloc_sbuf_tensor("ident", [128, 128], BF)
    identf = nc.alloc_sbuf_tensor("identf", [128, 128], F32)
    ones_bf = nc.alloc_sbuf_tensor("ones_bf", [128, 128], BF)
    P.pool(lambda e: e.memset(identf[:], 0.0), w=["identf"])
    P.pool(lambda e: e.affine_select(out=identf[:], in_=identf[:], pattern=[[-1, 128]], compare_op=ALU.not_equal,
                                     fill=1.0, base=0, channel_multiplier=1), r=["identf"], w=["identf"])
    P.pool(lambda e: e.tensor_copy(out=ident[:], in_=identf[:]), r=["identf"], w=["ident"])
    P.pool(lambda e: e.memset(ones_bf[:], 1.0), w=["ones_bf"])
    K.ident, K.identf, K.ones_bf = ident, identf, ones_bf

    phase0_mod(nc, P, I, S)
    if stage >= 1:
        phase1_full(nc, P, I, S, K)
    if stage >= 2:
        phase2_s5(nc, P, I, S, K)
    if stage >= 3:
        phase3_mla(nc, P, I, S, K)
        phase_ffn(nc, P, I, S, K, 0, S["H1"], S["H2"], 2, NTO)
    if stage >= 4:
        phase4_win(nc, P, I, S, K)
        phase_ffn(nc, P, I, S, K, 1, S["H3"], out, 0, OWN_LAT // 128)
    else:
        pass
    return nc


def phase0_mod(nc, P, I, S):
    with nc.sbuf_tensor("cst", [128, KC, 2], F32) as cst, nc.sbuf_tensor("sl", [128, KC, 2], F32) as sl, \
            nc.sbuf_tensor("SL", [128, 2, KC, 128], BF) as SL, \
            nc.sbuf_tensor("wb0", [128, KC, 512], BF) as wb0, nc.sbuf_tensor("wb1", [128, KC, 512], BF) as wb1, \
            nc.sbuf_tensor("wb2", [128, KC, 512], BF) as wb2, nc.sbuf_tensor("wb3", [128, KC, 512], BF) as wb3, \
            nc.sbuf_tensor("bb0", [128, 512], F32) as bb0, nc.sbuf_tensor("bb1", [128, 512], F32) as bb1, \
            nc.sbuf_tensor("bb2", [128, 512], F32) as bb2, nc.sbuf_tensor("bb3", [128, 512], F32) as bb3, \
            nc.sbuf_tensor("gmix", [128, D], F32) as gmix, nc.sbuf_tensor("gffn", [128, D], F32) as gffn, \
            nc.sbuf_tensor("modt", [128, 2, 6 * D], F32) as modt, \
            nc.psum_tensor("pm0", [128, 512], F32) as pm0, nc.psum_tensor("pm1", [128, 512], F32) as pm1:
        wb = [wb0, wb1, wb2, wb3]; bb = [bb0, bb1, bb2, bb3]; pm = [pm0, pm1]
        P.dma(SP, cst[:], I["cs"][:, :, :], w=["cst"])
        P.act(lambda e: e.activation(out=sl[:], in_=cst[:], func=AF.Silu), r=["cst"], w=["sl"])
        for t in range(2):
            for kc in range(KC):
                P.dve(lambda e, t=t, kc=kc: e.tensor_copy(out=SL[:, t, kc, :], in_=bcast(sl[:, kc, t:t + 1], [128, 128])),
                      r=["sl"], w=["SL"])
        n = 0
        for layer in range(2):
            P.dma(SP, gmix[:], bcast(I["norm_mix"][layer:layer + 1, :], [128, D]), w=["gmix"])
            P.dma(SP, gffn[:], bcast(I["norm_ffn"][layer:layer + 1, :], [128, D]), w=["gffn"])
            wv = I["ada_w"][layer].rearrange("(kc p) n -> p kc n", p=128)
            for j in range(12):
                b = n % 4; n += 1
                P.dma(POOL, wb[b][:], wv[:, :, j * 512:(j + 1) * 512], w=["wb%d" % b])
                P.dma(SP, bb[b][:], bcast(I["ada_b"][layer:layer + 1, j * 512:(j + 1) * 512], [128, 512]), w=["bb%d" % b])
                for t in range(2):
                    for kc in range(KC):
                        P.pe(lambda e, t=t, kc=kc, b=b: e.matmul(pm[t][:], lhsT=SL[:, t, kc, :], rhs=wb[b][:, kc, :],
                                                                  start=(kc == 0), stop=(kc == KC - 1)),
                             r=["SL", "wb%d" % b], w=["pm%d" % t])
                    P.dve(lambda e, t=t, b=b, j=j: e.tensor_tensor(out=modt[:, t, j * 512:(j + 1) * 512], in0=pm[t][:],
                                                                    in1=bb[b][:], op=ALU.add),
                          r=["pm%d" % t, "bb%d" % b], w=["modt"])
            for t in range(2):
                P.dve(lambda e, t=t: e.scalar_tensor_tensor(out=modt[:, t, D:2 * D], in0=modt[:, t, D:2 * D], scalar=1.0,
                                                             in1=gmix[:], op0=ALU.add, op1=ALU.mult),
                      r=["modt", "gmix"], w=["modt"])
                P.dve(lambda e, t=t: e.scalar_tensor_tensor(out=modt[:, t, 4 * D:5 * D], in0=modt[:, t, 4 * D:5 * D], scalar=1.0,
                                                             in1=gffn[:], op0=ALU.add, op1=ALU.mult),
                      r=["modt", "gffn"], w=["modt"])
            P.dma(ACT, S["mod"][layer], modt[:], r=["modt"], w=["modsc%d" % layer])
        P.emit()


def rstd_from_ssq(P, ssq, rstd, n, width, tag, sfx=""):
    P.dve(lambda e: e.tensor_scalar(out=rstd, in0=ssq, scalar1=1.0 / width, scalar2=EPS, op0=ALU.mult, op1=ALU.add),
          r=[tag + "ssq" + sfx], w=[tag + "rstd" + sfx])
    P.act(lambda e: e.activation(out=rstd, in_=rstd, func=AF.Sqrt), r=[tag + "rstd" + sfx], w=[tag + "rstd" + sfx])
    P.dve(lambda e: e.reciprocal(out=rstd, in_=rstd), r=[tag + "rstd" + sfx], w=[tag + "rstd" + sfx])


def norm_mod(P, x_ap, xkey, A_ap, B_ap, modkey, junk, ssq, rstd, tmp, a_bf, tag):
    P.act(lambda e: e.activation(out=junk, in_=x_ap, func=AF.Square), r=[xkey], w=[tag + "tmp"])
    P.dve(lambda e: e.reduce_sum(out=ssq, in_=junk, axis=AX.X), r=[tag + "tmp"], w=[tag + "ssq"])
    rstd_from_ssq(P, ssq, rstd, 1, D, tag)
    P.dve(lambda e: e.scalar_tensor_tensor(out=tmp, in0=x_ap, scalar=rstd, in1=A_ap, op0=ALU.mult, op1=ALU.mult),
          r=[xkey, tag + "rstd", modkey], w=[tag + "tmp"])
    P.pool(lambda e: e.tensor_tensor(out=a_bf, in0=tmp, in1=B_ap, op=ALU.add), r=[tag + "tmp", modkey], w=[tag + "a_bf"])


def transpose8(P, K, a_bf, pT, aT_out, tag, aTkey, pTkey=None):
    pTkey = pTkey or (tag + "pT")
    for kc in range(KC):
        P.pe(lambda e, kc=kc: e.transpose(pT[:, kc, :], a_bf[:, kc * 128:(kc + 1) * 128], K.ident[:]),
             r=[tag + "a_bf", "ident"], w=[pTkey])
    P.act(lambda e: e.activation(out=aT_out, in_=pT[:, :, :], func=AF.Copy), r=[pTkey], w=[aTkey])


def norm_mod_T(P, K, x_ap, xkey, A_ap, B_ap, modkey, junk, ssq, rstd, tmp, a_bf, pT, aT_out, tag, aTkey, pTkey=None):
    norm_mod(P, x_ap, xkey, A_ap, B_ap, modkey, junk, ssq, rstd, tmp, a_bf, tag)
    transpose8(P, K, a_bf, pT, aT_out, tag, aTkey, pTkey)


def run_pipeline(gens):
    active = []
    it = iter(gens)
    more = True
    while more or active:
        nxt = next(it, None) if more else None
        if nxt is None:
            more = False
        for g in list(active):
            try:
                next(g)
            except StopIteration:
                active.remove(g)
        if nxt is not None:
            try:
                next(nxt)
                active.append(nxt)
            except StopIteration:
                pass


def rope_inplace(P, eng_a, eng_b, x4, rp, t1, nh, rkeys, xkey, tkey):
    cosb = bcast(rp[:, 0:1, :], [128, nh, 64])
    for a in range(2):
        sl_ = slice(a * 32, (a + 1) * 32)
        sw = x4[:, :, sl_].rearrange("p h (w q) -> p h w q", w=2)[:, :, ::-1, :]
        t1v = t1[:, :, sl_].rearrange("p h (w q) -> p h w q", w=2)
        sinb = bcast(rp[:, 1:2, sl_], [128, nh, 32]).rearrange("p h (w q) -> p h w q", w=2)
        eng_a(lambda e, sw=sw, t1v=t1v, sinb=sinb: e.tensor_tensor(out=t1v, in0=sw, in1=sinb, op=ALU.mult),
              r=[xkey] + rkeys, w=[tkey])
    eng_b(lambda e: e.tensor_tensor(out=x4, in0=x4, in1=cosb, op=ALU.mult), r=[xkey, tkey] + rkeys, w=[xkey])
    eng_b(lambda e: e.tensor_tensor(out=x4, in0=x4, in1=t1, op=ALU.add), r=[xkey, tkey], w=[xkey])


def phase1_full(nc, P, I, S, K):
    from contextlib import ExitStack
    with ExitStack() as es:
        def sb(name, shape, dt):
            return es.enter_context(nc.sbuf_tensor(name, list(shape), dt))

        def ps(name, shape, dt=F32):
            return es.enter_context(nc.psum_tensor(name, list(shape), dt))
        Win = sb("Win", [128, KC, 832], BF); Wkvb = sb("Wkvb", [128, 2, 1024], BF)
        modv = sb("modv", [128, 2, 2 * D], F32)
        gkva = sb("gkva", [128, 256], F32); gk = sb("gk", [128, 192], F32)
        xt = [sb("xt%d" % i, [128, D], F32) for i in range(NB1)]
        rp = [sb("rp%d" % i, [128, 2, 64], F32) for i in range(NB1)]
        junk2 = [sb("junk%d" % i, [128, D], F32) for i in range(NB1)]; tmp2 = junk2
        a_bf2 = [sb("a_bf%d" % i, [128, D], BF) for i in range(NB1)]
        aT4 = [sb("aT4_%d" % i, [128, KC, 512], BF) for i in range(2)]
        UT = sb("UT", [128, 4, 8, NCH], BF)
        st2 = [sb("st%d" % i, [128, 16], F32) for i in range(NB1)]
        c_bf2 = [sb("c_bf%d" % i, [128, 256], BF) for i in range(NB1)]; cT2 = [sb("cT%d" % i, [128, 2, 128], BF) for i in range(NB1)]
        kf2 = [sb("kf%d" % i, [128, 4, 192], F32) for i in range(NB1)]; ksq2 = [sb("ksq%d" % i, [128, 4, 192], F32) for i in range(NB1)]
        t12 = [sb("t1_%d" % i, [128, 4, 64], F32) for i in range(NB1)]
        K_bf2 = [sb("K_bf%d" % i, [128, 4, 192], BF) for i in range(NB1)]; KTs = [sb("KTs%d" % i, [128, 8, 128], BF) for i in range(NB1)]
        V_bf = [sb("V_bf%d" % i, [128, 4, 128], BF) for i in range(NB1)]
        pTa = ps("pTa", [128, 8, 128], BF); pTk = ps("pTk", [128, 8, 128], BF); pcT = ps("pcT", [128, 8, 128], BF)
        pKV = ps("pKV", [128, 512]); pU = [ps("pU%d" % i, [128, 512]) for i in range(2)]
        pKVB = ps("pKVB", [128, 1024])

        wv = I["a_w_in"].rearrange("(kc p) n -> p kc n", p=128)
        P.dma(POOL, Win[:, :, 0:512], wv[:, :, 0:512], w=["Win"])
        P.dma(POOL, Win[:, :, 512:832], wv[:, :, 896:1216], w=["Win"])
        P.dma(POOL, Wkvb[:], I["mla_w_kv_b"].rearrange("(kc p) n -> p kc n", p=128), w=["Wkvb"])
        P.dma(SP, modv[:], S["mod"][0][:, :, 0:2 * D], r=["modsc0"], w=["modv"])
        P.dma(SP, gkva[:], bcast(I["mla_kva_norm"][0:1, :], [128, 256]), w=["gkva"])
        P.dma(SP, gk[:], bcast(I["mla_k_norm"][0:1, :], [128, 192]), w=["gk"])
        ropev = I["ropeA_f"]
        def p1_tile(tg, ti, b, aT, aTkey, typ, nb, cbase):
            rows = slice(tg * 128, (tg + 1) * 128)
            rb = b
            junk, tmp, a_bf, st, c_bf, cT, kf, ksq, t1, K_bf = (junk2[b], tmp2[b], a_bf2[b], st2[b], c_bf2[b], cT2[b], kf2[b],
                                                                  ksq2[b], t12[b], K_bf2[b])
            sfx = "_%d" % b
            P.dma(SP, xt[b][:], I["xf"][rows, :], w=["xt%d" % b])
            P.dma(SP, rp[rb][:], ropev[rows, :, :], w=["rp%d" % rb])
            norm_mod(P, xt[b][:], "xt%d" % b, modv[:, typ, D:2 * D], modv[:, typ, 0:D], "modv",
                     junk[:], st[:, 0:1], st[:, 1:2], tmp[:], a_bf[:], "p1" + sfx)
            transpose8(P, K, a_bf[:], pTa, aT[:, :, ti * 128:(ti + 1) * 128], "p1" + sfx, aTkey, pTkey="pTa")
            for kc in range(KC):
                P.pe(lambda e, kc=kc, aT=aT, ti=ti: e.matmul(pKV[:, 0:320], lhsT=aT[:, kc, ti * 128:(ti + 1) * 128],
                                                            rhs=Win[:, kc, 512:832], start=(kc == 0), stop=(kc == KC - 1)),
                     r=[aTkey, "Win"], w=["pKV"])
            P.act(lambda e: e.activation(out=junk[:, 0:256], in_=pKV[:, 0:256], func=AF.Square), r=["pKV"], w=["p1" + sfx + "tmp"])
            P.dve(lambda e: e.reduce_sum(out=st[:, 2:3], in_=junk[:, 0:256], axis=AX.X), r=["p1" + sfx + "tmp"], w=["cssq" + sfx])
            rstd_from_ssq(P, st[:, 2:3], st[:, 3:4], 1, 256, "c", sfx)
            P.dve(lambda e: e.scalar_tensor_tensor(out=c_bf[:], in0=pKV[:, 0:256], scalar=st[:, 3:4], in1=gkva[:],
                                                   op0=ALU.mult, op1=ALU.mult), r=["pKV", "crstd" + sfx, "gkva"], w=["c_bf" + sfx])
            P.act(lambda e: e.activation(out=kf[:, :, 128:192], in_=bcast(pKV[:, 256:320].unsqueeze(1), [128, 4, 64]),
                                         func=AF.Copy), r=["pKV"], w=["kf" + sfx])
            for k2 in range(2):
                P.pe(lambda e, k2=k2: e.transpose(pcT[:, k2, :], c_bf[:, k2 * 128:(k2 + 1) * 128], K.ident[:]),
                     r=["c_bf" + sfx, "ident"], w=["pcT"])
            P.dve(lambda e: e.tensor_copy(out=cT[:], in_=pcT[:, 0:2, :]), r=["pcT"], w=["cT" + sfx])
            for half in range(2):
                for k2 in range(2):
                    P.pe(lambda e, half=half, k2=k2: e.matmul(pKVB[:, half * 512:(half + 1) * 512], lhsT=cT[:, k2, :],
                                                              rhs=Wkvb[:, k2, half * 512:(half + 1) * 512],
                                                              start=(k2 == 0), stop=(k2 == 1)),
                         r=["cT" + sfx, "Wkvb"], w=["pKVB"])
            kvv = pKVB[:].rearrange("p (h c) -> p h c", h=4)
            P.act(lambda e, kvv=kvv: e.activation(out=kf[:, :, 0:128], in_=kvv[:, :, 0:128], func=AF.Copy), r=["pKVB"], w=["kf" + sfx])
            P.act(lambda e, kvv=kvv, b=b: e.activation(out=V_bf[b][:], in_=kvv[:, :, 128:256], func=AF.Copy),
                  r=["pKVB"], w=["V_bf%d" % b])
            P.dma(ACT, S["V"][rows, :], V_bf[b][:].rearrange("p h d -> p (h d)"), r=["V_bf%d" % b], w=["Vsc"])
            P.dve(lambda e: e.tensor_tensor(out=ksq[:], in0=kf[:], in1=kf[:], op=ALU.mult), r=["kf" + sfx], w=["ksq" + sfx])
            P.dve(lambda e: e.reduce_sum(out=st[:, 4:8], in_=ksq[:], axis=AX.X), r=["ksq" + sfx], w=["kssq" + sfx])
            rstd_from_ssq(P, st[:, 4:8], st[:, 8:12], 4, 192, "k", sfx)
            P.dve(lambda e: e.tensor_tensor(out=kf[:], in0=kf[:], in1=bcast(st[:, 8:12].unsqueeze(2), [128, 4, 192]),
                                            op=ALU.mult), r=["kf" + sfx, "krstd" + sfx], w=["kf" + sfx])
            P.pool(lambda e: e.tensor_tensor(out=kf[:], in0=kf[:], in1=bcast(gk[:].unsqueeze(1), [128, 4, 192]),
                                             op=ALU.mult), r=["kf" + sfx, "gk"], w=["kf" + sfx])
            rope_inplace(P, P.pool, P.dve, kf[:, :, 128:192], rp[rb], t1[:], 4, ["rp%d" % rb], "kf" + sfx, "t1" + sfx)
            P.act(lambda e: e.activation(out=K_bf[:], in_=kf[:], func=AF.Copy), r=["kf" + sfx], w=["K_bf" + sfx])
            for h in range(4):
                P.pe(lambda e, h=h: e.transpose(pTk[:, 2 * h, :], K_bf[:, h, 0:128], K.ident[:]), r=["K_bf" + sfx, "ident"], w=["pTk"])
                P.pe(lambda e, h=h: e.transpose(pTk[0:64, 2 * h + 1, :], K_bf[:, h, 128:192], K.ident[:]),
                     r=["K_bf" + sfx, "ident"], w=["pTk"])
            pv = pTk[:].rearrange("p (h two) t -> p h two t", two=2)
            kv_ = KTs[b][:].rearrange("p (h two) t -> p h two t", two=2)
            P.dve(lambda e, pv=pv, kv_=kv_: e.tensor_copy(out=kv_[:, :, 0, :], in_=pv[:, :, 0, :]), r=["pTk"], w=["KTs%d" % b])
            P.dve(lambda e, pv=pv, kv_=kv_: e.tensor_copy(out=kv_[0:64, :, 1, :], in_=pv[0:64, :, 1, :]), r=["pTk"], w=["KTs%d" % b])
            P.dma(ACT, S["KTa"][:, :, rows], kv_[:, :, 0, :], r=["KTs%d" % b], w=["KTa"])
            P.dma(ACT, S["KTb"][:, :, rows], kv_[0:64, :, 1, :], r=["KTs%d" % b], w=["KTb"])
            if ti == nb - 1:
                p1_u(aT, aTkey, nb, cbase)

        def p1_u(aT, aTkey, nb, cbase):
            ncol = nb * 16
            for fc in range(4):
                pu = pU[fc % 2]
                for kc in range(KC):
                    P.pe(lambda e, fc=fc, kc=kc, pu=pu, aT=aT, nb=nb: e.matmul(
                        pu[:, 0:nb * 128], lhsT=Win[:, kc, fc * 128:(fc + 1) * 128],
                        rhs=aT[:, kc, 0:nb * 128].rearrange("p (c j) -> p j c", j=8), start=(kc == 0), stop=(kc == KC - 1)),
                        r=[aTkey, "Win"], w=["pU%d" % (fc % 2)])
                P.act(lambda e, fc=fc, pu=pu, nb=nb, cbase=cbase, ncol=ncol: e.activation(
                    out=UT[:, fc, :, cbase:cbase + ncol], in_=pu[:, 0:nb * 128].rearrange("p (j c) -> p j c", j=8), func=AF.Copy),
                    r=["pU%d" % (fc % 2)], w=["UT"])

        batches = [(0, 2, 0, 1)] + [(2 + 4 * k, 4, 32 + 64 * k, 0) for k in range(16)]
        tiles = []
        nt = 0
        for bi, (t0, nb, cbase, typ) in enumerate(batches):
            aT = aT4[bi % 2]; aTkey = "aT4_%d" % (bi % 2)
            for ti in range(nb):
                tiles.append(P.capture(lambda t0=t0, ti=ti, nt=nt, aT=aT, aTkey=aTkey, typ=typ, nb=nb, cbase=cbase:
                                       p1_tile(t0 + ti, ti, nt % NB1, aT, aTkey, typ, nb, cbase))); nt += 1
        P.run_staged(tiles, skew=P1_SKEW)
        n = 0
        for fc in range(4):
            for gi in range(8):
                g = fc * 8 + gi
                q = SP if n % 2 == 0 else ACT; n += 1
                P.dma(q, S["U"][g].rearrange("(j s) c -> s j c", s=16), UT[gi * 16:(gi + 1) * 16, fc, :, :], r=["UT"], w=["Usc"])
        P.emit()


def _rope_tables(n_tok_lat):
    n = np.arange(n_tok_lat)
    row = (n // 64).astype(np.float32); col = (n % 64).astype(np.float32)
    inv = (10000.0 ** (-np.arange(16, dtype=np.float32) / 16)).astype(np.float32)
    ar = row[:, None] * inv; ac = col[:, None] * inv
    ang = np.concatenate([ar, ar, ac, ac], axis=-1).astype(np.float32)
    cos = np.cos(ang).astype(np.float32); sin = np.sin(ang).astype(np.float32)
    sgn = np.concatenate([-np.ones(16), np.ones(16), -np.ones(16), np.ones(16)]).astype(np.float32)
    return np.stack([cos, sin * sgn], axis=1)


def own_start(j):
    return min(max(2048 * j - 128, 0), SEQ - OWN_LAT)


def prep_core(inp, core, shared):
    b, j = core // 4, core % 4
    s = own_start(j)
    f = lambda a: np.ascontiguousarray(a, dtype=np.float32)
    m = dict(shared)
    m["xf"] = f(np.concatenate([inp["ctx"][b], inp["x"][b]], 0))
    m["xo"] = f(np.concatenate([inp["ctx"][b], inp["x"][b, s:s + OWN_LAT]], 0))
    m["oidx"] = np.concatenate([np.arange(CTX), CTX + s + np.arange(OWN_LAT)]).astype(np.int32).reshape(-1, 1)
    cs = np.stack([inp["c"][b].reshape(KC, 128).T, inp["c_ctx"].reshape(KC, 128).T], axis=-1)
    m["cs"] = f(cs)
    rl = shared["_rope_lat"]
    idr = np.zeros((CTX, 2, 64), np.float32); idr[:, 0, :] = 1.0
    m["ropeA_f"] = f(np.concatenate([idr, rl], 0))
    m["ropeA_o"] = f(np.concatenate([idr, rl[s:s + OWN_LAT]], 0))
    m["ropeC_o"] = m["ropeA_o"]
    del m["_rope_lat"]
    return m


def prep_shared(inp):
    f = lambda a: np.ascontiguousarray(a, dtype=np.float32)
    m = {}
    for k in ("ada_w", "ada_b", "norm_mix", "norm_ffn", "ffn_w_gate", "ffn_w_up", "ffn_w_down"):
        m[k] = f(inp[k])
    for k in ("a_w_in", "a_w_out", "s5_w_glu", "mla_w_q_b", "mla_w_kv_b", "c_w_in", "c_w_out"):
        m[k] = f(inp[k][0])
    for k in ("mla_qa_norm", "mla_kva_norm", "mla_q_norm", "mla_k_norm", "c_q_norm", "c_k_norm", "c_sink"):
        m[k] = f(inp[k][0].reshape(1, -1))

    def unit(a):
        a = np.asarray(a)
        d_, g_, p_ = a.shape[:3]
        a = a.reshape((d_, g_ // 2, 2, p_) + a.shape[3:])
        a = np.moveaxis(a, (2, 3), (0, 1))
        return a.reshape((2 * p_, d_ * (g_ // 2)) + a.shape[4:])
    lre, lim, ls = inp["s5_lam_re"][0], inp["s5_lam_im"][0], inp["s5_log_step"][0]
    lsb = np.broadcast_to(ls[:, :, None], lre.shape)
    m["s5t"] = f(np.stack([unit(lre), unit(lim), unit(lsb)], -1))
    m["s5b"] = f(np.stack([unit(inp["s5_b_re"][0]), unit(inp["s5_b_im"][0])], 2))
    cre = np.swapaxes(inp["s5_c_re"][0], 2, 3); cim = np.swapaxes(inp["s5_c_im"][0], 2, 3)
    m["s5c"] = f(np.stack([unit(cre), unit(cim)], 2))
    dd = np.asarray(inp["s5_d"][0]).reshape(32, 16)
    m["s5dd"] = f(np.broadcast_to(dd.T[None, :, :], (8, 16, 32)).reshape(128, 32))
    m["s5bg"] = f(np.asarray(inp["s5_b_glu"][0]).reshape(4, 128).T)
    m["_rope_lat"] = _rope_tables(SEQ)
    return m


_NC_CACHE = {}


def kernel(**inputs):
    inp = {k: np.asarray(v) for k, v in inputs.items()}
    if "nc" not in _NC_CACHE:
        _NC_CACHE["nc"] = build()
    nc = _NC_CACHE["nc"]
    shared = prep_shared(inp)
    in_maps = [prep_core(inp, c, shared) for c in range(8)]
    res = run_bass_kernel_spmd(nc, in_maps, core_ids=list(range(8)))
    outp = np.empty((2, SEQ, D), np.float32)
    for c in range(8):
        b, j = c // 4, c % 4
        off = 2048 * j - own_start(j)
        outp[b, 2048 * j:2048 * (j + 1)] = res.results[c]["out"][off:off + 2048]
    return outp


def cmul(eng, o_r, o_i, a_r, a_i, b_r, b_i, t1, t2, keys):
    eng(lambda e: e.tensor_tensor(out=t1, in0=a_r, in1=b_r, op=ALU.mult), r=keys, w=keys)
    eng(lambda e: e.tensor_tensor(out=t2, in0=a_i, in1=b_i, op=ALU.mult), r=keys, w=keys)
    eng(lambda e: e.tensor_tensor(out=o_r, in0=t1, in1=t2, op=ALU.subtract), r=keys, w=keys)
    eng(lambda e: e.tensor_tensor(out=t1, in0=a_r, in1=b_i, op=ALU.mult), r=keys, w=keys)
    eng(lambda e: e.tensor_tensor(out=t2, in0=a_i, in1=b_r, op=ALU.mult), r=keys, w=keys)
    eng(lambda e: e.tensor_tensor(out=o_i, in0=t1, in1=t2, op=ALU.add), r=keys, w=keys)


def s5_scan(eng, vr, vi, n, L, MU, NMI, us, seed, tot, tmp, key, tkey, fused):
    s1, s2, b1, b2, b3, b4 = tmp
    kk = [key, tkey]
    u0 = us.start

    def op2(out, in0, in1, op):
        eng(lambda e: e.tensor_tensor(out=out, in0=in0, in1=in1, op=op), r=kk, w=kk)

    def cp(out, in_):
        eng(lambda e: e.tensor_copy(out=out, in_=in_), r=kk, w=kk)

    def fma(out, in0, sc, in1):
        eng(lambda e: e.scalar_tensor_tensor(out=out, in0=in0, scalar=sc, in1=in1, op0=ALU.mult, op1=ALU.add), r=kk, w=kk)

    def groups(cnt):
        if cnt >= 64:
            return [(slice(0, 2), 2), (slice(2, 4), 2)]
        return [(slice(0, 4), 4)]

    def mub(c, l, usl, cnt):
        uu = slice(u0 + usl.start, u0 + usl.stop)
        return bcast(MU[:, c, l, uu].unsqueeze(2), [128, usl.stop - usl.start, cnt])
    for l in range(L):
        st = 1 << l; cnt = n >> (l + 1)
        ar, ai = vr[:, :, st - 1::2 * st], vi[:, :, st - 1::2 * st]
        br, bi = vr[:, :, 2 * st - 1::2 * st], vi[:, :, 2 * st - 1::2 * st]
        for (usl, nu) in groups(cnt):
            if fused and cnt >= 64:
                for q in range(usl.start, usl.stop):
                    mr = MU[:, 0, l, u0 + q:u0 + q + 1]; mi = MU[:, 1, l, u0 + q:u0 + q + 1]; nmi = NMI[:, l, u0 + q:u0 + q + 1]
                    fma(br[:, q, :], ar[:, q, :], mr, br[:, q, :])
                    fma(br[:, q, :], ai[:, q, :], nmi, br[:, q, :])
                    fma(bi[:, q, :], ar[:, q, :], mi, bi[:, q, :])
                    fma(bi[:, q, :], ai[:, q, :], mr, bi[:, q, :])
                continue
            p1 = (b1 if nu == 2 else s1)[:, :, 0:cnt]; p2 = (b2 if nu == 2 else s2)[:, :, 0:cnt]
            mr, mi = mub(0, l, usl, cnt), mub(1, l, usl, cnt)
            op2(p1, ar[:, usl, :], mr, ALU.mult); op2(p2, ai[:, usl, :], mi, ALU.mult)
            op2(br[:, usl, :], br[:, usl, :], p1, ALU.add); op2(br[:, usl, :], br[:, usl, :], p2, ALU.subtract)
            op2(p1, ar[:, usl, :], mi, ALU.mult); op2(p2, ai[:, usl, :], mr, ALU.mult)
            op2(bi[:, usl, :], bi[:, usl, :], p1, ALU.add); op2(bi[:, usl, :], bi[:, usl, :], p2, ALU.add)
    lr, li = vr[:, :, n - 1:n], vi[:, :, n - 1:n]
    if tot is not None:
        cp(tot[0], lr); cp(tot[1], li)
    if seed is None:
        eng(lambda e: e.memset(lr, 0.0), r=kk, w=kk)
        eng(lambda e: e.memset(li, 0.0), r=kk, w=kk)
    else:
        cp(lr, seed[0]); cp(li, seed[1])
    for l in reversed(range(L)):
        st = 1 << l; cnt = n >> (l + 1)
        ar, ai = vr[:, :, st - 1::2 * st], vi[:, :, st - 1::2 * st]
        br, bi = vr[:, :, 2 * st - 1::2 * st], vi[:, :, 2 * st - 1::2 * st]
        for (usl, nu) in groups(cnt):
            if nu == 2:
                tr, ti = b3[:, :, 0:cnt], b4[:, :, 0:cnt]
            else:
                tr, ti = s1[:, :, 32:32 + cnt], s2[:, :, 32:32 + cnt]
            cp(tr, br[:, usl, :]); cp(ti, bi[:, usl, :])
            if fused and cnt >= 64:
                for q2, q in enumerate(range(usl.start, usl.stop)):
                    mr = MU[:, 0, l, u0 + q:u0 + q + 1]; mi = MU[:, 1, l, u0 + q:u0 + q + 1]; nmi = NMI[:, l, u0 + q:u0 + q + 1]
                    fma(br[:, q, :], br[:, q, :], mr, ar[:, q, :])
                    fma(br[:, q, :], ti[:, q2, :], nmi, br[:, q, :])
                    fma(bi[:, q, :], bi[:, q, :], mr, ai[:, q, :])
                    fma(bi[:, q, :], tr[:, q2, :], mi, bi[:, q, :])
            else:
                p1 = (b1 if nu == 2 else s1)[:, :, 0:cnt]; p2 = (b2 if nu == 2 else s2)[:, :, 0:cnt]
                mr, mi = mub(0, l, usl, cnt), mub(1, l, usl, cnt)
                op2(p1, tr, mr, ALU.mult); op2(p2, ti, mi, ALU.mult)
                op2(p1, p1, p2, ALU.subtract); op2(br[:, usl, :], p1, ar[:, usl, :], ALU.add)
                op2(p1, tr, mi, ALU.mult); op2(p2, ti, mr, ALU.mult)
                op2(p1, p1, p2, ALU.add); op2(bi[:, usl, :], p1, ai[:, usl, :], ALU.add)
            cp(ar[:, usl, :], tr); cp(ai[:, usl, :], ti)


def phase2_s5(nc, P, I, S, K):
    from contextlib import ExitStack
    TWO_PI = 2.0 * math.pi
    with ExitStack() as es0:
        def sbp(name, shape, dt):
            return es0.enter_context(nc.sbuf_tensor(name, list(shape), dt))
        TOEP = sbp("TOEP", [128, 32, 128], BF)
        WSTAB = sbp("WSTAB", [128, 32, 2, 2, 128], BF)
        WOUT = sbp("WOUT", [128, 32, 2, 128], BF)
        MU = sbp("MU", [128, 2, 10, 32], F32)
        NMI = sbp("NMI", [128, 10, 32], F32)
        with ExitStack() as es:
            def sb(name, shape, dt=F32):
                return es.enter_context(nc.sbuf_tensor(name, list(shape), dt))
            T3 = sb("T3", [128, 32, 3]); B4 = sb("B4", [128, 32, 2, 16]); C4 = sb("C4", [128, 32, 2, 16])
            dd = sb("dd", [128, 32])
            sm = {n: sb("sm_" + n, [128, 32]) for n in
                  ("step", "rho", "th", "tq", "tf", "r", "s1", "hs", "c1", "sn", "cs", "ar", "ai", "den", "nr", "ni",
                   "cr", "ci", "ir", "ii", "tA", "tB", "am1")}
            tiq = sb("tiq", [128, 32], I32)
            PW = sb("PW", [128, 2, 32, 9]); NW = sb("NW", [128, 2, 32, 9])
            QA = sb("QA", [128, 2, 32, 9]); QB = sb("QB", [128, 2, 32, 9])
            bbar = sb("bbar", [128, 2, 32, 16])
            XB = sb("XB", [128, 2, 32, 8, 16]); XC = sb("XC", [128, 2, 32, 9, 16])
            WST = sb("WST", [128, 2, 32, 128], BF)
            big1 = sb("big1", [128, 32, 9, 16]); big2 = sb("big2", [128, 32, 9, 16])
            maskF = sb("maskF", [128, 128]); maskB = sb("maskB", [128, 128])
            tg = sb("tg", [128, 128]); tb = sb("tb", [128, 128])
            ptr = es.enter_context(nc.psum_tensor("ptr", [128, 2, 128], BF))
            ptf2 = es.enter_context(nc.psum_tensor("ptf2", [128, 128], F32))
            ptb2 = es.enter_context(nc.psum_tensor("ptb2", [128, 128], F32))
            ptf = es.enter_context(nc.psum_tensor("ptf", [128, 128], F32))
            ptb = es.enter_context(nc.psum_tensor("ptb", [128, 128], F32))
            kk = ["tab"]

            def dv(fn): P.dve(fn, r=kk, w=kk)

            def ac(fn): P.act(fn, r=kk, w=kk)
            P.dma(SP, T3[:], I["s5t"][:, :, :], w=kk); P.dma(SP, B4[:], I["s5b"][:, :, :, :], w=kk)
            P.dma(SP, C4[:], I["s5c"][:, :, :, :], w=kk); P.dma(SP, dd[:], I["s5dd"][:, :], w=kk)
            P.pool(lambda e: e.memset(maskF[:], 1.0), r=kk, w=kk)
            P.pool(lambda e: e.affine_select(out=maskF[:].rearrange("p (t s) -> p t s", s=16), in_=maskF[:].rearrange("p (t s) -> p t s", s=16),
                                             pattern=[[16, 8], [0, 16]], compare_op=ALU.is_ge, fill=0.0, base=15,
                                             channel_multiplier=-1), r=kk, w=kk)
            P.pool(lambda e: e.memset(maskB[:], 1.0), r=kk, w=kk)
            P.pool(lambda e: e.affine_select(out=maskB[:].rearrange("p (t s) -> p t s", s=16), in_=maskB[:].rearrange("p (t s) -> p t s", s=16),
                                             pattern=[[-16, 8], [0, 16]], compare_op=ALU.is_ge, fill=0.0, base=0,
                                             channel_multiplier=1), r=kk, w=kk)
            Lr, Li, Ls = T3[:, :, 0], T3[:, :, 1], T3[:, :, 2]
            s_ = {k: v[:] for k, v in sm.items()}

            def tt(o, a, b, op): dv(lambda e: e.tensor_tensor(out=o, in0=a, in1=b, op=op))

            def ts(o, a, m, add): dv(lambda e: e.tensor_scalar(out=o, in0=a, scalar1=m, scalar2=add, op0=ALU.mult, op1=ALU.add))
            ac(lambda e: e.activation(out=s_["step"], in_=Ls, func=AF.Exp))
            tt(s_["tA"], Lr, s_["step"], ALU.mult)
            ac(lambda e: e.activation(out=s_["rho"], in_=s_["tA"], func=AF.Exp))
            tt(s_["th"], Li, s_["step"], ALU.mult)
            ts(s_["tq"], s_["th"], 1.0 / TWO_PI, 0.0)
            dv(lambda e: e.tensor_copy(out=tiq[:], in_=s_["tq"]))
            dv(lambda e: e.tensor_copy(out=s_["tf"], in_=tiq[:]))
            tt(s_["r"], s_["tq"], s_["tf"], ALU.subtract)
            ac(lambda e: e.activation(out=s_["s1"], in_=s_["r"], func=AF.Sin, scale=math.pi))
            ac(lambda e: e.activation(out=s_["hs"], in_=s_["r"], func=AF.Sin, scale=math.pi / 2))
            tt(s_["tA"], s_["hs"], s_["hs"], ALU.mult); ts(s_["c1"], s_["tA"], -2.0, 1.0)
            tt(s_["tA"], s_["s1"], s_["c1"], ALU.mult); ts(s_["sn"], s_["tA"], 2.0, 0.0)
            tt(s_["tA"], s_["s1"], s_["s1"], ALU.mult); ts(s_["cs"], s_["tA"], -2.0, 1.0)
            tt(s_["ar"], s_["rho"], s_["cs"], ALU.mult); tt(s_["ai"], s_["rho"], s_["sn"], ALU.mult)
            tt(s_["tA"], Lr, Lr, ALU.mult); tt(s_["tB"], Li, Li, ALU.mult); tt(s_["den"], s_["tA"], s_["tB"], ALU.add)
            dv(lambda e: e.reciprocal(out=s_["den"], in_=s_["den"]))
            ts(s_["am1"], s_["ar"], 1.0, -1.0)
            tt(s_["tA"], s_["am1"], Lr, ALU.mult); tt(s_["tB"], s_["ai"], Li, ALU.mult); tt(s_["nr"], s_["tA"], s_["tB"], ALU.add)
            tt(s_["tA"], s_["ai"], Lr, ALU.mult); tt(s_["tB"], s_["am1"], Li, ALU.mult); tt(s_["ni"], s_["tA"], s_["tB"], ALU.subtract)
            tt(s_["cr"], s_["nr"], s_["den"], ALU.mult); tt(s_["ci"], s_["ni"], s_["den"], ALU.mult)
            tt(s_["tA"], s_["rho"], s_["rho"], ALU.mult)
            dv(lambda e: e.reciprocal(out=s_["tA"], in_=s_["tA"]))
            tt(s_["ir"], s_["ar"], s_["tA"], ALU.mult); tt(s_["tB"], s_["ai"], s_["tA"], ALU.mult); ts(s_["ii"], s_["tB"], -1.0, 0.0)
            for (W_, xr, xi) in ((PW, s_["ar"], s_["ai"]), (NW, s_["ir"], s_["ii"])):
                dv(lambda e, W_=W_: e.memset(W_[:, 0, :, 0], 1.0)); dv(lambda e, W_=W_: e.memset(W_[:, 1, :, 0], 0.0))
                for k in range(1, 9):
                    cmul(P.dve, W_[:, 0, :, k], W_[:, 1, :, k], W_[:, 0, :, k - 1], W_[:, 1, :, k - 1], xr, xi, s_["tA"], s_["tB"], kk)
            dv(lambda e: e.tensor_copy(out=MU[:, 0, 0, :], in_=PW[:, 0, :, 8])); dv(lambda e: e.tensor_copy(out=MU[:, 1, 0, :], in_=PW[:, 1, :, 8]))
            for l in range(1, 10):
                cmul(P.dve, MU[:, 0, l, :], MU[:, 1, l, :], MU[:, 0, l - 1, :], MU[:, 1, l - 1, :], MU[:, 0, l - 1, :], MU[:, 1, l - 1, :],
                     s_["tA"], s_["tB"], kk)
            dv(lambda e: e.tensor_scalar(out=NMI[:], in0=MU[:, 1], scalar1=-1.0, scalar2=None, op0=ALU.mult))
            b1 = big1[:, :, 0, :]; b2 = big2[:, :, 0, :]
            cmul(P.dve, bbar[:, 0], bbar[:, 1], bcast(s_["cr"].unsqueeze(2), [128, 32, 16]), bcast(s_["ci"].unsqueeze(2), [128, 32, 16]),
                 B4[:, :, 0, :], B4[:, :, 1, :], b1, b2, kk)
            for c in range(2):
                dv(lambda e, c=c: e.tensor_copy(out=QA[:, c, 0:16, :], in_=NW[:, c, 0:16, :]))
                dv(lambda e, c=c: e.tensor_copy(out=QA[:, c, 16:32, :], in_=PW[:, c, 16:32, :]))
                dv(lambda e, c=c: e.tensor_copy(out=QB[:, c, 0:16, :], in_=PW[:, c, 0:16, :]))
                dv(lambda e, c=c: e.tensor_copy(out=QB[:, c, 16:32, :], in_=NW[:, c, 16:32, :]))
            sh8 = [128, 32, 8, 16]; sh9 = [128, 32, 9, 16]
            cmul(P.dve, XB[:, 0], XB[:, 1], bcast(QA[:, 0, :, 0:8].unsqueeze(3), sh8), bcast(QA[:, 1, :, 0:8].unsqueeze(3), sh8),
                 bcast(bbar[:, 0].unsqueeze(2), sh8), bcast(bbar[:, 1].unsqueeze(2), sh8), big1[:, :, 0:8, :], big2[:, :, 0:8, :], kk)
            cmul(P.dve, XC[:, 0], XC[:, 1], bcast(QB[:, 0, :, :].unsqueeze(3), sh9), bcast(QB[:, 1, :, :].unsqueeze(3), sh9),
                 bcast(C4[:, :, 0, :].unsqueeze(2), sh9), bcast(C4[:, :, 1, :].unsqueeze(2), sh9), big1[:], big2[:], kk)
            for c in range(2):
                dv(lambda e, c=c: e.tensor_copy(out=WST[:, c, 16:32, :].rearrange("p u (k s) -> p u k s", s=16), in_=XB[:, c, 16:32]))
            sh16 = [128, 16, 8, 16]
            cmul(P.dve, WST[:, 0, 0:16, :].rearrange("p u (k s) -> p u k s", s=16), WST[:, 1, 0:16, :].rearrange("p u (k s) -> p u k s", s=16),
                 bcast(PW[:, 0, 0:16, 7:8].unsqueeze(3), sh16), bcast(PW[:, 1, 0:16, 7:8].unsqueeze(3), sh16),
                 XB[:, 0, 0:16], XB[:, 1, 0:16], big1[:, 0:16, 0:8, :], big2[:, 0:16, 0:8, :], kk)
            dv(lambda e: e.tensor_copy(out=WOUT[:, 0:16, 0, :].rearrange("p u (k s) -> p u k s", s=16), in_=XC[:, 0, 0:16, 1:9, :]))
            dv(lambda e: e.tensor_scalar(out=WOUT[:, 0:16, 1, :].rearrange("p u (k s) -> p u k s", s=16), in0=XC[:, 1, 0:16, 1:9, :], scalar1=-1.0, scalar2=None, op0=ALU.mult))
            wob_r = WOUT[:, 16:32, 0, :].rearrange("p u (k s) -> p u k s", s=16)
            wob_i = WOUT[:, 16:32, 1, :].rearrange("p u (k s) -> p u k s", s=16)
            cmul(P.dve, wob_r, wob_i, bcast(PW[:, 0, 16:32, 8:9].unsqueeze(3), sh16), bcast(PW[:, 1, 16:32, 8:9].unsqueeze(3), sh16),
                 XC[:, 0, 16:32, 0:8, :], XC[:, 1, 16:32, 0:8, :], big1[:, 0:16, 0:8, :], big2[:, 0:16, 0:8, :], kk)
            dv(lambda e: e.tensor_scalar(out=wob_i, in0=wob_i, scalar1=-1.0, scalar2=None, op0=ALU.mult))
            P.pool(lambda e: e.memset(WSTAB[:], 0.0), r=kk, w=kk)
            for u in range(32):
                for c in range(2):
                    P.pe(lambda e, u=u, c=c: e.transpose(ptr[:, c, :], WST[:, c, u, :], K.ident[:]), r=kk + ["ident"], w=kk)
                dv(lambda e, u=u: e.tensor_copy(out=WSTAB[:, u, :, 0, 0:64], in_=ptr[:, :, 0:64]))
                dv(lambda e, u=u: e.tensor_copy(out=WSTAB[:, u, :, 1, 64:128], in_=ptr[:, :, 64:128]))
            for g in range(32):
                gp, g2 = g // 2, g % 2
                rows = slice(g2 * 64, g2 * 64 + 64)
                for (u, pt_, pt2_) in ((gp, ptf, ptf2), (16 + gp, ptb, ptb2)):
                    P.pe(lambda e, u=u, pt_=pt_, rows=rows: e.matmul(pt_[:], lhsT=XB[rows, 0, u].rearrange("p k s -> p (k s)"),
                                                                    rhs=XC[rows, 0, u, 0:8, :].rearrange("p k s -> p (k s)"),
                                                                    start=True, stop=True), r=kk, w=kk)
                    P.pe(lambda e, u=u, pt2_=pt2_, rows=rows: e.matmul(pt2_[:], lhsT=XB[rows, 1, u].rearrange("p k s -> p (k s)"),
                                                                      rhs=XC[rows, 1, u, 0:8, :].rearrange("p k s -> p (k s)"),
                                                                      start=True, stop=True), r=kk, w=kk)
                dv(lambda e: e.tensor_copy(out=tg[:], in_=ptf[:]))
                tt(tg[:], tg[:], ptf2[:], ALU.subtract)
                tt(tg[:], tg[:], maskF[:], ALU.mult)
                dv(lambda e: e.tensor_copy(out=tb[:], in_=ptb[:]))
                tt(tb[:], tb[:], ptb2[:], ALU.subtract)
                tt(tb[:], tb[:], maskB[:], ALU.mult)
                tt(tg[:], tg[:], tb[:], ALU.add)
                dv(lambda e, g=g: e.scalar_tensor_tensor(out=TOEP[:, g, :], in0=K.identf[:], scalar=dd[:, g:g + 1], in1=tg[:],
                                                         op0=ALU.mult, op1=ALU.add))
            P.emit()
        s5_main(nc, P, I, S, K, TOEP, WSTAB, WOUT, MU, NMI)


def s5_main(nc, P, I, S, K, TOEP, WSTAB, WOUT, MU, NMI):
    from contextlib import ExitStack
    blocks = [(0, 32), (32, 544), (544, 1056)]
    with ExitStack() as es:
        def sb(name, shape, dt=F32):
            return es.enter_context(nc.sbuf_tensor(name, list(shape), dt))
        Ug2 = [sb("Ug%d" % i, [128, 8, NCH], BF) for i in range(2)]
        SCr = sb("SCr", [128, 8, NCH]); SCi = sb("SCi", [128, 8, NCH])
        Hr = sb("Hr", [128, 8, NCH], BF); Hi = sb("Hi", [128, 8, NCH], BF)
        tmpall = sb("stmpall", [128, 2 * 1024])
        tsm = [sb("stsm%d" % i, [128, 4, 64]) for i in range(2)]
        YGs = [sb("YGs%d" % i, [128, NCH], BF) for i in range(2)]

        def bigt(i):
            return tmpall[:, 1024 * i:1024 * (i + 1)].rearrange("p (a b) -> p a b", a=2)
        tmp = [tsm[0][:], tsm[1][:], None, None, bigt(0), bigt(1)]
        tot = sb("tot", [128, 2, 2, 4, 1])
        pS = [es.enter_context(nc.psum_tensor("pS%d" % i, [128, 512], F32)) for i in range(2)]
        pY = [es.enter_context(nc.psum_tensor("pY%d" % i, [128, 512], F32)) for i in range(2)]
        cnt = {"n": 0, "yg": 0}

        def load_u(gb):
            P.dma(SP, Ug2[gb % 2][:], S["U"][8 * gb:8 * gb + 8].rearrange("g p c -> p g c"), r=["Usc"], w=["Ug%d" % (gb % 2)])

        def s_job(gb, d_):
            Ug = Ug2[gb % 2]; uk = "Ug%d" % (gb % 2)
            for i in range(4):
                q = d_ * 4 + i; u = d_ * 16 + 4 * gb + i
                for c in range(2):
                    SC = SCr if c == 0 else SCi
                    for (c0, c1) in blocks:
                        n = cnt["n"]; cnt["n"] += 1
                        p_ = pS[n % 2]; pk = "pS%d" % (n % 2)
                        P.pe(lambda e, p_=p_, u=u, c=c, i=i, c0=c0, c1=c1, Ug=Ug: e.matmul(
                            p_[:, 0:c1 - c0], lhsT=WSTAB[:, u, c, 0, :], rhs=Ug[:, 2 * i, c0:c1], start=True, stop=False),
                            r=[uk], w=[pk])
                        P.pe(lambda e, p_=p_, u=u, c=c, i=i, c0=c0, c1=c1, Ug=Ug: e.matmul(
                            p_[:, 0:c1 - c0], lhsT=WSTAB[:, u, c, 1, :], rhs=Ug[:, 2 * i + 1, c0:c1], start=False, stop=True),
                            r=[uk], w=[pk])
                        P.act(lambda e, p_=p_, SC=SC, q=q, c0=c0, c1=c1: e.activation(out=SC[:, q, c0:c1], in_=p_[:, 0:c1 - c0],
                                                                                  func=AF.Copy), r=[pk], w=["sc%d" % d_])

        def scan_job(gb, d_):
            us = slice(d_ * 16 + 4 * gb, d_ * 16 + 4 * gb + 4)
            qs = slice(d_ * 4, d_ * 4 + 4)
            vrc, vic = SCr[:, qs, 0:32], SCi[:, qs, 0:32]
            vrl, vil = SCr[:, qs, 32:NCH], SCi[:, qs, 32:NCH]
            if d_ == 1:
                vrc, vic, vrl, vil = vrc[:, :, ::-1], vic[:, :, ::-1], vrl[:, :, ::-1], vil[:, :, ::-1]
            tt_ = (tot[:, d_, 0], tot[:, d_, 1])
            s5_scan(P.dve, vrc, vic, 32, 5, MU, NMI, us, None, tt_, tmp, "sc%d" % d_, "stmp0", True)
            s5_scan(P.dve, vrl, vil, 1024, 10, MU, NMI, us, tt_, None, tmp, "sc%d" % d_, "stmp0", True)
            P.act(lambda e, qs=qs: e.activation(out=Hr[:, qs, :], in_=SCr[:, qs, :], func=AF.Copy), r=["sc%d" % d_], w=["Hr%d" % d_])
            P.pool(lambda e, qs=qs: e.tensor_copy(out=Hi[:, qs, :], in_=SCi[:, qs, :]), r=["sc%d" % d_], w=["Hi%d" % d_])

        def y_job(gb):
            Ug = Ug2[gb % 2]; uk = "Ug%d" % (gb % 2)
            for i8 in range(8):
                g = 8 * gb + i8; gpl = i8 // 2; g2 = i8 % 2
                rows = slice(g2 * 64, g2 * 64 + 64)
                yb = cnt["yg"] % 2; cnt["yg"] += 1
                yg = YGs[yb]; yk = "YGs%d" % yb
                for (c0, c1) in blocks:
                    n = cnt["n"]; cnt["n"] += 1
                    p_ = pY[n % 2]; pk = "pY%d" % (n % 2)
                    w_ = c1 - c0
                    P.pe(lambda e, p_=p_, g=g, i8=i8, c0=c0, c1=c1, w_=w_, Ug=Ug: e.matmul(p_[:, 0:w_], lhsT=TOEP[:, g, :], rhs=Ug[:, i8, c0:c1],
                                                                                        start=True, stop=False), r=[uk], w=[pk])
                    for d_ in range(2):
                        q = d_ * 4 + gpl; u = d_ * 16 + 4 * gb + gpl
                        P.pe(lambda e, p_=p_, u=u, q=q, c0=c0, c1=c1, w_=w_, rows=rows: e.matmul(
                            p_[:, 0:w_], lhsT=WOUT[rows, u, 0, :], rhs=Hr[rows, q, c0:c1], start=False, stop=False), r=["Hr%d" % d_], w=[pk])
                        P.pe(lambda e, p_=p_, u=u, q=q, c0=c0, c1=c1, w_=w_, rows=rows, d_=d_: e.matmul(
                            p_[:, 0:w_], lhsT=WOUT[rows, u, 1, :], rhs=Hi[rows, q, c0:c1], start=False, stop=(d_ == 1)), r=["Hi%d" % d_], w=[pk])
                    P.act(lambda e, p_=p_, yg=yg, c0=c0, c1=c1, w_=w_: e.activation(out=yg[:, c0:c1], in_=p_[:, 0:w_], func=AF.Gelu),
                          r=[pk], w=[yk])
                for t in range(8):
                    q_ = SP if t % 2 == 0 else ACT
                    P.dma(q_, S["YT"][g * 16:(g + 1) * 16, t, :], yg[t * 16:(t + 1) * 16, :], r=[yk], w=["YTsc"])

        load_u(0)
        s_job(0, 0); s_job(0, 1)
        for gb in range(4):
            if gb + 1 < 4:
                load_u(gb + 1)
            scan_job(gb, 0)
            if gb + 1 < 4:
                s_job(gb + 1, 0)
            scan_job(gb, 1)
            if gb + 1 < 4:
                s_job(gb + 1, 1)
            y_job(gb)
        P.emit()
    s5_glu(nc, P, I, S, K)


def s5_glu(nc, P, I, S, K):
    from contextlib import ExitStack
    CW = 256
    with ExitStack() as es:
        def sb(name, shape, dt=F32):
            return es.enter_context(nc.sbuf_tensor(name, list(shape), dt))
        YT = sb("YT", [128, 4, 8, NCH], BF)
        Wg = sb("Wglu", [128, 4, 512], BF); bg = sb("bglu", [128, 4])
        sg = [sb("sg%d" % i, [128, 4, CW], BF) for i in range(2)]
        so = [sb("so%d" % i, [128, 4, CW], BF) for i in range(2)]
        stm = [sb("stm%d" % i, [128, 512], BF) for i in range(3)]
        pz = [es.enter_context(nc.psum_tensor("pz%d" % i, [128, 4, CW], F32)) for i in range(2)]
        pt = [es.enter_context(nc.psum_tensor("ptg%d" % i, [128, 4, 128], BF)) for i in range(2)]
        for fc in range(4):
            P.dma(SP, YT[:, fc], S["YT"][fc * 128:(fc + 1) * 128], r=["YTsc"], w=["YT"])
        P.dma(POOL, Wg[:], I["s5_w_glu"].rearrange("(kc p) n -> p kc n", p=128), w=["Wglu"])
        P.dma(SP, bg[:], I["s5bg"][:, :], w=["bglu"])
        s5v = S["S5"].rearrange("(c t) f -> t c f", t=8)
        n = 0; nt_ = 0
        cblocks = [(0, 32)] + [(32 + CW * k, 32 + CW * (k + 1)) for k in range(1024 // CW)]
        for t in range(8):
            for (c0, c1) in cblocks:
                b = n % 2; n += 1
                w_ = c1 - c0
                for fo in range(4):
                    for fi in range(4):
                        P.pe(lambda e, b=b, fo=fo, fi=fi, t=t, c0=c0, c1=c1, w_=w_: e.matmul(
                            pz[b][:, fo, 0:w_], lhsT=Wg[:, fi, fo * 128:(fo + 1) * 128], rhs=YT[:, fi, t, c0:c1],
                            start=(fi == 0), stop=(fi == 3)), r=["YT", "Wglu"], w=["pz%d" % b])
                    P.act(lambda e, b=b, fo=fo, w_=w_: e.activation(out=sg[b][:, fo, 0:w_], in_=pz[b][:, fo, 0:w_], func=AF.Sigmoid,
                                                                    bias=bg[:, fo:fo + 1], scale=1.0), r=["pz%d" % b, "bglu"], w=["sg%d" % b])
                P.dve(lambda e, b=b, t=t, c0=c0, c1=c1, w_=w_: e.tensor_tensor(out=so[b][:, :, 0:w_], in0=YT[:, :, t, c0:c1],
                                                                              in1=sg[b][:, :, 0:w_], op=ALU.mult),
                      r=["YT", "sg%d" % b], w=["so%d" % b])
                for s0 in range(0, w_, 128):
                    sw = min(128, w_ - s0)
                    tb = nt_ % 2; sbi = nt_ % 3; nt_ += 1
                    for fo in range(4):
                        P.pe(lambda e, b=b, fo=fo, s0=s0, sw=sw, tb=tb: e.transpose(pt[tb][0:sw, fo, :], so[b][:, fo, s0:s0 + sw], K.ident[:]),
                             r=["so%d" % b, "ident"], w=["ptg%d" % tb])
                    P.dve(lambda e, sw=sw, tb=tb, sbi=sbi: e.tensor_copy(out=stm[sbi][0:sw, :], in_=pt[tb][0:sw].rearrange("p a f -> p (a f)")),
                          r=["ptg%d" % tb], w=["stm%d" % sbi])
                    P.dma(ACT, s5v[t, c0 + s0:c0 + s0 + sw, :], stm[sbi][0:sw, :], r=["stm%d" % sbi], w=["S5sc"])
        P.emit()


MLA_SCALE = 192 ** -0.5
WIN_SCALE = 64 ** -0.5


def phase3_mla(nc, P, I, S, K):
    from contextlib import ExitStack
    with ExitStack() as es0:
        def sbp(name, shape, dt):
            return es0.enter_context(nc.sbuf_tensor(name, list(shape), dt))
        QTa = sbp("QTa", [128, 4, NOWN], BF); QTb = sbp("QTb", [128, 4, NOWN], BF); OT = sbp("OT", [128, 4, NOWN], BF)
        modv = sbp("modv3", [128, 2, 3 * D], F32)
        P.dma(SP, modv[:], S["mod"][0][:, :, 0:3 * D], r=["modsc0"], w=["modv3"])
        with ExitStack() as es:
            def sb(name, shape, dt=F32):
                return es.enter_context(nc.sbuf_tensor(name, list(shape), dt))

            def ps(name, shape, dt=F32):
                return es.enter_context(nc.psum_tensor(name, list(shape), dt))
            Wqi = sb("Wqi", [128, KC, 384], BF); Wqb = sb("Wqb", [128, 3, 768], BF)
            gqa = sb("gqa", [128, 384]); gq = sb("gq", [128, 192])
            xt = [sb("xq%d" % i, [128, D]) for i in range(2)]; rp = [sb("rq%d" % i, [128, 2, 64]) for i in range(2)]
            tmp2 = [sb("tmpq%d" % i, [128, D]) for i in range(2)]; a_bf2 = [sb("a_bfq%d" % i, [128, D], BF) for i in range(2)]
            aT = [sb("aTq%d" % i, [128, KC, 128], BF) for i in range(2)]
            st2 = [sb("stq%d" % i, [128, 16]) for i in range(2)]
            cq_bf2 = [sb("cq_bf%d" % i, [128, 384], BF) for i in range(2)]; cqT2 = [sb("cqT%d" % i, [128, 3, 128], BF) for i in range(2)]
            qf2 = [sb("qf%d" % i, [128, 4, 192]) for i in range(2)]; qsq2 = [sb("qsq%d" % i, [128, 4, 192]) for i in range(2)]
            t12 = [sb("t1q%d" % i, [128, 4, 64]) for i in range(2)]
            Q_bf2 = [sb("Q_bf%d" % i, [128, 4, 192], BF) for i in range(2)]
            pTa = ps("pTaq", [128, 8, 128], BF); pQ = ps("pQ", [128, 384]); pcq = ps("pcq", [128, 3, 128], BF)
            pQB = ps("pQB", [128, 1024]); pTq = ps("pTq", [128, 8, 128], BF)
            wv = I["a_w_in"].rearrange("(kc p) n -> p kc n", p=128)
            P.dma(POOL, Wqi[:], wv[:, :, 512:896], w=["Wqi"])
            P.dma(POOL, Wqb[:], I["mla_w_q_b"].rearrange("(kc p) n -> p kc n", p=128), w=["Wqb"])
            P.dma(SP, gqa[:], bcast(I["mla_qa_norm"][0:1, :], [128, 384]), w=["gqa"])
            P.dma(SP, gq[:], bcast(I["mla_q_norm"][0:1, :], [128, 192]), w=["gq"])
            P.dve(lambda e: e.tensor_scalar(out=gq[:], in0=gq[:], scalar1=MLA_SCALE, scalar2=None, op0=ALU.mult), r=["gq"], w=["gq"])
            def q_tile(t):
                b = t % 2; typ = 1 if t < 2 else 0
                sfx = "_%d" % b
                tmp, a_bf, st, cq_bf, cqT, qf, qsq, t1, Q_bf = (tmp2[b], a_bf2[b], st2[b], cq_bf2[b], cqT2[b], qf2[b], qsq2[b], t12[b], Q_bf2[b])
                rows = slice(t * 128, (t + 1) * 128)
                P.dma(SP, xt[b][:], I["xo"][rows, :], w=["xq%d" % b])
                P.dma(SP, rp[b][:], I["ropeA_o"][rows, :, :], w=["rq%d" % b])
                norm_mod_T(P, K, xt[b][:], "xq%d" % b, modv[:, typ, D:2 * D], modv[:, typ, 0:D], "modv3",
                           tmp[:], st[:, 0:1], st[:, 1:2], tmp[:], a_bf[:], pTa, aT[b][:], "p3" + sfx, "aTq%d" % b, pTkey="p3pT")
                for kc in range(KC):
                    P.pe(lambda e, kc=kc, b=b: e.matmul(pQ[:], lhsT=aT[b][:, kc, :], rhs=Wqi[:, kc, :], start=(kc == 0), stop=(kc == KC - 1)),
                         r=["aTq%d" % b, "Wqi"], w=["pQ"])
                P.act(lambda e: e.activation(out=tmp[:, 0:384], in_=pQ[:], func=AF.Square), r=["pQ"], w=["p3" + sfx + "tmp"])
                P.dve(lambda e: e.reduce_sum(out=st[:, 2:3], in_=tmp[:, 0:384], axis=AX.X), r=["p3" + sfx + "tmp"], w=["qassq" + sfx])
                rstd_from_ssq(P, st[:, 2:3], st[:, 3:4], 1, 384, "qa", sfx)
                P.dve(lambda e: e.scalar_tensor_tensor(out=cq_bf[:], in0=pQ[:], scalar=st[:, 3:4], in1=gqa[:], op0=ALU.mult, op1=ALU.mult),
                      r=["pQ", "qarstd" + sfx, "gqa"], w=["cq_bf" + sfx])
                for k3 in range(3):
                    P.pe(lambda e, k3=k3: e.transpose(pcq[:, k3, :], cq_bf[:, k3 * 128:(k3 + 1) * 128], K.ident[:]), r=["cq_bf" + sfx, "ident"], w=["pcq"])
                P.dve(lambda e: e.tensor_copy(out=cqT[:], in_=pcq[:]), r=["pcq"], w=["cqT" + sfx])
                for (c0, c1) in ((0, 512), (512, 768)):
                    for k3 in range(3):
                        P.pe(lambda e, k3=k3, c0=c0, c1=c1: e.matmul(pQB[:, c0:c1], lhsT=cqT[:, k3, :], rhs=Wqb[:, k3, c0:c1],
                                                                     start=(k3 == 0), stop=(k3 == 2)), r=["cqT" + sfx, "Wqb"], w=["pQB"])
                P.act(lambda e: e.activation(out=qf[:], in_=pQB[:, 0:768].rearrange("p (h c) -> p h c", h=4), func=AF.Copy), r=["pQB"], w=["qf" + sfx])
                P.dve(lambda e: e.tensor_tensor(out=qsq[:], in0=qf[:], in1=qf[:], op=ALU.mult), r=["qf" + sfx], w=["qsq" + sfx])
                P.dve(lambda e: e.reduce_sum(out=st[:, 4:8], in_=qsq[:], axis=AX.X), r=["qsq" + sfx], w=["qssq" + sfx])
                rstd_from_ssq(P, st[:, 4:8], st[:, 8:12], 4, 192, "q", sfx)
                P.dve(lambda e: e.tensor_tensor(out=qf[:], in0=qf[:], in1=bcast(st[:, 8:12].unsqueeze(2), [128, 4, 192]), op=ALU.mult),
                      r=["qf" + sfx, "qrstd" + sfx], w=["qf" + sfx])
                P.pool(lambda e: e.tensor_tensor(out=qf[:], in0=qf[:], in1=bcast(gq[:].unsqueeze(1), [128, 4, 192]), op=ALU.mult),
                       r=["qf" + sfx, "gq"], w=["qf" + sfx])
                rope_inplace(P, P.pool, P.dve, qf[:, :, 128:192], rp[b], t1[:], 4, ["rq%d" % b], "qf" + sfx, "t1q" + sfx)
                P.act(lambda e: e.activation(out=Q_bf[:], in_=qf[:], func=AF.Copy), r=["qf" + sfx], w=["Q_bf" + sfx])
                for h in range(4):
                    P.pe(lambda e, h=h: e.transpose(pTq[:, 2 * h, :], Q_bf[:, h, 0:128], K.ident[:]), r=["Q_bf" + sfx, "ident"], w=["pTq"])
                    P.pe(lambda e, h=h: e.transpose(pTq[0:64, 2 * h + 1, :], Q_bf[:, h, 128:192], K.ident[:]), r=["Q_bf" + sfx, "ident"], w=["pTq"])
                pv = pTq[:].rearrange("p (h two) t -> p h two t", two=2)
                P.dve(lambda e, pv=pv, rows=rows: e.tensor_copy(out=QTa[:, :, rows], in_=pv[:, :, 0, :]), r=["pTq"], w=["QTa_%d" % t])
                P.dve(lambda e, pv=pv, rows=rows: e.tensor_copy(out=QTb[0:64, :, rows], in_=pv[0:64, :, 1, :]), r=["pTq"], w=["QTb_%d" % t])
            tiles = [P.capture(lambda t=t: q_tile(t)) for t in range(NTO)]
            K.q_stages = len(tiles[2])
            P.run_staged(tiles, skew=max(1, (len(tiles[2]) + 1) // 2))
            P.emit()
        with ExitStack() as es:
            def sb(name, shape, dt=F32):
                return es.enter_context(nc.sbuf_tensor(name, list(shape), dt))

            def ps(name, shape, dt=F32):
                return es.enter_context(nc.psum_tensor(name, list(shape), dt))
            KA = [sb("KA%d" % i, [128, NF], BF) for i in range(2)]; KB = [sb("KB%d" % i, [128, NF], BF) for i in range(2)]
            VH = [sb("VH%d" % i, [128, NTF, 128], BF) for i in range(2)]
            PT = [sb("PT%d" % i, [128, 512], BF) for i in range(4)]
            rec = sb("rec", [128, 512])
            pS = [ps("pSa%d" % i, [128, 512]) for i in range(2)]; pO = ps("pO", [128, 512]); pDen = ps("pDen", [128, 512])
            vsv = S["V"].rearrange("(kt p) (h d) -> p kt h d", p=128, h=4)
            blocks = [(0, 256, 2)] + [(256 + 512 * k, 256 + 512 * (k + 1), NTF) for k in range(4)] + [(2304, 2560, NTF)]
            npt = 0; nps = 0
            for h in range(4):
                hb = h % 2
                P.dma(SP, KA[hb][:], S["KTa"][:, h, :], r=["KTa"], w=["KA%d" % hb])
                P.dma(SP, KB[hb][0:64, :], S["KTb"][:, h, :], r=["KTb"], w=["KB%d" % hb])
                P.dma(POOL, VH[hb][:], vsv[:, :, h, :], r=["Vsc"], w=["VH%d" % hb])
                for (q0, q1, nkt) in blocks:
                    w_ = q1 - q0
                    pend = None
                    for kt in range(nkt + 1):
                        if kt < nkt:
                            sbi = nps % 2; nps += 1
                            ks = slice(kt * 128, (kt + 1) * 128)
                            P.pe(lambda e, sbi=sbi, hb=hb, ks=ks, q0=q0, q1=q1, w_=w_, h=h: e.matmul(
                                pS[sbi][:, 0:w_], lhsT=KA[hb][:, ks], rhs=QTa[:, h, q0:q1], start=True, stop=False),
                                r=["KA%d" % hb, "QTa"], w=["pSa%d" % sbi])
                            P.pe(lambda e, sbi=sbi, hb=hb, ks=ks, q0=q0, q1=q1, w_=w_, h=h: e.matmul(
                                pS[sbi][:, 0:w_], lhsT=KB[hb][0:64, ks], rhs=QTb[0:64, h, q0:q1], start=False, stop=True),
                                r=["KB%d" % hb, "QTb"], w=["pSa%d" % sbi])
                            pi = npt % 4; npt += 1
                            P.act(lambda e, sbi=sbi, pi=pi, w_=w_: e.activation(out=PT[pi][:, 0:w_], in_=pS[sbi][:, 0:w_], func=AF.Exp),
                                  r=["pSa%d" % sbi], w=["PT%d" % pi])
                            cur = (kt, pi)
                        else:
                            cur = None
                        if pend is not None:
                            k0, p0 = pend
                            P.pe(lambda e, k0=k0, p0=p0, hb=hb, w_=w_, nkt=nkt: e.matmul(
                                pO[:, 0:w_], lhsT=VH[hb][:, k0, :], rhs=PT[p0][:, 0:w_], start=(k0 == 0), stop=(k0 == nkt - 1)),
                                r=["VH%d" % hb, "PT%d" % p0], w=["pO"])
                            P.pe(lambda e, k0=k0, p0=p0, w_=w_, nkt=nkt: e.matmul(
                                pDen[:, 0:w_], lhsT=K.ones_bf[:], rhs=PT[p0][:, 0:w_], start=(k0 == 0), stop=(k0 == nkt - 1)),
                                r=["ones_bf", "PT%d" % p0], w=["pDen"])
                        pend = cur
                    P.dve(lambda e, w_=w_: e.reciprocal(out=rec[:, 0:w_], in_=pDen[:, 0:w_]), r=["pDen"], w=["rec"])
                    P.dve(lambda e, w_=w_, h=h, q0=q0, q1=q1: e.tensor_tensor(out=OT[:, h, q0:q1], in0=pO[:, 0:w_], in1=rec[:, 0:w_], op=ALU.mult),
                          r=["pO", "rec"], w=["OT"])
            P.emit()
        with ExitStack() as es:
            def sb(name, shape, dt=F32):
                return es.enter_context(nc.sbuf_tensor(name, list(shape), dt))

            def ps(name, shape, dt=F32):
                return es.enter_context(nc.psum_tensor(name, list(shape), dt))
            Wo = sb("Wo", [128, KC, D], BF)
            oi = [sb("oi%d" % i, [128, 1], I32) for i in range(2)]
            s5g = [sb("s5g%d" % i, [128, 512], BF) for i in range(2)]
            s5T = [sb("s5T%d" % i, [128, 4, 128], BF) for i in range(2)]
            xt = [sb("xo%d" % i, [128, D]) for i in range(2)]; tmp = sb("tmpo", [128, D]); h1 = [sb("h1_%d" % i, [128, D]) for i in range(2)]
            pT = ps("pTo", [128, 4, 128], BF); pOut = [ps("pOut%d" % i, [128, D]) for i in range(2)]
            P.dma(POOL, Wo[:], I["a_w_out"].rearrange("(kc p) n -> p kc n", p=128), w=["Wo"])
            for t in range(NTO):
                b = t % 2; typ = 1 if t < 2 else 0
                rows = slice(t * 128, (t + 1) * 128)
                P.dma(SP, xt[b][:], I["xo"][rows, :], w=["xo%d" % b])
                P.dma(SP, oi[b][:], I["oidx"][rows, :], w=["oi%d" % b])
                P.add(POOL, lambda e, b=b: e.indirect_dma_start(out=s5g[b][:], out_offset=None, in_=S["S5"][:, :],
                                                                in_offset=bass.IndirectOffsetOnAxis(ap=oi[b][:, 0:1], axis=0)),
                      r=["oi%d" % b, "S5sc"], w=["s5g%d" % b], dma=True)
                for fc in range(4):
                    P.pe(lambda e, b=b, fc=fc: e.transpose(pT[:, fc, :], s5g[b][:, fc * 128:(fc + 1) * 128], K.ident[:]),
                         r=["s5g%d" % b, "ident"], w=["pTo"])
                P.act(lambda e, b=b: e.activation(out=s5T[b][:], in_=pT[:], func=AF.Copy), r=["pTo"], w=["s5T%d" % b])
                for half in range(2):
                    hs = slice(half * 512, (half + 1) * 512)
                    for ch in range(8):
                        lhs = s5T[b][:, ch, :] if ch < 4 else OT[:, ch - 4, rows]
                        P.pe(lambda e, b=b, ch=ch, hs=hs, lhs=lhs: e.matmul(pOut[b][:, hs], lhsT=lhs, rhs=Wo[:, ch, hs], start=(ch == 0), stop=(ch == 7)),
                             r=["s5T%d" % b, "OT", "Wo"], w=["pOut%d" % b])
                P.dve(lambda e, b=b, typ=typ: e.tensor_tensor(out=tmp[:], in0=pOut[b][:], in1=modv[:, typ, 2 * D:3 * D], op=ALU.mult),
                      r=["pOut%d" % b, "modv3"], w=["tmpo"])
                P.pool(lambda e, b=b: e.tensor_tensor(out=h1[b][:], in0=tmp[:], in1=xt[b][:], op=ALU.add), r=["tmpo", "xo%d" % b], w=["h1_%d" % b])
                P.dma(ACT, S["H1"][rows, :], h1[b][:], r=["h1_%d" % b], w=["H1sc"])
            P.emit()


def phase_ffn(nc, P, I, S, K, layer, Hin, Hout, n_ctx_tiles, ntiles):
    from contextlib import ExitStack
    tag = "f%d" % layer
    NBT = 3
    WCH = [(0, 6), (6, 12), (12, 17), (17, 22)]
    with ExitStack() as es:
        def sb(name, shape, dt=F32):
            return es.enter_context(nc.sbuf_tensor(tag + name, list(shape), dt))

        def ps(name, shape, dt=F32):
            return es.enter_context(nc.psum_tensor(tag + name, list(shape), dt))
        Wg = sb("Wg", [128, KC, HID], BF); Wu = sb("Wu", [128, KC, HID], BF); Wd = sb("Wd", [128, HC, D], BF)
        modv = sb("modv", [128, 2, 3 * D])
        xn = [sb("xn%d" % i, [128, D]) for i in range(2)]; xr = sb("xr", [128, D]); tmp = sb("tmp", [128, D]); a_bf = sb("a_bf", [128, D], BF)
        aT4 = [sb("aT4_%d" % i, [128, KC, NBT * 128], BF) for i in range(2)]; actT = sb("actT", [128, HC, NBT * 128], BF)
        sg = [sb("sg%d" % i, [128, NBT * 128]) for i in range(2)]
        st = sb("st", [128, 4])
        pTa = ps("pTa", [128, 8, 128], BF)
        pG = [ps("pG%d" % i, [128, 512]) for i in range(2)]; pU = [ps("pU%d" % i, [128, 512]) for i in range(2)]
        pD = ps("pD", [128, D])
        P.dma(SP, modv[:], S["mod"][layer][:, :, 3 * D:6 * D], r=["modsc%d" % layer], w=[tag + "modv"])
        wgv = I["ffn_w_gate"][layer].rearrange("(kc p) n -> p kc n", p=128)
        wuv = I["ffn_w_up"][layer].rearrange("(kc p) n -> p kc n", p=128)
        wdv = I["ffn_w_down"][layer].rearrange("(j p) n -> p j n", p=128)
        for ci, (j0, j1) in enumerate(WCH):
            cs_ = slice(j0 * 128, j1 * 128)
            P.dma(POOL, Wg[:, :, cs_], wgv[:, :, cs_], w=[tag + "Wg%d" % ci])
            P.dma(POOL, Wu[:, :, cs_], wuv[:, :, cs_], w=[tag + "Wu%d" % ci])
        for ci, (j0, j1) in enumerate(WCH):
            P.dma(POOL, Wd[:, j0:j1, :], wdv[:, j0:j1, :], w=[tag + "Wd%d" % ci])
        wch_of = {}
        for ci, (j0, j1) in enumerate(WCH):
            for j in range(j0, j1):
                wch_of[j] = ci
        batches = []
        t0 = 0
        while t0 < ntiles:
            nb = min(NBT, ntiles - t0); batches.append((t0, nb)); t0 += nb

        def norm_tile(bi, ti):
            t0, nb = batches[bi]
            pb = bi % 2
            t = t0 + ti; typ = 1 if t < n_ctx_tiles else 0
            rows = slice(t * 128, (t + 1) * 128)
            xb_ = t % 2
            xk = tag + "xn%d" % xb_
            P.dma(SP, xn[xb_][:], Hin[rows, :], w=[xk])
            norm_mod_T(P, K, xn[xb_][:], xk, modv[:, typ, D:2 * D], modv[:, typ, 0:D], tag + "modv",
                       tmp[:], st[:, 0:1], st[:, 1:2], tmp[:], a_bf[:], pTa, aT4[pb][:, :, ti * 128:(ti + 1) * 128], tag,
                       tag + "aT4_%d" % pb)
        for ti in range(batches[0][1]):
            norm_tile(0, ti)
        for bi, (t0, nb) in enumerate(batches):
            pb = bi % 2
            aT = aT4[pb]; aTk = tag + "aT4_%d" % pb
            ncol = nb * 128
            nxt = list(range(batches[bi + 1][1])) if bi + 1 < len(batches) else []
            for j in range(HC):
                b = j % 2
                js = slice(j * 128, (j + 1) * 128)
                ci = wch_of[j]
                for kc in range(KC):
                    P.pe(lambda e, b=b, kc=kc, js=js, ncol=ncol, aT=aT: e.matmul(pG[b][:, 0:ncol], lhsT=Wg[:, kc, js], rhs=aT[:, kc, 0:ncol],
                                                                                 start=(kc == 0), stop=(kc == KC - 1)),
                         r=[tag + "Wg%d" % ci, aTk], w=[tag + "pG%d" % b])
                for kc in range(KC):
                    P.pe(lambda e, b=b, kc=kc, js=js, ncol=ncol, aT=aT: e.matmul(pU[b][:, 0:ncol], lhsT=Wu[:, kc, js], rhs=aT[:, kc, 0:ncol],
                                                                                 start=(kc == 0), stop=(kc == KC - 1)),
                         r=[tag + "Wu%d" % ci, aTk], w=[tag + "pU%d" % b])
                P.act(lambda e, b=b, ncol=ncol: e.activation(out=sg[b][:, 0:ncol], in_=pG[b][:, 0:ncol], func=AF.Silu),
                      r=[tag + "pG%d" % b], w=[tag + "sg%d" % b])
                P.dve(lambda e, b=b, j=j, ncol=ncol: e.tensor_tensor(out=actT[:, j, 0:ncol], in0=sg[b][:, 0:ncol], in1=pU[b][:, 0:ncol], op=ALU.mult),
                      r=[tag + "sg%d" % b, tag + "pU%d" % b], w=[tag + "actT"])
                if nxt and j in (2, 9, 16):
                    norm_tile(bi + 1, nxt.pop(0))
            while nxt:
                norm_tile(bi + 1, nxt.pop(0))
            for ti in range(nb):
                t = t0 + ti; typ = 1 if t < n_ctx_tiles else 0
                rows = slice(t * 128, (t + 1) * 128)
                P.dma(SP, xr[:], Hin[rows, :], w=[tag + "xr"])
                for half in range(2):
                    hs = slice(half * 512, (half + 1) * 512)
                    for j in range(HC):
                        P.pe(lambda e, j=j, ti=ti, hs=hs: e.matmul(pD[:, hs], lhsT=actT[:, j, ti * 128:(ti + 1) * 128], rhs=Wd[:, j, hs],
                                                                   start=(j == 0), stop=(j == HC - 1)),
                             r=[tag + "actT", tag + "Wd%d" % wch_of[j]], w=[tag + "pD"])
                P.dve(lambda e, typ=typ: e.tensor_tensor(out=tmp[:], in0=pD[:], in1=modv[:, typ, 2 * D:3 * D], op=ALU.mult),
                      r=[tag + "pD", tag + "modv"], w=[tag + "tmp"])
                P.pool(lambda e: e.tensor_tensor(out=xr[:], in0=xr[:], in1=tmp[:], op=ALU.add),
                       r=[tag + "tmp", tag + "xr"], w=[tag + "xr"])
                P.dma(ACT, Hout[rows, :], xr[:], r=[tag + "xr"], w=[tag + "hout"])
        P.emit()


def phase4_win(nc, P, I, S, K):
    from contextlib import ExitStack
    with ExitStack() as es:
        def sb(name, shape, dt=F32):
            return es.enter_context(nc.sbuf_tensor("w_" + name, list(shape), dt))

        def ps(name, shape, dt=F32):
            return es.enter_context(nc.psum_tensor("w_" + name, list(shape), dt))

        def two(name, shape, dt=F32):
            return [sb(name + str(i), shape, dt) for i in range(2)]
        Wi = sb("Wi", [128, KC, 1536], BF); Wo = sb("Wo", [128, 16, D], BF)
        modv = sb("modv", [128, 2, 3 * D])
        gq = sb("gq", [128, 64]); gk = sb("gk", [128, 64]); esk = sb("esk", [128, 16])
        mtmp = sb("mtmp", [128, 128]); maskP = sb("maskP", [128, 128], BF); maskN = sb("maskN", [128, 128], BF)
        KT1 = sb("KT1", [128, 4, NOWN], BF); V1 = sb("V1", [128, NTO, 256], BF)
        xt = [sb("xt%d" % i, [128, D]) for i in range(3)]; rp = [sb("rp%d" % i, [128, 2, 64]) for i in range(2)]
        tmp2 = two("tmp", [128, D]); a_bf2 = two("a_bf", [128, D], BF)
        aT = [sb("aT%d" % i, [128, KC, 128], BF) for i in range(2)]
        st2 = two("st", [128, 64])
        qf2 = two("qf", [128, 16, 64]); qsq2 = two("qsq", [128, 16, 64]); t12 = two("t1", [128, 16, 64])
        Q_bf2 = two("Q_bf", [128, 16, 64], BF); QT = [sb("QT%d" % i, [128, 16, 128], BF) for i in range(3)]
        kf2 = two("kf", [128, 4, 64]); ksq2 = two("ksq", [128, 4, 64]); K_bf2 = two("K_bf", [128, 4, 64], BF)
        PT = [sb("PT%d" % i, [128, 512], BF) for i in range(3)]
        o_bf2 = two("o_bf", [128, 16, 128], BF); den2 = two("den", [128, 512])
        pBig = ps("pBig", [128, 4, 512]); pKVx = ps("pKVx", [128, 512]); pT1 = ps("pT1", [128, 8, 128], BF)
        pS = [ps("pS%d" % i, [128, 512]) for i in range(2)]

        P.dma(POOL, Wi[:], I["c_w_in"].rearrange("(kc p) n -> p kc n", p=128), w=["w_Wi"])
        P.dma(POOL, Wo[0:64, :, :], I["c_w_out"].rearrange("(h d) n -> d h n", d=64), w=["w_Wo"])
        P.dma(SP, modv[:], S["mod"][1][:, :, 0:3 * D], r=["modsc1"], w=["w_modv"])
        P.dma(SP, gq[:], bcast(I["c_q_norm"][0:1, :], [128, 64]), w=["w_gq"])
        P.dma(SP, gk[:], bcast(I["c_k_norm"][0:1, :], [128, 64]), w=["w_gk"])
        P.dma(SP, esk[:], bcast(I["c_sink"][0:1, :], [128, 16]), w=["w_esk"])
        P.dve(lambda e: e.tensor_scalar(out=gq[:], in0=gq[:], scalar1=WIN_SCALE, scalar2=None, op0=ALU.mult), r=["w_gq"], w=["w_gq"])
        P.act(lambda e: e.activation(out=esk[:], in_=esk[:], func=AF.Exp), r=["w_esk"], w=["w_esk"])
        for (mk, pat, cm) in ((maskP, [[-1, 128]], 1), (maskN, [[1, 128]], -1)):
            P.pool(lambda e: e.memset(mtmp[:], 1.0), r=["w_mtmp"], w=["w_mtmp"])
            P.pool(lambda e, pat=pat, cm=cm: e.affine_select(out=mtmp[:], in_=mtmp[:], pattern=pat, compare_op=ALU.is_ge, fill=0.0,
                                                             base=0, channel_multiplier=cm), r=["w_mtmp"], w=["w_mtmp"])
            P.pool(lambda e, mk=mk: e.tensor_copy(out=mk[:], in_=mtmp[:]), r=["w_mtmp"], w=["w_mask"])
        cnt = {"ps": 0, "pt": 0}

        def qkv_tile(t):
            b = t % 2; xb = t % 3; typ = 1 if t < 2 else 0
            sfx = "_%d" % b
            tmp, a_bf, st, qf, qsq, t1, Q_bf, kf, ksq, K_bf = (tmp2[b], a_bf2[b], st2[b], qf2[b], qsq2[b], t12[b], Q_bf2[b],
                                                              kf2[b], ksq2[b], K_bf2[b])
            rows = slice(t * 128, (t + 1) * 128)
            xk = "w_xt%d" % xb
            P.dma(SP, xt[xb][:], S["H2"][rows, :], w=[xk])
            P.dma(SP, rp[b][:], I["ropeC_o"][rows, :, :], w=["w_rp%d" % b])
            norm_mod_T(P, K, xt[xb][:], xk, modv[:, typ, D:2 * D], modv[:, typ, 0:D], "w_modv",
                       tmp[:], st[:, 0:1], st[:, 1:2], tmp[:], a_bf[:], pT1, aT[b][:], "w_" + sfx, "w_aT%d" % b, pTkey="w_pT")
            for kc in range(KC):
                P.pe(lambda e, kc=kc, b=b: e.matmul(pKVx[:], lhsT=aT[b][:, kc, :], rhs=Wi[:, kc, 1024:1536],
                                                    start=(kc == 0), stop=(kc == KC - 1)), r=["w_aT%d" % b, "w_Wi"], w=["w_pKV"])
            if t >= 2:
                for cb in range(2):
                    for kc in range(KC):
                        P.pe(lambda e, cb=cb, kc=kc, b=b: e.matmul(pBig[:, cb, :], lhsT=aT[b][:, kc, :], rhs=Wi[:, kc, cb * 512:(cb + 1) * 512],
                                                                   start=(kc == 0), stop=(kc == KC - 1)), r=["w_aT%d" % b, "w_Wi"], w=["w_pA"])
            P.act(lambda e: e.activation(out=kf[:], in_=pKVx[:, 0:256].rearrange("p (h d) -> p h d", h=4), func=AF.Copy),
                  r=["w_pKV"], w=["w_kf" + sfx])
            P.act(lambda e, t=t: e.activation(out=V1[:, t, :], in_=pKVx[:, 256:512], func=AF.Copy), r=["w_pKV"], w=["w_V1_%d" % t])
            if t >= 2:
                P.act(lambda e: e.activation(out=qf[:], in_=pBig[:, 0:2, :].rearrange("p a (h d) -> p (a h) d", d=64), func=AF.Copy),
                      r=["w_pA"], w=["w_qf" + sfx])
            P.dve(lambda e: e.tensor_tensor(out=ksq[:], in0=kf[:], in1=kf[:], op=ALU.mult), r=["w_kf" + sfx], w=["w_ksq" + sfx])
            P.dve(lambda e: e.reduce_sum(out=st[:, 4:8], in_=ksq[:], axis=AX.X), r=["w_ksq" + sfx], w=["w_kssq" + sfx])
            rstd_from_ssq(P, st[:, 4:8], st[:, 8:12], 4, 64, "w_k", sfx)
            P.dve(lambda e: e.tensor_tensor(out=kf[:], in0=kf[:], in1=bcast(st[:, 8:12].unsqueeze(2), [128, 4, 64]), op=ALU.mult),
                  r=["w_kf" + sfx, "w_krstd" + sfx], w=["w_kf" + sfx])
            P.pool(lambda e: e.tensor_tensor(out=kf[:], in0=kf[:], in1=bcast(gk[:].unsqueeze(1), [128, 4, 64]), op=ALU.mult),
                   r=["w_kf" + sfx, "w_gk"], w=["w_kf" + sfx])
            rope_inplace(P, P.pool, P.dve, kf[:], rp[b], t1[:, 0:4, :], 4, ["w_rp%d" % b], "w_kf" + sfx, "w_t1" + sfx)
            P.act(lambda e: e.activation(out=K_bf[:], in_=kf[:], func=AF.Copy), r=["w_kf" + sfx], w=["w_K_bf" + sfx])
            for h in range(4):
                P.pe(lambda e, h=h: e.transpose(pT1[0:64, h, :], K_bf[:, h, :], K.ident[:]), r=["w_K_bf" + sfx, "ident"], w=["w_pT"])
            P.dve(lambda e, rows=rows: e.tensor_copy(out=KT1[0:64, :, rows], in_=pT1[0:64, 0:4, :]), r=["w_pT"], w=["w_KT1_%d" % t])
            if t < 2:
                return
            P.dve(lambda e: e.tensor_tensor(out=qsq[:], in0=qf[:], in1=qf[:], op=ALU.mult), r=["w_qf" + sfx], w=["w_qsq" + sfx])
            P.dve(lambda e: e.reduce_sum(out=st[:, 16:32], in_=qsq[:], axis=AX.X), r=["w_qsq" + sfx], w=["w_qssq" + sfx])
            rstd_from_ssq(P, st[:, 16:32], st[:, 32:48], 16, 64, "w_q", sfx)
            P.dve(lambda e: e.tensor_tensor(out=qf[:], in0=qf[:], in1=bcast(st[:, 32:48].unsqueeze(2), [128, 16, 64]), op=ALU.mult),
                  r=["w_qf" + sfx, "w_qrstd" + sfx], w=["w_qf" + sfx])
            P.pool(lambda e: e.tensor_tensor(out=qf[:], in0=qf[:], in1=bcast(gq[:].unsqueeze(1), [128, 16, 64]), op=ALU.mult),
                   r=["w_qf" + sfx, "w_gq"], w=["w_qf" + sfx])
            rope_inplace(P, P.pool, P.dve, qf[:], rp[b], t1[:], 16, ["w_rp%d" % b], "w_qf" + sfx, "w_t1" + sfx)
            P.act(lambda e: e.activation(out=Q_bf[:], in_=qf[:], func=AF.Copy), r=["w_qf" + sfx], w=["w_Q_bf" + sfx])
            for hh in range(2):
                for h in range(8):
                    P.pe(lambda e, h=h, hh=hh: e.transpose(pT1[0:64, h, :], Q_bf[:, 8 * hh + h, :], K.ident[:]),
                         r=["w_Q_bf" + sfx, "ident"], w=["w_pT"])
                P.dve(lambda e, t=t, hh=hh: e.tensor_copy(out=QT[t % 3][0:64, 8 * hh:8 * hh + 8, :], in_=pT1[0:64, :, :]),
                      r=["w_pT"], w=["w_QT%d" % (t % 3)])

        def attn_tile(t):
            i = t - 2
            b = t % 2
            sfx = "_%d" % b
            o_bf, den, tmp = o_bf2[b], den2[b], tmp2[b]
            qt = QT[t % 3]; qk = "w_QT%d" % (t % 3)
            keyt = [(0, None), (1, None)]
            if i >= 1:
                keyt.append((t - 1, maskP))
            keyt.append((t, None))
            if i <= NTO - 4:
                keyt.append((t + 1, maskN))
            nk = len(keyt)
            pO = pBig[0:64, 2, :]; pDen = pBig[0:64, 3, :]
            for kh in range(4):
                pend = None
                for n_ in range(nk + 1):
                    cur = None
                    if n_ < nk:
                        kt, mk = keyt[n_]
                        sbi = cnt["ps"] % 2; cnt["ps"] += 1
                        pi = cnt["pt"] % 3; cnt["pt"] += 1
                        P.pe(lambda e, sbi=sbi, kt=kt, kh=kh: e.matmul(pS[sbi][:], lhsT=KT1[0:64, kh, kt * 128:(kt + 1) * 128],
                                                                        rhs=qt[0:64, 4 * kh:4 * kh + 4, :], start=True, stop=True),
                             r=["w_KT1_%d" % kt, qk], w=["w_pS%d" % sbi])
                        P.act(lambda e, sbi=sbi, pi=pi: e.activation(out=PT[pi][:], in_=pS[sbi][:], func=AF.Exp),
                              r=["w_pS%d" % sbi], w=["w_PT%d" % pi])
                        if mk is not None:
                            P.dve(lambda e, pi=pi, mk=mk: e.tensor_tensor(out=PT[pi][:].rearrange("p (g q) -> p g q", g=4),
                                                                          in0=PT[pi][:].rearrange("p (g q) -> p g q", g=4),
                                                                          in1=bcast(mk[:].unsqueeze(1), [128, 4, 128]), op=ALU.mult),
                                  r=["w_PT%d" % pi, "w_mask"], w=["w_PT%d" % pi])
                        cur = (n_, kt, pi)
                    if pend is not None:
                        n0, k0, p0 = pend
                        P.pe(lambda e, n0=n0, k0=k0, p0=p0, kh=kh: e.matmul(pO, lhsT=V1[:, k0, kh * 64:(kh + 1) * 64], rhs=PT[p0][:],
                                                                            start=(n0 == 0), stop=(n0 == nk - 1)),
                             r=["w_V1_%d" % k0, "w_PT%d" % p0], w=["w_pB"])
                        P.pe(lambda e, n0=n0, p0=p0: e.matmul(pDen, lhsT=K.ones_bf[:, 0:64], rhs=PT[p0][:], start=(n0 == 0), stop=(n0 == nk - 1)),
                             r=["ones_bf", "w_PT%d" % p0], w=["w_pC"])
                    pend = cur
                P.dve(lambda e, kh=kh: e.tensor_tensor(out=den[0:64, :].rearrange("p (g q) -> p g q", g=4),
                                                       in0=pDen.rearrange("p (g q) -> p g q", g=4),
                                                       in1=bcast(esk[0:64, 4 * kh:4 * kh + 4].unsqueeze(2), [64, 4, 128]), op=ALU.add),
                      r=["w_pC", "w_esk"], w=["w_den" + sfx])
                P.dve(lambda e: e.reciprocal(out=den[0:64, :], in_=den[0:64, :]), r=["w_den" + sfx], w=["w_den" + sfx])
                P.dve(lambda e, kh=kh: e.tensor_tensor(out=o_bf[0:64, 4 * kh:4 * kh + 4, :].rearrange("p g q -> p (g q)"), in0=pO,
                                                       in1=den[0:64, :], op=ALU.mult), r=["w_pB", "w_den" + sfx], w=["w_o_bf" + sfx])
            for half in range(2):
                for h in range(16):
                    P.pe(lambda e, half=half, h=h: e.matmul(pBig[:, half, :], lhsT=o_bf[0:64, h, :], rhs=Wo[0:64, h, half * 512:(half + 1) * 512],
                                                            start=(h == 0), stop=(h == 15)), r=["w_o_bf" + sfx, "w_Wo"], w=["w_pA"])
            hb = i % 2
            P.dve(lambda e: e.tensor_tensor(out=tmp[:], in0=pBig[:, 0:2, :].rearrange("p a c -> p (a c)"), in1=modv[:, 0, 2 * D:3 * D], op=ALU.mult),
                  r=["w_pA", "w_modv"], w=["w_" + sfx + "tmp"])
            P.pool(lambda e, t=t: e.tensor_tensor(out=tmp[:], in0=tmp[:], in1=xt[t % 3][:], op=ALU.add),
                   r=["w_" + sfx + "tmp", "w_xt%d" % (t % 3)], w=["w_" + sfx + "tmp"])
            P.dma(ACT, S["H3"][i * 128:(i + 1) * 128, :], tmp[:], r=["w_" + sfx + "tmp"], w=["H3sc"])

        order = []
        for t in range(NTO):
            order.append(("q", t))
            if t - 1 >= 2:
                order.append(("a", t - 1))
        order.append(("a", NTO - 1))
        if P4_STAGED:
            tiles = [P.capture((lambda t=t: qkv_tile(t)) if k == "q" else (lambda t=t: attn_tile(t))) for (k, t) in order]
            nst = sorted(len(x) for x in tiles)[len(tiles) // 2]
            K.p4_info = (nst, [len(x) for x in tiles[:8]])
            P.run_staged(tiles, skew=max(1, nst // P4_DIV))
        else:
            for (k, t) in order:
                (qkv_tile if k == "q" else attn_tile)(t)
        P.emit()
```

```python
import math
import numpy as np
import concourse.bass as bass
import concourse.mybir as mybir
from concourse.bass_utils import run_bass_kernel_spmd

F32 = mybir.dt.float32
BF = mybir.dt.bfloat16
I32 = mybir.dt.int32
AF = mybir.ActivationFunctionType
ALU = mybir.AluOpType
AX = mybir.AxisListType

D = 1024; KC = 8; SEQ = 8192; CTX = 256; NF = SEQ + CTX; NTF = NF // 128
OWN_LAT = 2304; NOWN = CTX + OWN_LAT; NTO = NOWN // 128
HID = 2816; HC = HID // 128
EPS = 1e-6
NCH = NF // 8
PE, ACT, DVE, POOL, SP = "pe", "act", "dve", "pool", "sp"
ENGS = (PE, ACT, DVE, POOL, SP)
NSLOT = {SP: 12, POOL: 12, ACT: 12}
SAME_SYNC = True
NB1 = 3
P1_SKEW = 9
P4_STAGED = True
P4_DIV = 2


class Op:
    __slots__ = ("idx", "eng", "fn", "deps", "need", "signal", "count", "is_dma", "slot", "val")


class Prog:
    def __init__(self, nc):
        self.nc = nc
        self.ops = []
        self.lastw = {}
        self.rd_eng = {}
        self.rd_dma = {}
        self.esem = {e: nc.alloc_semaphore("es_" + e) for e in ENGS}
        self.slots = {q: [nc.alloc_semaphore("ds_%s%d" % (q, i)) for i in range(n)] for q, n in NSLOT.items()}
        self.slot_n = {q: 0 for q in NSLOT}
        self.slot_last = {q: [None] * n for q, n in NSLOT.items()}
        self.cnt = {e: 0 for e in ENGS}
        self.seen_slot = {e: {} for e in ENGS}
        self.seen_idx = {e: {} for e in ENGS}
        self.emitted = 0
        self.epoch = 0
        self.last_op = {e: None for e in ENGS}

    capturing = None

    def capture(self, body):
        self.capturing = {"stages": [[]], "w": {}, "r": {}, "n": 0}
        body()
        st = [x for x in self.capturing["stages"] if x]
        self.capturing = None
        return st

    def _cap_add(self, eng, fn, r, w, dma):
        cap = self.capturing
        cap["n"] += 1
        eid = ("dma", eng, cap["n"]) if dma else eng
        conflict = False
        for k in list(r) + list(w):
            if k in cap["w"] and cap["w"][k] != eid:
                conflict = True
        for k in w:
            if k in cap["r"] and (cap["r"][k] - {eid}):
                conflict = True
        if conflict:
            cap["stages"].append([]); cap["w"] = {}; cap["r"] = {}
        cap["stages"][-1].append((eng, fn, tuple(r), tuple(w), dma))
        for k in w:
            cap["w"][k] = eid
        for k in r:
            cap["r"].setdefault(k, set()).add(eid)
        return None

    def run_staged(self, tiles, skew=1):
        info = []
        for st in tiles:
            lastw, lastt = {}, {}
            for si, stage in enumerate(st):
                for (eng, fn, r, w, dma) in stage:
                    for k in r:
                        lastt[k] = si
                    for k in w:
                        lastt[k] = si; lastw[k] = si
            info.append((lastw, lastt))
        active = []
        i = 0; step = 0
        while i < len(tiles) or active:
            if i < len(tiles) and step % skew == 0:
                active.append([i, 0]); i += 1
            progressed = False
            for ai, a in enumerate(active):
                t, si = a
                stage = tiles[t][si]
                ok = True
                for (eng, fn, r, w, dma) in stage:
                    for o in active[:ai]:
                        ow, ot = info[o[0]]
                        for k in w:
                            if ot.get(k, -1) >= o[1]:
                                ok = False
                        for k in r:
                            if ow.get(k, -1) >= o[1]:
                                ok = False
                    if not ok:
                        break
                if not ok:
                    continue
                for (eng, fn, r, w, dma) in stage:
                    self.add(eng, fn, r, w, dma)
                a[1] += 1
                progressed = True
            active = [a for a in active if a[1] < len(tiles[a[0]])]
            assert progressed or not active
            step += 1

    def add(self, eng, fn, r=(), w=(), dma=False):
        if self.capturing is not None:
            return self._cap_add(eng, fn, r, w, dma)
        op = Op()
        op.idx = len(self.ops); op.eng = eng; op.fn = fn; op.is_dma = dma
        op.signal = False; op.count = None; op.need = None; op.slot = None; op.val = None
        deps = {}
        for k in r:
            d = self.lastw.get(k)
            if d is not None:
                deps[d.idx] = d
        for k in w:
            d = self.lastw.get(k)
            if d is not None:
                deps[d.idx] = d
            for d in self.rd_eng.get(k, {}).values():
                deps[d.idx] = d
            for d in self.rd_dma.get(k, ()):
                deps[d.idx] = d
        if dma:
            q = eng
            n = self.slot_n[q]; self.slot_n[q] = n + 1
            si = n % NSLOT[q]
            op.slot = si; op.val = 16 * (n // NSLOT[q] + 1)
            prev = self.slot_last[q][si]
            if prev is not None:
                deps[prev.idx] = prev
            self.slot_last[q][si] = op
        op.deps = list(deps.values())
        for k in r:
            if dma:
                self.rd_dma.setdefault(k, []).append(op)
            else:
                self.rd_eng.setdefault(k, {})[eng] = op
        for k in w:
            self.lastw[k] = op
            self.rd_eng[k] = {}
            self.rd_dma[k] = []
        self.ops.append(op)
        if not dma:
            self.last_op[eng] = op
        return op

    def pe(self, fn, r=(), w=()): return self.add(PE, fn, r, w)
    def act(self, fn, r=(), w=()): return self.add(ACT, fn, r, w)
    def dve(self, fn, r=(), w=()): return self.add(DVE, fn, r, w)
    def pool(self, fn, r=(), w=()): return self.add(POOL, fn, r, w)

    def dma(self, q, out, in_, r=(), w=(), **kw):
        return self.add(q, lambda e: e.dma_start(out=out, in_=in_, **kw), r, w, dma=True)

    def barrier(self):
        deps = [o for o in self.last_op.values() if o is not None and o.idx >= self.epoch]
        for q in NSLOT:
            for o in self.slot_last[q]:
                if o is not None and o.idx >= self.epoch:
                    deps.append(o)
        saved = dict(self.last_op)
        for e in ENGS:
            op = self.add(e, None)
            op.deps = [d for d in deps]
        self.last_op = saved

    def emit(self):
        self.barrier()
        new = self.ops[self.emitted:]
        for op in new:
            need = []
            for d in sorted(op.deps, key=lambda o: o.idx):
                if d.idx < self.epoch:
                    continue
                if d.is_dma:
                    key = (d.eng, d.slot)
                    if self.seen_slot[op.eng].get(key, 0) >= d.val:
                        continue
                    self.seen_slot[op.eng][key] = d.val
                    need.append(d)
                else:
                    if d.eng == op.eng and not op.is_dma and op.fn is not None:
                        if op.eng == PE or not SAME_SYNC:
                            continue
                    if self.seen_idx[op.eng].get(d.eng, -1) >= d.idx:
                        continue
                    self.seen_idx[op.eng][d.eng] = d.idx
                    d.signal = True
                    need.append(d)
            op.need = need
        for op in new:
            if not op.is_dma and op.signal:
                self.cnt[op.eng] += 1
                op.count = self.cnt[op.eng]
        per = {e: [o for o in new if o.eng == e] for e in ENGS}
        esem, slots = self.esem, self.slots

        def run(e, lst):
            for op in lst:
                for d in op.need:
                    if d.is_dma:
                        e.wait_ge(slots[d.eng][d.slot], d.val)
                    else:
                        e.wait_ge(esem[d.eng], d.count)
                if op.fn is None:
                    continue
                ins = op.fn(e)
                if op.is_dma:
                    ins.then_inc(slots[op.eng][op.slot], 16)
                elif op.signal:
                    ins.then_inc(esem[op.eng], 1)

        with self.nc.Block() as blk:
            blk.tensor(lambda e: run(e, per[PE]))
            blk.scalar(lambda e: run(e, per[ACT]))
            blk.vector(lambda e: run(e, per[DVE]))
            blk.gpsimd(lambda e: run(e, per[POOL]))
            blk.sync(lambda e: run(e, per[SP]))
        self.emitted = len(self.ops)
        self.epoch = len(self.ops)


def bcast(ap, shape):
    return ap.to_broadcast(list(shape))


class Ctx:
    pass


def build(stage=99, debug=False):
    nc = bass.Bass("TRN2", target_bir_lowering=False)
    P = Prog(nc)
    K = Ctx()

    def din(name, shape, dt=F32):
        return nc.dram_tensor(name, list(shape), dt, kind="ExternalInput").ap()

    def dscr(name, shape, dt):
        return nc.dram_tensor(name, list(shape), dt, kind=("ExternalOutput" if debug else "Internal")).ap()

    I = {}
    I["xf"] = din("xf", [NF, D]); I["xo"] = din("xo", [NOWN, D]); I["oidx"] = din("oidx", [NOWN, 1], I32)
    I["cs"] = din("cs", [128, KC, 2])
    I["ropeA_f"] = din("ropeA_f", [NF, 2, 64]); I["ropeA_o"] = din("ropeA_o", [NOWN, 2, 64])
    I["ropeC_o"] = din("ropeC_o", [NOWN, 2, 64])
    I["ada_w"] = din("ada_w", [2, D, 6 * D]); I["ada_b"] = din("ada_b", [2, 6 * D])
    I["norm_mix"] = din("norm_mix", [2, D]); I["norm_ffn"] = din("norm_ffn", [2, D])
    I["ffn_w_gate"] = din("ffn_w_gate", [2, D, HID]); I["ffn_w_up"] = din("ffn_w_up", [2, D, HID])
    I["ffn_w_down"] = din("ffn_w_down", [2, HID, D])
    I["a_w_in"] = din("a_w_in", [D, 1216]); I["a_w_out"] = din("a_w_out", [D, D])
    I["s5t"] = din("s5t", [128, 32, 3]); I["s5b"] = din("s5b", [128, 32, 2, 16]); I["s5c"] = din("s5c", [128, 32, 2, 16])
    I["s5dd"] = din("s5dd", [128, 32]); I["s5_w_glu"] = din("s5_w_glu", [512, 512]); I["s5bg"] = din("s5bg", [128, 4])
    I["mla_qa_norm"] = din("mla_qa_norm", [1, 384]); I["mla_w_q_b"] = din("mla_w_q_b", [384, 768])
    I["mla_kva_norm"] = din("mla_kva_norm", [1, 256]); I["mla_w_kv_b"] = din("mla_w_kv_b", [256, 1024])
    I["mla_q_norm"] = din("mla_q_norm", [1, 192]); I["mla_k_norm"] = din("mla_k_norm", [1, 192])
    I["c_w_in"] = din("c_w_in", [D, 1536]); I["c_w_out"] = din("c_w_out", [D, D])
    I["c_q_norm"] = din("c_q_norm", [1, 64]); I["c_k_norm"] = din("c_k_norm", [1, 64]); I["c_sink"] = din("c_sink", [1, 16])
    out = nc.dram_tensor("out", [OWN_LAT, D], F32, kind="ExternalOutput").ap()

    S = {}
    S["mod"] = dscr("modsc", [2, 128, 2, 6 * D], F32)
    S["KTa"] = dscr("KTa", [128, 4, NF], BF); S["KTb"] = dscr("KTb", [64, 4, NF], BF)
    S["V"] = dscr("Vsc", [NF, 512], BF); S["U"] = dscr("Usc", [32, 128, NCH], BF)
    S["YT"] = dscr("YTsc", [512, 8, NCH], BF); S["S5"] = dscr("S5sc", [NF, 512], BF)
    S["H1"] = dscr("H1sc", [NOWN, D], F32); S["H2"] = dscr("H2sc", [NOWN, D], F32)
    S["H3"] = dscr("H3sc", [OWN_LAT, D], F32)

    ident = nc.alloc_sbuf_tensor("ident", [128, 128], BF)
    identf = nc.alloc_sbuf_tensor("identf", [128, 128], F32)
    ones_bf = nc.alloc_sbuf_tensor("ones_bf", [128, 128], BF)
    P.pool(lambda e: e.memset(identf[:], 0.0), w=["identf"])
    P.pool(lambda e: e.affine_select(out=identf[:], in_=identf[:], pattern=[[-1, 128]], compare_op=ALU.not_equal,
                                     fill=1.0, base=0, channel_multiplier=1), r=["identf"], w=["identf"])
    P.pool(lambda e: e.tensor_copy(out=ident[:], in_=identf[:]), r=["identf"], w=["ident"])
    P.pool(lambda e: e.memset(ones_bf[:], 1.0), w=["ones_bf"])
    K.ident, K.identf, K.ones_bf = ident, identf, ones_bf

    phase0_mod(nc, P, I, S)
    if stage >= 1:
        phase1_full(nc, P, I, S, K)
    if stage >= 2:
        phase2_s5(nc, P, I, S, K)
    if stage >= 3:
        phase3_mla(nc, P, I, S, K)
        phase_ffn(nc, P, I, S, K, 0, S["H1"], S["H2"], 2, NTO)
    if stage >= 4:
        phase4_win(nc, P, I, S, K)
        phase_ffn(nc, P, I, S, K, 1, S["H3"], out, 0, OWN_LAT // 128)
    else:
        pass
    return nc


def phase0_mod(nc, P, I, S):
    from contextlib import ExitStack
    NWB = 8
    with ExitStack() as es:
        def sb(name, shape, dt):
            return es.enter_context(nc.sbuf_tensor(name, list(shape), dt))
        cst = sb("cst", [128, KC, 2], F32); sl = sb("sl", [128, KC, 2], F32); SL = sb("SL", [128, 2, KC, 128], BF)
        wb = [sb("wb%d" % i, [128, KC, 512], BF) for i in range(NWB)]
        bb = [sb("bb%d" % i, [128, 512], F32) for i in range(NWB)]
        gmix = sb("gmix", [128, D], F32); gffn = sb("gffn", [128, D], F32)
        modt = sb("modt", [128, 2, 6 * D], F32)
        pm0 = es.enter_context(nc.psum_tensor("pm0", [128, 512], F32)); pm1 = es.enter_context(nc.psum_tensor("pm1", [128, 512], F32))
        pm = [pm0, pm1]
        P.dma(SP, cst[:], I["cs"][:, :, :], w=["cst"])
        P.act(lambda e: e.activation(out=sl[:], in_=cst[:], func=AF.Silu), r=["cst"], w=["sl"])
        for t in range(2):
            for kc in range(KC):
                P.dve(lambda e, t=t, kc=kc: e.tensor_copy(out=SL[:, t, kc, :], in_=bcast(sl[:, kc, t:t + 1], [128, 128])),
                      r=["sl"], w=["SL"])
        n = 0
        for layer in range(2):
            P.dma(SP, gmix[:], bcast(I["norm_mix"][layer:layer + 1, :], [128, D]), w=["gmix"])
            P.dma(SP, gffn[:], bcast(I["norm_ffn"][layer:layer + 1, :], [128, D]), w=["gffn"])
            wv = I["ada_w"][layer].rearrange("(kc p) n -> p kc n", p=128)
            for j in range(12):
                b = n % NWB; n += 1
                P.dma(POOL, wb[b][:], wv[:, :, j * 512:(j + 1) * 512], w=["wb%d" % b])
                P.dma(SP, bb[b][:], bcast(I["ada_b"][layer:layer + 1, j * 512:(j + 1) * 512], [128, 512]), w=["bb%d" % b])
                for t in range(2):
                    for kc in range(KC):
                        P.pe(lambda e, t=t, kc=kc, b=b: e.matmul(pm[t][:], lhsT=SL[:, t, kc, :], rhs=wb[b][:, kc, :],
                                                                  start=(kc == 0), stop=(kc == KC - 1)),
                             r=["SL", "wb%d" % b], w=["pm%d" % t])
                    P.dve(lambda e, t=t, b=b, j=j: e.tensor_tensor(out=modt[:, t, j * 512:(j + 1) * 512], in0=pm[t][:],
                                                                    in1=bb[b][:], op=ALU.add),
                          r=["pm%d" % t, "bb%d" % b], w=["modt"])
            for t in range(2):
                P.dve(lambda e, t=t: e.scalar_tensor_tensor(out=modt[:, t, D:2 * D], in0=modt[:, t, D:2 * D], scalar=1.0,
                                                             in1=gmix[:], op0=ALU.add, op1=ALU.mult),
                      r=["modt", "gmix"], w=["modt"])
                P.dve(lambda e, t=t: e.scalar_tensor_tensor(out=modt[:, t, 4 * D:5 * D], in0=modt[:, t, 4 * D:5 * D], scalar=1.0,
                                                             in1=gffn[:], op0=ALU.add, op1=ALU.mult),
                      r=["modt", "gffn"], w=["modt"])
            P.dma(ACT, S["mod"][layer], modt[:], r=["modt"], w=["modsc%d" % layer])
        P.emit()


def rstd_from_ssq(P, ssq, rstd, n, width, tag, sfx=""):
    P.dve(lambda e: e.tensor_scalar(out=rstd, in0=ssq, scalar1=1.0 / width, scalar2=EPS, op0=ALU.mult, op1=ALU.add),
          r=[tag + "ssq" + sfx], w=[tag + "rstd" + sfx])
    P.act(lambda e: e.activation(out=rstd, in_=rstd, func=AF.Sqrt), r=[tag + "rstd" + sfx], w=[tag + "rstd" + sfx])
    P.dve(lambda e: e.reciprocal(out=rstd, in_=rstd), r=[tag + "rstd" + sfx], w=[tag + "rstd" + sfx])


def norm_mod(P, x_ap, xkey, A_ap, B_ap, modkey, junk, ssq, rstd, tmp, a_bf, tag):
    P.act(lambda e: e.activation(out=junk, in_=x_ap, func=AF.Square), r=[xkey], w=[tag + "tmp"])
    P.dve(lambda e: e.reduce_sum(out=ssq, in_=junk, axis=AX.X), r=[tag + "tmp"], w=[tag + "ssq"])
    rstd_from_ssq(P, ssq, rstd, 1, D, tag)
    P.dve(lambda e: e.scalar_tensor_tensor(out=tmp, in0=x_ap, scalar=rstd, in1=A_ap, op0=ALU.mult, op1=ALU.mult),
          r=[xkey, tag + "rstd", modkey], w=[tag + "tmp"])
    P.pool(lambda e: e.tensor_tensor(out=a_bf, in0=tmp, in1=B_ap, op=ALU.add), r=[tag + "tmp", modkey], w=[tag + "a_bf"])


def transpose8(P, K, a_bf, pT, aT_out, tag, aTkey, pTkey=None):
    pTkey = pTkey or (tag + "pT")
    for kc in range(KC):
        P.pe(lambda e, kc=kc: e.transpose(pT[:, kc, :], a_bf[:, kc * 128:(kc + 1) * 128], K.ident[:]),
             r=[tag + "a_bf", "ident"], w=[pTkey])
    P.act(lambda e: e.activation(out=aT_out, in_=pT[:, :, :], func=AF.Copy), r=[pTkey], w=[aTkey])


def norm_mod_T(P, K, x_ap, xkey, A_ap, B_ap, modkey, junk, ssq, rstd, tmp, a_bf, pT, aT_out, tag, aTkey, pTkey=None):
    norm_mod(P, x_ap, xkey, A_ap, B_ap, modkey, junk, ssq, rstd, tmp, a_bf, tag)
    transpose8(P, K, a_bf, pT, aT_out, tag, aTkey, pTkey)


def run_pipeline(gens):
    active = []
    it = iter(gens)
    more = True
    while more or active:
        nxt = next(it, None) if more else None
        if nxt is None:
            more = False
        for g in list(active):
            try:
                next(g)
            except StopIteration:
                active.remove(g)
        if nxt is not None:
            try:
                next(nxt)
                active.append(nxt)
            except StopIteration:
                pass


def rope_inplace(P, eng_a, eng_b, x4, rp, t1, nh, rkeys, xkey, tkey):
    cosb = bcast(rp[:, 0:1, :], [128, nh, 64])
    for a in range(2):
        sl_ = slice(a * 32, (a + 1) * 32)
        sw = x4[:, :, sl_].rearrange("p h (w q) -> p h w q", w=2)[:, :, ::-1, :]
        t1v = t1[:, :, sl_].rearrange("p h (w q) -> p h w q", w=2)
        sinb = bcast(rp[:, 1:2, sl_], [128, nh, 32]).rearrange("p h (w q) -> p h w q", w=2)
        eng_a(lambda e, sw=sw, t1v=t1v, sinb=sinb: e.tensor_tensor(out=t1v, in0=sw, in1=sinb, op=ALU.mult),
              r=[xkey] + rkeys, w=[tkey])
    eng_b(lambda e: e.tensor_tensor(out=x4, in0=x4, in1=cosb, op=ALU.mult), r=[xkey, tkey] + rkeys, w=[xkey])
    eng_b(lambda e: e.tensor_tensor(out=x4, in0=x4, in1=t1, op=ALU.add), r=[xkey, tkey], w=[xkey])


def phase1_full(nc, P, I, S, K):
    from contextlib import ExitStack
    with ExitStack() as es:
        def sb(name, shape, dt):
            return es.enter_context(nc.sbuf_tensor(name, list(shape), dt))

        def ps(name, shape, dt=F32):
            return es.enter_context(nc.psum_tensor(name, list(shape), dt))
        Win = sb("Win", [128, KC, 832], BF); Wkvb = sb("Wkvb", [128, 2, 1024], BF)
        modv = sb("modv", [128, 2, 2 * D], F32)
        gkva = sb("gkva", [128, 256], F32); gk = sb("gk", [128, 192], F32)
        xt = [sb("xt%d" % i, [128, D], F32) for i in range(NB1)]
        rp = [sb("rp%d" % i, [128, 2, 64], F32) for i in range(NB1)]
        junk2 = [sb("junk%d" % i, [128, D], F32) for i in range(NB1)]; tmp2 = junk2
        a_bf2 = [sb("a_bf%d" % i, [128, D], BF) for i in range(NB1)]
        aT4 = [sb("aT4_%d" % i, [128, KC, 512], BF) for i in range(2)]
        UT = sb("UT", [128, 4, 8, NCH], BF)
        st2 = [sb("st%d" % i, [128, 16], F32) for i in range(NB1)]
        c_bf2 = [sb("c_bf%d" % i, [128, 256], BF) for i in range(NB1)]; cT2 = [sb("cT%d" % i, [128, 2, 128], BF) for i in range(NB1)]
        kf2 = [sb("kf%d" % i, [128, 4, 192], F32) for i in range(NB1)]; ksq2 = [sb("ksq%d" % i, [128, 4, 192], F32) for i in range(NB1)]
        t12 = [sb("t1_%d" % i, [128, 4, 64], F32) for i in range(NB1)]
        K_bf2 = [sb("K_bf%d" % i, [128, 4, 192], BF) for i in range(NB1)]; KTs = [sb("KTs%d" % i, [128, 8, 128], BF) for i in range(NB1)]
        V_bf = [sb("V_bf%d" % i, [128, 4, 128], BF) for i in range(NB1)]
        pTa = ps("pTa", [128, 8, 128], BF); pTk = ps("pTk", [128, 8, 128], BF); pcT = ps("pcT", [128, 8, 128], BF)
        pKV = ps("pKV", [128, 512]); pU = [ps("pU%d" % i, [128, 512]) for i in range(2)]
        pKVB = ps("pKVB", [128, 1024])

        wv = I["a_w_in"].rearrange("(kc p) n -> p kc n", p=128)
        P.dma(POOL, Win[:, :, 0:512], wv[:, :, 0:512], w=["Win"])
        P.dma(POOL, Win[:, :, 512:832], wv[:, :, 896:1216], w=["Win"])
        P.dma(POOL, Wkvb[:], I["mla_w_kv_b"].rearrange("(kc p) n -> p kc n", p=128), w=["Wkvb"])
        P.dma(SP, modv[:], S["mod"][0][:, :, 0:2 * D], r=["modsc0"], w=["modv"])
        P.dma(SP, gkva[:], bcast(I["mla_kva_norm"][0:1, :], [128, 256]), w=["gkva"])
        P.dma(SP, gk[:], bcast(I["mla_k_norm"][0:1, :], [128, 192]), w=["gk"])
        ropev = I["ropeA_f"]
        def p1_tile(tg, ti, b, aT, aTkey, typ, nb, cbase):
            rows = slice(tg * 128, (tg + 1) * 128)
            rb = b
            junk, tmp, a_bf, st, c_bf, cT, kf, ksq, t1, K_bf = (junk2[b], tmp2[b], a_bf2[b], st2[b], c_bf2[b], cT2[b], kf2[b],
                                                                  ksq2[b], t12[b], K_bf2[b])
            sfx = "_%d" % b
            P.dma(SP, xt[b][:], I["xf"][rows, :], w=["xt%d" % b])
            P.dma(SP, rp[rb][:], ropev[rows, :, :], w=["rp%d" % rb])
            norm_mod(P, xt[b][:], "xt%d" % b, modv[:, typ, D:2 * D], modv[:, typ, 0:D], "modv",
                     junk[:], st[:, 0:1], st[:, 1:2], tmp[:], a_bf[:], "p1" + sfx)
            transpose8(P, K, a_bf[:], pTa, aT[:, :, ti * 128:(ti + 1) * 128], "p1" + sfx, aTkey, pTkey="pTa")
            for kc in range(KC):
                P.pe(lambda e, kc=kc, aT=aT, ti=ti: e.matmul(pKV[:, 0:320], lhsT=aT[:, kc, ti * 128:(ti + 1) * 128],
                                                            rhs=Win[:, kc, 512:832], start=(kc == 0), stop=(kc == KC - 1)),
                     r=[aTkey, "Win"], w=["pKV"])
            P.act(lambda e: e.activation(out=junk[:, 0:256], in_=pKV[:, 0:256], func=AF.Square), r=["pKV"], w=["p1" + sfx + "tmp"])
            P.dve(lambda e: e.reduce_sum(out=st[:, 2:3], in_=junk[:, 0:256], axis=AX.X), r=["p1" + sfx + "tmp"], w=["cssq" + sfx])
            rstd_from_ssq(P, st[:, 2:3], st[:, 3:4], 1, 256, "c", sfx)
            P.dve(lambda e: e.scalar_tensor_tensor(out=c_bf[:], in0=pKV[:, 0:256], scalar=st[:, 3:4], in1=gkva[:],
                                                   op0=ALU.mult, op1=ALU.mult), r=["pKV", "crstd" + sfx, "gkva"], w=["c_bf" + sfx])
            P.act(lambda e: e.activation(out=kf[:, :, 128:192], in_=bcast(pKV[:, 256:320].unsqueeze(1), [128, 4, 64]),
                                         func=AF.Copy), r=["pKV"], w=["kf" + sfx])
            for k2 in range(2):
                P.pe(lambda e, k2=k2: e.transpose(pcT[:, k2, :], c_bf[:, k2 * 128:(k2 + 1) * 128], K.ident[:]),
                     r=["c_bf" + sfx, "ident"], w=["pcT"])
            P.dve(lambda e: e.tensor_copy(out=cT[:], in_=pcT[:, 0:2, :]), r=["pcT"], w=["cT" + sfx])
            for half in range(2):
                for k2 in range(2):
                    P.pe(lambda e, half=half, k2=k2: e.matmul(pKVB[:, half * 512:(half + 1) * 512], lhsT=cT[:, k2, :],
                                                              rhs=Wkvb[:, k2, half * 512:(half + 1) * 512],
                                                              start=(k2 == 0), stop=(k2 == 1)),
                         r=["cT" + sfx, "Wkvb"], w=["pKVB"])
            kvv = pKVB[:].rearrange("p (h c) -> p h c", h=4)
            P.act(lambda e, kvv=kvv: e.activation(out=kf[:, :, 0:128], in_=kvv[:, :, 0:128], func=AF.Copy), r=["pKVB"], w=["kf" + sfx])
            P.act(lambda e, kvv=kvv, b=b: e.activation(out=V_bf[b][:], in_=kvv[:, :, 128:256], func=AF.Copy),
                  r=["pKVB"], w=["V_bf%d" % b])
            P.dma(ACT, S["V"][rows, :], V_bf[b][:].rearrange("p h d -> p (h d)"), r=["V_bf%d" % b], w=["Vsc"])
            P.dve(lambda e: e.tensor_tensor(out=ksq[:], in0=kf[:], in1=kf[:], op=ALU.mult), r=["kf" + sfx], w=["ksq" + sfx])
            P.dve(lambda e: e.reduce_sum(out=st[:, 4:8], in_=ksq[:], axis=AX.X), r=["ksq" + sfx], w=["kssq" + sfx])
            rstd_from_ssq(P, st[:, 4:8], st[:, 8:12], 4, 192, "k", sfx)
            P.dve(lambda e: e.tensor_tensor(out=kf[:], in0=kf[:], in1=bcast(st[:, 8:12].unsqueeze(2), [128, 4, 192]),
                                            op=ALU.mult), r=["kf" + sfx, "krstd" + sfx], w=["kf" + sfx])
            P.pool(lambda e: e.tensor_tensor(out=kf[:], in0=kf[:], in1=bcast(gk[:].unsqueeze(1), [128, 4, 192]),
                                             op=ALU.mult), r=["kf" + sfx, "gk"], w=["kf" + sfx])
            rope_inplace(P, P.pool, P.dve, kf[:, :, 128:192], rp[rb], t1[:], 4, ["rp%d" % rb], "kf" + sfx, "t1" + sfx)
            P.act(lambda e: e.activation(out=K_bf[:], in_=kf[:], func=AF.Copy), r=["kf" + sfx], w=["K_bf" + sfx])
            for h in range(4):
                P.pe(lambda e, h=h: e.transpose(pTk[:, 2 * h, :], K_bf[:, h, 0:128], K.ident[:]), r=["K_bf" + sfx, "ident"], w=["pTk"])
                P.pe(lambda e, h=h: e.transpose(pTk[0:64, 2 * h + 1, :], K_bf[:, h, 128:192], K.ident[:]),
                     r=["K_bf" + sfx, "ident"], w=["pTk"])
            pv = pTk[:].rearrange("p (h two) t -> p h two t", two=2)
            kv_ = KTs[b][:].rearrange("p (h two) t -> p h two t", two=2)
            P.dve(lambda e, pv=pv, kv_=kv_: e.tensor_copy(out=kv_[:, :, 0, :], in_=pv[:, :, 0, :]), r=["pTk"], w=["KTs%d" % b])
            P.dve(lambda e, pv=pv, kv_=kv_: e.tensor_copy(out=kv_[0:64, :, 1, :], in_=pv[0:64, :, 1, :]), r=["pTk"], w=["KTs%d" % b])
            P.dma(ACT, S["KTa"][:, :, rows], kv_[:, :, 0, :], r=["KTs%d" % b], w=["KTa"])
            P.dma(ACT, S["KTb"][:, :, rows], kv_[0:64, :, 1, :], r=["KTs%d" % b], w=["KTb"])
            if ti == nb - 1:
                p1_u(aT, aTkey, nb, cbase)

        def p1_u(aT, aTkey, nb, cbase):
            ncol = nb * 16
            for fc in range(4):
                pu = pU[fc % 2]
                for kc in range(KC):
                    P.pe(lambda e, fc=fc, kc=kc, pu=pu, aT=aT, nb=nb: e.matmul(
                        pu[:, 0:nb * 128], lhsT=Win[:, kc, fc * 128:(fc + 1) * 128],
                        rhs=aT[:, kc, 0:nb * 128].rearrange("p (c j) -> p j c", j=8), start=(kc == 0), stop=(kc == KC - 1)),
                        r=[aTkey, "Win"], w=["pU%d" % (fc % 2)])
                P.act(lambda e, fc=fc, pu=pu, nb=nb, cbase=cbase, ncol=ncol: e.activation(
                    out=UT[:, fc, :, cbase:cbase + ncol], in_=pu[:, 0:nb * 128].rearrange("p (j c) -> p j c", j=8), func=AF.Copy),
                    r=["pU%d" % (fc % 2)], w=["UT"])

        batches = [(0, 2, 0, 1)] + [(2 + 4 * k, 4, 32 + 64 * k, 0) for k in range(16)]
        tiles = []
        nt = 0
        for bi, (t0, nb, cbase, typ) in enumerate(batches):
            aT = aT4[bi % 2]; aTkey = "aT4_%d" % (bi % 2)
            for ti in range(nb):
                tiles.append(P.capture(lambda t0=t0, ti=ti, nt=nt, aT=aT, aTkey=aTkey, typ=typ, nb=nb, cbase=cbase:
                                       p1_tile(t0 + ti, ti, nt % NB1, aT, aTkey, typ, nb, cbase))); nt += 1
        P.run_staged(tiles, skew=P1_SKEW)
        n = 0
        for fc in range(4):
            for gi in range(8):
                g = fc * 8 + gi
                q = SP if n % 2 == 0 else ACT; n += 1
                P.dma(q, S["U"][g].rearrange("(j s) c -> s j c", s=16), UT[gi * 16:(gi + 1) * 16, fc, :, :], r=["UT"], w=["Usc"])
        P.emit()


def _rope_tables(n_tok_lat):
    n = np.arange(n_tok_lat)
    row = (n // 64).astype(np.float32); col = (n % 64).astype(np.float32)
    inv = (10000.0 ** (-np.arange(16, dtype=np.float32) / 16)).astype(np.float32)
    ar = row[:, None] * inv; ac = col[:, None] * inv
    ang = np.concatenate([ar, ar, ac, ac], axis=-1).astype(np.float32)
    cos = np.cos(ang).astype(np.float32); sin = np.sin(ang).astype(np.float32)
    sgn = np.concatenate([-np.ones(16), np.ones(16), -np.ones(16), np.ones(16)]).astype(np.float32)
    return np.stack([cos, sin * sgn], axis=1)


def own_start(j):
    return min(max(2048 * j - 128, 0), SEQ - OWN_LAT)


def prep_core(inp, core, shared):
    b, j = core // 4, core % 4
    s = own_start(j)
    f = lambda a: np.ascontiguousarray(a, dtype=np.float32)
    m = dict(shared)
    m["xf"] = f(np.concatenate([inp["ctx"][b], inp["x"][b]], 0))
    m["xo"] = f(np.concatenate([inp["ctx"][b], inp["x"][b, s:s + OWN_LAT]], 0))
    m["oidx"] = np.concatenate([np.arange(CTX), CTX + s + np.arange(OWN_LAT)]).astype(np.int32).reshape(-1, 1)
    cs = np.stack([inp["c"][b].reshape(KC, 128).T, inp["c_ctx"].reshape(KC, 128).T], axis=-1)
    m["cs"] = f(cs)
    rl = shared["_rope_lat"]
    idr = np.zeros((CTX, 2, 64), np.float32); idr[:, 0, :] = 1.0
    m["ropeA_f"] = f(np.concatenate([idr, rl], 0))
    m["ropeA_o"] = f(np.concatenate([idr, rl[s:s + OWN_LAT]], 0))
    m["ropeC_o"] = m["ropeA_o"]
    del m["_rope_lat"]
    return m


def prep_shared(inp):
    f = lambda a: np.ascontiguousarray(a, dtype=np.float32)
    m = {}
    for k in ("ada_w", "ada_b", "norm_mix", "norm_ffn", "ffn_w_gate", "ffn_w_up", "ffn_w_down"):
        m[k] = f(inp[k])
    for k in ("a_w_in", "a_w_out", "s5_w_glu", "mla_w_q_b", "mla_w_kv_b", "c_w_in", "c_w_out"):
        m[k] = f(inp[k][0])
    for k in ("mla_qa_norm", "mla_kva_norm", "mla_q_norm", "mla_k_norm", "c_q_norm", "c_k_norm", "c_sink"):
        m[k] = f(inp[k][0].reshape(1, -1))

    def unit(a):
        a = np.asarray(a)
        d_, g_, p_ = a.shape[:3]
        a = a.reshape((d_, g_ // 2, 2, p_) + a.shape[3:])
        a = np.moveaxis(a, (2, 3), (0, 1))
        return a.reshape((2 * p_, d_ * (g_ // 2)) + a.shape[4:])
    lre, lim, ls = inp["s5_lam_re"][0], inp["s5_lam_im"][0], inp["s5_log_step"][0]
    lsb = np.broadcast_to(ls[:, :, None], lre.shape)
    m["s5t"] = f(np.stack([unit(lre), unit(lim), unit(lsb)], -1))
    m["s5b"] = f(np.stack([unit(inp["s5_b_re"][0]), unit(inp["s5_b_im"][0])], 2))
    cre = np.swapaxes(inp["s5_c_re"][0], 2, 3); cim = np.swapaxes(inp["s5_c_im"][0], 2, 3)
    m["s5c"] = f(np.stack([unit(cre), unit(cim)], 2))
    dd = np.asarray(inp["s5_d"][0]).reshape(32, 16)
    m["s5dd"] = f(np.broadcast_to(dd.T[None, :, :], (8, 16, 32)).reshape(128, 32))
    m["s5bg"] = f(np.asarray(inp["s5_b_glu"][0]).reshape(4, 128).T)
    m["_rope_lat"] = _rope_tables(SEQ)
    return m


_NC_CACHE = {}


def kernel(**inputs):
    inp = {k: np.asarray(v) for k, v in inputs.items()}
    if "nc" not in _NC_CACHE:
        _NC_CACHE["nc"] = build()
    nc = _NC_CACHE["nc"]
    shared = prep_shared(inp)
    in_maps = [prep_core(inp, c, shared) for c in range(8)]
    res = run_bass_kernel_spmd(nc, in_maps, core_ids=list(range(8)))
    outp = np.empty((2, SEQ, D), np.float32)
    for c in range(8):
        b, j = c // 4, c % 4
        off = 2048 * j - own_start(j)
        outp[b, 2048 * j:2048 * (j + 1)] = res.results[c]["out"][off:off + 2048]
    return outp


def cmul(eng, o_r, o_i, a_r, a_i, b_r, b_i, t1, t2, keys):
    eng(lambda e: e.tensor_tensor(out=t1, in0=a_r, in1=b_r, op=ALU.mult), r=keys, w=keys)
    eng(lambda e: e.tensor_tensor(out=t2, in0=a_i, in1=b_i, op=ALU.mult), r=keys, w=keys)
    eng(lambda e: e.tensor_tensor(out=o_r, in0=t1, in1=t2, op=ALU.subtract), r=keys, w=keys)
    eng(lambda e: e.tensor_tensor(out=t1, in0=a_r, in1=b_i, op=ALU.mult), r=keys, w=keys)
    eng(lambda e: e.tensor_tensor(out=t2, in0=a_i, in1=b_r, op=ALU.mult), r=keys, w=keys)
    eng(lambda e: e.tensor_tensor(out=o_i, in0=t1, in1=t2, op=ALU.add), r=keys, w=keys)


def s5_scan(eng, vr, vi, n, L, MU, NMI, us, seed, tot, tmp, key, tkey, fused):
    s1, s2, b1, b2, b3, b4 = tmp
    kk = [key, tkey]
    u0 = us.start

    def op2(out, in0, in1, op):
        eng(lambda e: e.tensor_tensor(out=out, in0=in0, in1=in1, op=op), r=kk, w=kk)

    def cp(out, in_):
        eng(lambda e: e.tensor_copy(out=out, in_=in_), r=kk, w=kk)

    def fma(out, in0, sc, in1):
        eng(lambda e: e.scalar_tensor_tensor(out=out, in0=in0, scalar=sc, in1=in1, op0=ALU.mult, op1=ALU.add), r=kk, w=kk)

    def groups(cnt):
        if cnt >= 64:
            return [(slice(0, 2), 2), (slice(2, 4), 2)]
        return [(slice(0, 4), 4)]

    def mub(c, l, usl, cnt):
        uu = slice(u0 + usl.start, u0 + usl.stop)
        return bcast(MU[:, c, l, uu].unsqueeze(2), [128, usl.stop - usl.start, cnt])
    for l in range(L):
        st = 1 << l; cnt = n >> (l + 1)
        ar, ai = vr[:, :, st - 1::2 * st], vi[:, :, st - 1::2 * st]
        br, bi = vr[:, :, 2 * st - 1::2 * st], vi[:, :, 2 * st - 1::2 * st]
        for (usl, nu) in groups(cnt):
            if fused and cnt >= 64:
                for q in range(usl.start, usl.stop):
                    mr = MU[:, 0, l, u0 + q:u0 + q + 1]; mi = MU[:, 1, l, u0 + q:u0 + q + 1]; nmi = NMI[:, l, u0 + q:u0 + q + 1]
                    fma(br[:, q, :], ar[:, q, :], mr, br[:, q, :])
                    fma(br[:, q, :], ai[:, q, :], nmi, br[:, q, :])
                    fma(bi[:, q, :], ar[:, q, :], mi, bi[:, q, :])
                    fma(bi[:, q, :], ai[:, q, :], mr, bi[:, q, :])
                continue
            p1 = (b1 if nu == 2 else s1)[:, :, 0:cnt]; p2 = (b2 if nu == 2 else s2)[:, :, 0:cnt]
            mr, mi = mub(0, l, usl, cnt), mub(1, l, usl, cnt)
            op2(p1, ar[:, usl, :], mr, ALU.mult); op2(p2, ai[:, usl, :], mi, ALU.mult)
            op2(br[:, usl, :], br[:, usl, :], p1, ALU.add); op2(br[:, usl, :], br[:, usl, :], p2, ALU.subtract)
            op2(p1, ar[:, usl, :], mi, ALU.mult); op2(p2, ai[:, usl, :], mr, ALU.mult)
            op2(bi[:, usl, :], bi[:, usl, :], p1, ALU.add); op2(bi[:, usl, :], bi[:, usl, :], p2, ALU.add)
    lr, li = vr[:, :, n - 1:n], vi[:, :, n - 1:n]
    if tot is not None:
        cp(tot[0], lr); cp(tot[1], li)
    if seed is None:
        eng(lambda e: e.memset(lr, 0.0), r=kk, w=kk)
        eng(lambda e: e.memset(li, 0.0), r=kk, w=kk)
    else:
        cp(lr, seed[0]); cp(li, seed[1])
    for l in reversed(range(L)):
        st = 1 << l; cnt = n >> (l + 1)
        ar, ai = vr[:, :, st - 1::2 * st], vi[:, :, st - 1::2 * st]
        br, bi = vr[:, :, 2 * st - 1::2 * st], vi[:, :, 2 * st - 1::2 * st]
        for (usl, nu) in groups(cnt):
            if nu == 2:
                tr, ti = b3[:, :, 0:cnt], b4[:, :, 0:cnt]
            else:
                tr, ti = s1[:, :, 32:32 + cnt], s2[:, :, 32:32 + cnt]
            cp(tr, br[:, usl, :]); cp(ti, bi[:, usl, :])
            if fused and cnt >= 64:
                for q2, q in enumerate(range(usl.start, usl.stop)):
                    mr = MU[:, 0, l, u0 + q:u0 + q + 1]; mi = MU[:, 1, l, u0 + q:u0 + q + 1]; nmi = NMI[:, l, u0 + q:u0 + q + 1]
                    fma(br[:, q, :], br[:, q, :], mr, ar[:, q, :])
                    fma(br[:, q, :], ti[:, q2, :], nmi, br[:, q, :])
                    fma(bi[:, q, :], bi[:, q, :], mr, ai[:, q, :])
                    fma(bi[:, q, :], tr[:, q2, :], mi, bi[:, q, :])
            else:
                p1 = (b1 if nu == 2 else s1)[:, :, 0:cnt]; p2 = (b2 if nu == 2 else s2)[:, :, 0:cnt]
                mr, mi = mub(0, l, usl, cnt), mub(1, l, usl, cnt)
                op2(p1, tr, mr, ALU.mult); op2(p2, ti, mi, ALU.mult)
                op2(p1, p1, p2, ALU.subtract); op2(br[:, usl, :], p1, ar[:, usl, :], ALU.add)
                op2(p1, tr, mi, ALU.mult); op2(p2, ti, mr, ALU.mult)
                op2(p1, p1, p2, ALU.add); op2(bi[:, usl, :], p1, ai[:, usl, :], ALU.add)
            cp(ar[:, usl, :], tr); cp(ai[:, usl, :], ti)


def phase2_s5(nc, P, I, S, K):
    from contextlib import ExitStack
    TWO_PI = 2.0 * math.pi
    with ExitStack() as es0:
        def sbp(name, shape, dt):
            return es0.enter_context(nc.sbuf_tensor(name, list(shape), dt))
        TOEP = sbp("TOEP", [128, 32, 128], BF)
        WSTAB = sbp("WSTAB", [128, 32, 2, 2, 128], BF)
        WOUT = sbp("WOUT", [128, 32, 2, 128], BF)
        MU = sbp("MU", [128, 2, 10, 32], F32)
        NMI = sbp("NMI", [128, 10, 32], F32)
        with ExitStack() as es:
            def sb(name, shape, dt=F32):
                return es.enter_context(nc.sbuf_tensor(name, list(shape), dt))
            T3 = sb("T3", [128, 32, 3]); B4 = sb("B4", [128, 32, 2, 16]); C4 = sb("C4", [128, 32, 2, 16])
            dd = sb("dd", [128, 32])
            sm = {n: sb("sm_" + n, [128, 32]) for n in
                  ("step", "rho", "th", "tq", "tf", "r", "s1", "hs", "c1", "sn", "cs", "ar", "ai", "den", "nr", "ni",
                   "cr", "ci", "ir", "ii", "tA", "tB", "am1")}
            tiq = sb("tiq", [128, 32], I32)
            PW = sb("PW", [128, 2, 32, 9]); NW = sb("NW", [128, 2, 32, 9])
            QA = sb("QA", [128, 2, 32, 9]); QB = sb("QB", [128, 2, 32, 9])
            bbar = sb("bbar", [128, 2, 32, 16])
            XB = sb("XB", [128, 2, 32, 8, 16]); XC = sb("XC", [128, 2, 32, 9, 16])
            WST = sb("WST", [128, 2, 32, 128], BF)
            big1 = sb("big1", [128, 32, 9, 16]); big2 = sb("big2", [128, 32, 9, 16])
            maskF = sb("maskF", [128, 128]); maskB = sb("maskB", [128, 128])
            tg = sb("tg", [128, 128]); tb = sb("tb", [128, 128])
            ptr = es.enter_context(nc.psum_tensor("ptr", [128, 2, 128], BF))
            ptf2 = es.enter_context(nc.psum_tensor("ptf2", [128, 128], F32))
            ptb2 = es.enter_context(nc.psum_tensor("ptb2", [128, 128], F32))
            ptf = es.enter_context(nc.psum_tensor("ptf", [128, 128], F32))
            ptb = es.enter_context(nc.psum_tensor("ptb", [128, 128], F32))
            kk = ["tab"]

            def dv(fn): P.dve(fn, r=kk, w=kk)

            def ac(fn): P.act(fn, r=kk, w=kk)
            P.dma(SP, T3[:], I["s5t"][:, :, :], w=kk); P.dma(SP, B4[:], I["s5b"][:, :, :, :], w=kk)
            P.dma(SP, C4[:], I["s5c"][:, :, :, :], w=kk); P.dma(SP, dd[:], I["s5dd"][:, :], w=kk)
            P.pool(lambda e: e.memset(maskF[:], 1.0), r=kk, w=kk)
            P.pool(lambda e: e.affine_select(out=maskF[:].rearrange("p (t s) -> p t s", s=16), in_=maskF[:].rearrange("p (t s) -> p t s", s=16),
                                             pattern=[[16, 8], [0, 16]], compare_op=ALU.is_ge, fill=0.0, base=15,
                                             channel_multiplier=-1), r=kk, w=kk)
            P.pool(lambda e: e.memset(maskB[:], 1.0), r=kk, w=kk)
            P.pool(lambda e: e.affine_select(out=maskB[:].rearrange("p (t s) -> p t s", s=16), in_=maskB[:].rearrange("p (t s) -> p t s", s=16),
                                             pattern=[[-16, 8], [0, 16]], compare_op=ALU.is_ge, fill=0.0, base=0,
                                             channel_multiplier=1), r=kk, w=kk)
            Lr, Li, Ls = T3[:, :, 0], T3[:, :, 1], T3[:, :, 2]
            s_ = {k: v[:] for k, v in sm.items()}

            def tt(o, a, b, op): dv(lambda e: e.tensor_tensor(out=o, in0=a, in1=b, op=op))

            def ts(o, a, m, add): dv(lambda e: e.tensor_scalar(out=o, in0=a, scalar1=m, scalar2=add, op0=ALU.mult, op1=ALU.add))
            ac(lambda e: e.activation(out=s_["step"], in_=Ls, func=AF.Exp))
            tt(s_["tA"], Lr, s_["step"], ALU.mult)
            ac(lambda e: e.activation(out=s_["rho"], in_=s_["tA"], func=AF.Exp))
            tt(s_["th"], Li, s_["step"], ALU.mult)
            ts(s_["tq"], s_["th"], 1.0 / TWO_PI, 0.0)
            dv(lambda e: e.tensor_copy(out=tiq[:], in_=s_["tq"]))
            dv(lambda e: e.tensor_copy(out=s_["tf"], in_=tiq[:]))
            tt(s_["r"], s_["tq"], s_["tf"], ALU.subtract)
            ac(lambda e: e.activation(out=s_["s1"], in_=s_["r"], func=AF.Sin, scale=math.pi))
            ac(lambda e: e.activation(out=s_["hs"], in_=s_["r"], func=AF.Sin, scale=math.pi / 2))
            tt(s_["tA"], s_["hs"], s_["hs"], ALU.mult); ts(s_["c1"], s_["tA"], -2.0, 1.0)
            tt(s_["tA"], s_["s1"], s_["c1"], ALU.mult); ts(s_["sn"], s_["tA"], 2.0, 0.0)
            tt(s_["tA"], s_["s1"], s_["s1"], ALU.mult); ts(s_["cs"], s_["tA"], -2.0, 1.0)
            tt(s_["ar"], s_["rho"], s_["cs"], ALU.mult); tt(s_["ai"], s_["rho"], s_["sn"], ALU.mult)
            tt(s_["tA"], Lr, Lr, ALU.mult); tt(s_["tB"], Li, Li, ALU.mult); tt(s_["den"], s_["tA"], s_["tB"], ALU.add)
            dv(lambda e: e.reciprocal(out=s_["den"], in_=s_["den"]))
            ts(s_["am1"], s_["ar"], 1.0, -1.0)
            tt(s_["tA"], s_["am1"], Lr, ALU.mult); tt(s_["tB"], s_["ai"], Li, ALU.mult); tt(s_["nr"], s_["tA"], s_["tB"], ALU.add)
            tt(s_["tA"], s_["ai"], Lr, ALU.mult); tt(s_["tB"], s_["am1"], Li, ALU.mult); tt(s_["ni"], s_["tA"], s_["tB"], ALU.subtract)
            tt(s_["cr"], s_["nr"], s_["den"], ALU.mult); tt(s_["ci"], s_["ni"], s_["den"], ALU.mult)
            tt(s_["tA"], s_["rho"], s_["rho"], ALU.mult)
            dv(lambda e: e.reciprocal(out=s_["tA"], in_=s_["tA"]))
            tt(s_["ir"], s_["ar"], s_["tA"], ALU.mult); tt(s_["tB"], s_["ai"], s_["tA"], ALU.mult); ts(s_["ii"], s_["tB"], -1.0, 0.0)
            for (W_, xr, xi) in ((PW, s_["ar"], s_["ai"]), (NW, s_["ir"], s_["ii"])):
                dv(lambda e, W_=W_: e.memset(W_[:, 0, :, 0], 1.0)); dv(lambda e, W_=W_: e.memset(W_[:, 1, :, 0], 0.0))
                for k in range(1, 9):
                    cmul(P.dve, W_[:, 0, :, k], W_[:, 1, :, k], W_[:, 0, :, k - 1], W_[:, 1, :, k - 1], xr, xi, s_["tA"], s_["tB"], kk)
            dv(lambda e: e.tensor_copy(out=MU[:, 0, 0, :], in_=PW[:, 0, :, 8])); dv(lambda e: e.tensor_copy(out=MU[:, 1, 0, :], in_=PW[:, 1, :, 8]))
            for l in range(1, 10):
                cmul(P.dve, MU[:, 0, l, :], MU[:, 1, l, :], MU[:, 0, l - 1, :], MU[:, 1, l - 1, :], MU[:, 0, l - 1, :], MU[:, 1, l - 1, :],
                     s_["tA"], s_["tB"], kk)
            dv(lambda e: e.tensor_scalar(out=NMI[:], in0=MU[:, 1], scalar1=-1.0, scalar2=None, op0=ALU.mult))
            b1 = big1[:, :, 0, :]; b2 = big2[:, :, 0, :]
            cmul(P.dve, bbar[:, 0], bbar[:, 1], bcast(s_["cr"].unsqueeze(2), [128, 32, 16]), bcast(s_["ci"].unsqueeze(2), [128, 32, 16]),
                 B4[:, :, 0, :], B4[:, :, 1, :], b1, b2, kk)
            for c in range(2):
                dv(lambda e, c=c: e.tensor_copy(out=QA[:, c, 0:16, :], in_=NW[:, c, 0:16, :]))
                dv(lambda e, c=c: e.tensor_copy(out=QA[:, c, 16:32, :], in_=PW[:, c, 16:32, :]))
                dv(lambda e, c=c: e.tensor_copy(out=QB[:, c, 0:16, :], in_=PW[:, c, 0:16, :]))
                dv(lambda e, c=c: e.tensor_copy(out=QB[:, c, 16:32, :], in_=NW[:, c, 16:32, :]))
            sh8 = [128, 32, 8, 16]; sh9 = [128, 32, 9, 16]
            cmul(P.dve, XB[:, 0], XB[:, 1], bcast(QA[:, 0, :, 0:8].unsqueeze(3), sh8), bcast(QA[:, 1, :, 0:8].unsqueeze(3), sh8),
                 bcast(bbar[:, 0].unsqueeze(2), sh8), bcast(bbar[:, 1].unsqueeze(2), sh8), big1[:, :, 0:8, :], big2[:, :, 0:8, :], kk)
            cmul(P.dve, XC[:, 0], XC[:, 1], bcast(QB[:, 0, :, :].unsqueeze(3), sh9), bcast(QB[:, 1, :, :].unsqueeze(3), sh9),
                 bcast(C4[:, :, 0, :].unsqueeze(2), sh9), bcast(C4[:, :, 1, :].unsqueeze(2), sh9), big1[:], big2[:], kk)
            for c in range(2):
                dv(lambda e, c=c: e.tensor_copy(out=WST[:, c, 16:32, :].rearrange("p u (k s) -> p u k s", s=16), in_=XB[:, c, 16:32]))
            sh16 = [128, 16, 8, 16]
            cmul(P.dve, WST[:, 0, 0:16, :].rearrange("p u (k s) -> p u k s", s=16), WST[:, 1, 0:16, :].rearrange("p u (k s) -> p u k s", s=16),
                 bcast(PW[:, 0, 0:16, 7:8].unsqueeze(3), sh16), bcast(PW[:, 1, 0:16, 7:8].unsqueeze(3), sh16),
                 XB[:, 0, 0:16], XB[:, 1, 0:16], big1[:, 0:16, 0:8, :], big2[:, 0:16, 0:8, :], kk)
            dv(lambda e: e.tensor_copy(out=WOUT[:, 0:16, 0, :].rearrange("p u (k s) -> p u k s", s=16), in_=XC[:, 0, 0:16, 1:9, :]))
            dv(lambda e: e.tensor_scalar(out=WOUT[:, 0:16, 1, :].rearrange("p u (k s) -> p u k s", s=16), in0=XC[:, 1, 0:16, 1:9, :], scalar1=-1.0, scalar2=None, op0=ALU.mult))
            wob_r = WOUT[:, 16:32, 0, :].rearrange("p u (k s) -> p u k s", s=16)
            wob_i = WOUT[:, 16:32, 1, :].rearrange("p u (k s) -> p u k s", s=16)
            cmul(P.dve, wob_r, wob_i, bcast(PW[:, 0, 16:32, 8:9].unsqueeze(3), sh16), bcast(PW[:, 1, 16:32, 8:9].unsqueeze(3), sh16),
                 XC[:, 0, 16:32, 0:8, :], XC[:, 1, 16:32, 0:8, :], big1[:, 0:16, 0:8, :], big2[:, 0:16, 0:8, :], kk)
            dv(lambda e: e.tensor_scalar(out=wob_i, in0=wob_i, scalar1=-1.0, scalar2=None, op0=ALU.mult))
            P.pool(lambda e: e.memset(WSTAB[:], 0.0), r=kk, w=kk)
            for u in range(32):
                for c in range(2):
                    P.pe(lambda e, u=u, c=c: e.transpose(ptr[:, c, :], WST[:, c, u, :], K.ident[:]), r=kk + ["ident"], w=kk)
                dv(lambda e, u=u: e.tensor_copy(out=WSTAB[:, u, :, 0, 0:64], in_=ptr[:, :, 0:64]))
                dv(lambda e, u=u: e.tensor_copy(out=WSTAB[:, u, :, 1, 64:128], in_=ptr[:, :, 64:128]))
            for g in range(32):
                gp, g2 = g // 2, g % 2
                rows = slice(g2 * 64, g2 * 64 + 64)
                for (u, pt_, pt2_) in ((gp, ptf, ptf2), (16 + gp, ptb, ptb2)):
                    P.pe(lambda e, u=u, pt_=pt_, rows=rows: e.matmul(pt_[:], lhsT=XB[rows, 0, u].rearrange("p k s -> p (k s)"),
                                                                    rhs=XC[rows, 0, u, 0:8, :].rearrange("p k s -> p (k s)"),
                                                                    start=True, stop=True), r=kk, w=kk)
                    P.pe(lambda e, u=u, pt2_=pt2_, rows=rows: e.matmul(pt2_[:], lhsT=XB[rows, 1, u].rearrange("p k s -> p (k s)"),
                                                                      rhs=XC[rows, 1, u, 0:8, :].rearrange("p k s -> p (k s)"),
                                                                      start=True, stop=True), r=kk, w=kk)
                dv(lambda e: e.tensor_copy(out=tg[:], in_=ptf[:]))
                tt(tg[:], tg[:], ptf2[:], ALU.subtract)
                tt(tg[:], tg[:], maskF[:], ALU.mult)
                dv(lambda e: e.tensor_copy(out=tb[:], in_=ptb[:]))
                tt(tb[:], tb[:], ptb2[:], ALU.subtract)
                tt(tb[:], tb[:], maskB[:], ALU.mult)
                tt(tg[:], tg[:], tb[:], ALU.add)
                dv(lambda e, g=g: e.scalar_tensor_tensor(out=TOEP[:, g, :], in0=K.identf[:], scalar=dd[:, g:g + 1], in1=tg[:],
                                                         op0=ALU.mult, op1=ALU.add))
            P.emit()
        s5_main(nc, P, I, S, K, TOEP, WSTAB, WOUT, MU, NMI)


def s5_main(nc, P, I, S, K, TOEP, WSTAB, WOUT, MU, NMI):
    from contextlib import ExitStack
    blocks = [(0, 32), (32, 544), (544, 1056)]
    with ExitStack() as es:
        def sb(name, shape, dt=F32):
            return es.enter_context(nc.sbuf_tensor(name, list(shape), dt))
        Ug2 = [sb("Ug%d" % i, [128, 8, NCH], BF) for i in range(2)]
        SCr = sb("SCr", [128, 8, NCH]); SCi = sb("SCi", [128, 8, NCH])
        Hr = sb("Hr", [128, 8, NCH], BF); Hi = sb("Hi", [128, 8, NCH], BF)
        tmpall = sb("stmpall", [128, 2 * 1024])
        tsm = [sb("stsm%d" % i, [128, 4, 64]) for i in range(2)]
        YGs = [sb("YGs%d" % i, [128, NCH], BF) for i in range(2)]

        def bigt(i):
            return tmpall[:, 1024 * i:1024 * (i + 1)].rearrange("p (a b) -> p a b", a=2)
        tmp = [tsm[0][:], tsm[1][:], None, None, bigt(0), bigt(1)]
        tot = sb("tot", [128, 2, 2, 4, 1])
        pS = [es.enter_context(nc.psum_tensor("pS%d" % i, [128, 512], F32)) for i in range(2)]
        pY = [es.enter_context(nc.psum_tensor("pY%d" % i, [128, 512], F32)) for i in range(2)]
        cnt = {"n": 0, "yg": 0}

        def load_u(gb):
            P.dma(SP, Ug2[gb % 2][:], S["U"][8 * gb:8 * gb + 8].rearrange("g p c -> p g c"), r=["Usc"], w=["Ug%d" % (gb % 2)])

        def s_job(gb, d_):
            Ug = Ug2[gb % 2]; uk = "Ug%d" % (gb % 2)
            for i in range(4):
                q = d_ * 4 + i; u = d_ * 16 + 4 * gb + i
                for c in range(2):
                    SC = SCr if c == 0 else SCi
                    for (c0, c1) in blocks:
                        n = cnt["n"]; cnt["n"] += 1
                        p_ = pS[n % 2]; pk = "pS%d" % (n % 2)
                        P.pe(lambda e, p_=p_, u=u, c=c, i=i, c0=c0, c1=c1, Ug=Ug: e.matmul(
                            p_[:, 0:c1 - c0], lhsT=WSTAB[:, u, c, 0, :], rhs=Ug[:, 2 * i, c0:c1], start=True, stop=False),
                            r=[uk], w=[pk])
                        P.pe(lambda e, p_=p_, u=u, c=c, i=i, c0=c0, c1=c1, Ug=Ug: e.matmul(
                            p_[:, 0:c1 - c0], lhsT=WSTAB[:, u, c, 1, :], rhs=Ug[:, 2 * i + 1, c0:c1], start=False, stop=True),
                            r=[uk], w=[pk])
                        P.act(lambda e, p_=p_, SC=SC, q=q, c0=c0, c1=c1: e.activation(out=SC[:, q, c0:c1], in_=p_[:, 0:c1 - c0],
                                                                                  func=AF.Copy), r=[pk], w=["sc%d" % d_])

        def scan_job(gb, d_):
            us = slice(d_ * 16 + 4 * gb, d_ * 16 + 4 * gb + 4)
            qs = slice(d_ * 4, d_ * 4 + 4)
            vrc, vic = SCr[:, qs, 0:32], SCi[:, qs, 0:32]
            vrl, vil = SCr[:, qs, 32:NCH], SCi[:, qs, 32:NCH]
            if d_ == 1:
                vrc, vic, vrl, vil = vrc[:, :, ::-1], vic[:, :, ::-1], vrl[:, :, ::-1], vil[:, :, ::-1]
            tt_ = (tot[:, d_, 0], tot[:, d_, 1])
            s5_scan(P.dve, vrc, vic, 32, 5, MU, NMI, us, None, tt_, tmp, "sc%d" % d_, "stmp0", True)
            s5_scan(P.dve, vrl, vil, 1024, 10, MU, NMI, us, tt_, None, tmp, "sc%d" % d_, "stmp0", True)
            P.act(lambda e, qs=qs: e.activation(out=Hr[:, qs, :], in_=SCr[:, qs, :], func=AF.Copy), r=["sc%d" % d_], w=["Hr%d" % d_])
            P.pool(lambda e, qs=qs: e.tensor_copy(out=Hi[:, qs, :], in_=SCi[:, qs, :]), r=["sc%d" % d_], w=["Hi%d" % d_])

        def y_job(gb):
            Ug = Ug2[gb % 2]; uk = "Ug%d" % (gb % 2)
            for i8 in range(8):
                g = 8 * gb + i8; gpl = i8 // 2; g2 = i8 % 2
                rows = slice(g2 * 64, g2 * 64 + 64)
                yb = cnt["yg"] % 2; cnt["yg"] += 1
                yg = YGs[yb]; yk = "YGs%d" % yb
                for (c0, c1) in blocks:
                    n = cnt["n"]; cnt["n"] += 1
                    p_ = pY[n % 2]; pk = "pY%d" % (n % 2)
                    w_ = c1 - c0
                    P.pe(lambda e, p_=p_, g=g, i8=i8, c0=c0, c1=c1, w_=w_, Ug=Ug: e.matmul(p_[:, 0:w_], lhsT=TOEP[:, g, :], rhs=Ug[:, i8, c0:c1],
                                                                                        start=True, stop=False), r=[uk], w=[pk])
                    for d_ in range(2):
                        q = d_ * 4 + gpl; u = d_ * 16 + 4 * gb + gpl
                        P.pe(lambda e, p_=p_, u=u, q=q, c0=c0, c1=c1, w_=w_, rows=rows: e.matmul(
                            p_[:, 0:w_], lhsT=WOUT[rows, u, 0, :], rhs=Hr[rows, q, c0:c1], start=False, stop=False), r=["Hr%d" % d_], w=[pk])
                        P.pe(lambda e, p_=p_, u=u, q=q, c0=c0, c1=c1, w_=w_, rows=rows, d_=d_: e.matmul(
                            p_[:, 0:w_], lhsT=WOUT[rows, u, 1, :], rhs=Hi[rows, q, c0:c1], start=False, stop=(d_ == 1)), r=["Hi%d" % d_], w=[pk])
                    P.act(lambda e, p_=p_, yg=yg, c0=c0, c1=c1, w_=w_: e.activation(out=yg[:, c0:c1], in_=p_[:, 0:w_], func=AF.Gelu),
                          r=[pk], w=[yk])
                for t in range(8):
                    q_ = SP if t % 2 == 0 else ACT
                    P.dma(q_, S["YT"][g * 16:(g + 1) * 16, t, :], yg[t * 16:(t + 1) * 16, :], r=[yk], w=["YTsc"])

        load_u(0)
        s_job(0, 0); s_job(0, 1)
        for gb in range(4):
            if gb + 1 < 4:
                load_u(gb + 1)
            scan_job(gb, 0)
            if gb + 1 < 4:
                s_job(gb + 1, 0)
            scan_job(gb, 1)
            if gb + 1 < 4:
                s_job(gb + 1, 1)
            y_job(gb)
        P.emit()
    s5_glu(nc, P, I, S, K)


def s5_glu(nc, P, I, S, K):
    from contextlib import ExitStack
    CW = 256
    with ExitStack() as es:
        def sb(name, shape, dt=F32):
            return es.enter_context(nc.sbuf_tensor(name, list(shape), dt))
        YT = sb("YT", [128, 4, 8, NCH], BF)
        Wg = sb("Wglu", [128, 4, 512], BF); bg = sb("bglu", [128, 4])
        sg = [sb("sg%d" % i, [128, 4, CW], BF) for i in range(2)]
        so = [sb("so%d" % i, [128, 4, CW], BF) for i in range(2)]
        stm = [sb("stm%d" % i, [128, 512], BF) for i in range(3)]
        pz = [es.enter_context(nc.psum_tensor("pz%d" % i, [128, 4, CW], F32)) for i in range(2)]
        pt = [es.enter_context(nc.psum_tensor("ptg%d" % i, [128, 4, 128], BF)) for i in range(2)]
        for fc in range(4):
            P.dma(SP, YT[:, fc], S["YT"][fc * 128:(fc + 1) * 128], r=["YTsc"], w=["YT"])
        P.dma(POOL, Wg[:], I["s5_w_glu"].rearrange("(kc p) n -> p kc n", p=128), w=["Wglu"])
        P.dma(SP, bg[:], I["s5bg"][:, :], w=["bglu"])
        s5v = S["S5"].rearrange("(c t) f -> t c f", t=8)
        n = 0; nt_ = 0
        cblocks = [(0, 32)] + [(32 + CW * k, 32 + CW * (k + 1)) for k in range(1024 // CW)]
        for t in range(8):
            for (c0, c1) in cblocks:
                b = n % 2; n += 1
                w_ = c1 - c0
                for fo in range(4):
                    for fi in range(4):
                        P.pe(lambda e, b=b, fo=fo, fi=fi, t=t, c0=c0, c1=c1, w_=w_: e.matmul(
                            pz[b][:, fo, 0:w_], lhsT=Wg[:, fi, fo * 128:(fo + 1) * 128], rhs=YT[:, fi, t, c0:c1],
                            start=(fi == 0), stop=(fi == 3)), r=["YT", "Wglu"], w=["pz%d" % b])
                    P.act(lambda e, b=b, fo=fo, w_=w_: e.activation(out=sg[b][:, fo, 0:w_], in_=pz[b][:, fo, 0:w_], func=AF.Sigmoid,
                                                                    bias=bg[:, fo:fo + 1], scale=1.0), r=["pz%d" % b, "bglu"], w=["sg%d" % b])
                P.dve(lambda e, b=b, t=t, c0=c0, c1=c1, w_=w_: e.tensor_tensor(out=so[b][:, :, 0:w_], in0=YT[:, :, t, c0:c1],
                                                                              in1=sg[b][:, :, 0:w_], op=ALU.mult),
                      r=["YT", "sg%d" % b], w=["so%d" % b])
                for s0 in range(0, w_, 128):
                    sw = min(128, w_ - s0)
                    tb = nt_ % 2; sbi = nt_ % 3; nt_ += 1
                    for fo in range(4):
                        P.pe(lambda e, b=b, fo=fo, s0=s0, sw=sw, tb=tb: e.transpose(pt[tb][0:sw, fo, :], so[b][:, fo, s0:s0 + sw], K.ident[:]),
                             r=["so%d" % b, "ident"], w=["ptg%d" % tb])
                    P.dve(lambda e, sw=sw, tb=tb, sbi=sbi: e.tensor_copy(out=stm[sbi][0:sw, :], in_=pt[tb][0:sw].rearrange("p a f -> p (a f)")),
                          r=["ptg%d" % tb], w=["stm%d" % sbi])
                    P.dma(ACT, s5v[t, c0 + s0:c0 + s0 + sw, :], stm[sbi][0:sw, :], r=["stm%d" % sbi], w=["S5sc"])
        P.emit()


MLA_SCALE = 192 ** -0.5
WIN_SCALE = 64 ** -0.5


def phase3_mla(nc, P, I, S, K):
    from contextlib import ExitStack
    with ExitStack() as es0:
        def sbp(name, shape, dt):
            return es0.enter_context(nc.sbuf_tensor(name, list(shape), dt))
        QTa = sbp("QTa", [128, 4, NOWN], BF); QTb = sbp("QTb", [128, 4, NOWN], BF); OT = sbp("OT", [128, 4, NOWN], BF)
        modv = sbp("modv3", [128, 2, 3 * D], F32)
        P.dma(SP, modv[:], S["mod"][0][:, :, 0:3 * D], r=["modsc0"], w=["modv3"])
        with ExitStack() as es:
            def sb(name, shape, dt=F32):
                return es.enter_context(nc.sbuf_tensor(name, list(shape), dt))

            def ps(name, shape, dt=F32):
                return es.enter_context(nc.psum_tensor(name, list(shape), dt))
            Wqi = sb("Wqi", [128, KC, 384], BF); Wqb = sb("Wqb", [128, 3, 768], BF)
            gqa = sb("gqa", [128, 384]); gq = sb("gq", [128, 192])
            xt = [sb("xq%d" % i, [128, D]) for i in range(2)]; rp = [sb("rq%d" % i, [128, 2, 64]) for i in range(2)]
            tmp2 = [sb("tmpq%d" % i, [128, D]) for i in range(2)]; a_bf2 = [sb("a_bfq%d" % i, [128, D], BF) for i in range(2)]
            aT = [sb("aTq%d" % i, [128, KC, 128], BF) for i in range(2)]
            st2 = [sb("stq%d" % i, [128, 16]) for i in range(2)]
            cq_bf2 = [sb("cq_bf%d" % i, [128, 384], BF) for i in range(2)]; cqT2 = [sb("cqT%d" % i, [128, 3, 128], BF) for i in range(2)]
            qf2 = [sb("qf%d" % i, [128, 4, 192]) for i in range(2)]; qsq2 = [sb("qsq%d" % i, [128, 4, 192]) for i in range(2)]
            t12 = [sb("t1q%d" % i, [128, 4, 64]) for i in range(2)]
            Q_bf2 = [sb("Q_bf%d" % i, [128, 4, 192], BF) for i in range(2)]
            pTa = ps("pTaq", [128, 8, 128], BF); pQ = ps("pQ", [128, 384]); pcq = ps("pcq", [128, 3, 128], BF)
            pQB = ps("pQB", [128, 1024]); pTq = ps("pTq", [128, 8, 128], BF)
            wv = I["a_w_in"].rearrange("(kc p) n -> p kc n", p=128)
            P.dma(POOL, Wqi[:], wv[:, :, 512:896], w=["Wqi"])
            P.dma(POOL, Wqb[:], I["mla_w_q_b"].rearrange("(kc p) n -> p kc n", p=128), w=["Wqb"])
            P.dma(SP, gqa[:], bcast(I["mla_qa_norm"][0:1, :], [128, 384]), w=["gqa"])
            P.dma(SP, gq[:], bcast(I["mla_q_norm"][0:1, :], [128, 192]), w=["gq"])
            P.dve(lambda e: e.tensor_scalar(out=gq[:], in0=gq[:], scalar1=MLA_SCALE, scalar2=None, op0=ALU.mult), r=["gq"], w=["gq"])
            def q_tile(t):
                b = t % 2; typ = 1 if t < 2 else 0
                sfx = "_%d" % b
                tmp, a_bf, st, cq_bf, cqT, qf, qsq, t1, Q_bf = (tmp2[b], a_bf2[b], st2[b], cq_bf2[b], cqT2[b], qf2[b], qsq2[b], t12[b], Q_bf2[b])
                rows = slice(t * 128, (t + 1) * 128)
                P.dma(SP, xt[b][:], I["xo"][rows, :], w=["xq%d" % b])
                P.dma(SP, rp[b][:], I["ropeA_o"][rows, :, :], w=["rq%d" % b])
                norm_mod_T(P, K, xt[b][:], "xq%d" % b, modv[:, typ, D:2 * D], modv[:, typ, 0:D], "modv3",
                           tmp[:], st[:, 0:1], st[:, 1:2], tmp[:], a_bf[:], pTa, aT[b][:], "p3" + sfx, "aTq%d" % b, pTkey="p3pT")
                for kc in range(KC):
                    P.pe(lambda e, kc=kc, b=b: e.matmul(pQ[:], lhsT=aT[b][:, kc, :], rhs=Wqi[:, kc, :], start=(kc == 0), stop=(kc == KC - 1)),
                         r=["aTq%d" % b, "Wqi"], w=["pQ"])
                P.act(lambda e: e.activation(out=tmp[:, 0:384], in_=pQ[:], func=AF.Square), r=["pQ"], w=["p3" + sfx + "tmp"])
                P.dve(lambda e: e.reduce_sum(out=st[:, 2:3], in_=tmp[:, 0:384], axis=AX.X), r=["p3" + sfx + "tmp"], w=["qassq" + sfx])
                rstd_from_ssq(P, st[:, 2:3], st[:, 3:4], 1, 384, "qa", sfx)
                P.dve(lambda e: e.scalar_tensor_tensor(out=cq_bf[:], in0=pQ[:], scalar=st[:, 3:4], in1=gqa[:], op0=ALU.mult, op1=ALU.mult),
                      r=["pQ", "qarstd" + sfx, "gqa"], w=["cq_bf" + sfx])
                for k3 in range(3):
                    P.pe(lambda e, k3=k3: e.transpose(pcq[:, k3, :], cq_bf[:, k3 * 128:(k3 + 1) * 128], K.ident[:]), r=["cq_bf" + sfx, "ident"], w=["pcq"])
                P.dve(lambda e: e.tensor_copy(out=cqT[:], in_=pcq[:]), r=["pcq"], w=["cqT" + sfx])
                for (c0, c1) in ((0, 512), (512, 768)):
                    for k3 in range(3):
                        P.pe(lambda e, k3=k3, c0=c0, c1=c1: e.matmul(pQB[:, c0:c1], lhsT=cqT[:, k3, :], rhs=Wqb[:, k3, c0:c1],
                                                                     start=(k3 == 0), stop=(k3 == 2)), r=["cqT" + sfx, "Wqb"], w=["pQB"])
                P.act(lambda e: e.activation(out=qf[:], in_=pQB[:, 0:768].rearrange("p (h c) -> p h c", h=4), func=AF.Copy), r=["pQB"], w=["qf" + sfx])
                P.dve(lambda e: e.tensor_tensor(out=qsq[:], in0=qf[:], in1=qf[:], op=ALU.mult), r=["qf" + sfx], w=["qsq" + sfx])
                P.dve(lambda e: e.reduce_sum(out=st[:, 4:8], in_=qsq[:], axis=AX.X), r=["qsq" + sfx], w=["qssq" + sfx])
                rstd_from_ssq(P, st[:, 4:8], st[:, 8:12], 4, 192, "q", sfx)
                P.dve(lambda e: e.tensor_tensor(out=qf[:], in0=qf[:], in1=bcast(st[:, 8:12].unsqueeze(2), [128, 4, 192]), op=ALU.mult),
                      r=["qf" + sfx, "qrstd" + sfx], w=["qf" + sfx])
                P.pool(lambda e: e.tensor_tensor(out=qf[:], in0=qf[:], in1=bcast(gq[:].unsqueeze(1), [128, 4, 192]), op=ALU.mult),
                       r=["qf" + sfx, "gq"], w=["qf" + sfx])
                rope_inplace(P, P.pool, P.dve, qf[:, :, 128:192], rp[b], t1[:], 4, ["rq%d" % b], "qf" + sfx, "t1q" + sfx)
                P.act(lambda e: e.activation(out=Q_bf[:], in_=qf[:], func=AF.Copy), r=["qf" + sfx], w=["Q_bf" + sfx])
                for h in range(4):
                    P.pe(lambda e, h=h: e.transpose(pTq[:, 2 * h, :], Q_bf[:, h, 0:128], K.ident[:]), r=["Q_bf" + sfx, "ident"], w=["pTq"])
                    P.pe(lambda e, h=h: e.transpose(pTq[0:64, 2 * h + 1, :], Q_bf[:, h, 128:192], K.ident[:]), r=["Q_bf" + sfx, "ident"], w=["pTq"])
                pv = pTq[:].rearrange("p (h two) t -> p h two t", two=2)
                P.dve(lambda e, pv=pv, rows=rows: e.tensor_copy(out=QTa[:, :, rows], in_=pv[:, :, 0, :]), r=["pTq"], w=["QTa_%d" % t])
                P.dve(lambda e, pv=pv, rows=rows: e.tensor_copy(out=QTb[0:64, :, rows], in_=pv[0:64, :, 1, :]), r=["pTq"], w=["QTb_%d" % t])
            tiles = [P.capture(lambda t=t: q_tile(t)) for t in range(NTO)]
            K.q_stages = len(tiles[2])
            P.run_staged(tiles, skew=max(1, (len(tiles[2]) + 1) // 2))
            P.emit()
        with ExitStack() as es:
            def sb(name, shape, dt=F32):
                return es.enter_context(nc.sbuf_tensor(name, list(shape), dt))

            def ps(name, shape, dt=F32):
                return es.enter_context(nc.psum_tensor(name, list(shape), dt))
            KA = [sb("KA%d" % i, [128, NF], BF) for i in range(2)]; KB = [sb("KB%d" % i, [128, NF], BF) for i in range(2)]
            VH = [sb("VH%d" % i, [128, NTF, 128], BF) for i in range(2)]
            PT = [sb("PT%d" % i, [128, 512], BF) for i in range(4)]
            rec = sb("rec", [128, 512])
            pS = [ps("pSa%d" % i, [128, 512]) for i in range(2)]; pO = ps("pO", [128, 512]); pDen = ps("pDen", [128, 512])
            vsv = S["V"].rearrange("(kt p) (h d) -> p kt h d", p=128, h=4)
            blocks = [(0, 256, 2)] + [(256 + 512 * k, 256 + 512 * (k + 1), NTF) for k in range(4)] + [(2304, 2560, NTF)]
            npt = 0; nps = 0
            for h in range(4):
                hb = h % 2
                P.dma(SP, KA[hb][:], S["KTa"][:, h, :], r=["KTa"], w=["KA%d" % hb])
                P.dma(SP, KB[hb][0:64, :], S["KTb"][:, h, :], r=["KTb"], w=["KB%d" % hb])
                P.dma(POOL, VH[hb][:], vsv[:, :, h, :], r=["Vsc"], w=["VH%d" % hb])
                for (q0, q1, nkt) in blocks:
                    w_ = q1 - q0
                    pend = None
                    for kt in range(nkt + 1):
                        if kt < nkt:
                            sbi = nps % 2; nps += 1
                            ks = slice(kt * 128, (kt + 1) * 128)
                            P.pe(lambda e, sbi=sbi, hb=hb, ks=ks, q0=q0, q1=q1, w_=w_, h=h: e.matmul(
                                pS[sbi][:, 0:w_], lhsT=KA[hb][:, ks], rhs=QTa[:, h, q0:q1], start=True, stop=False),
                                r=["KA%d" % hb, "QTa"], w=["pSa%d" % sbi])
                            P.pe(lambda e, sbi=sbi, hb=hb, ks=ks, q0=q0, q1=q1, w_=w_, h=h: e.matmul(
                                pS[sbi][:, 0:w_], lhsT=KB[hb][0:64, ks], rhs=QTb[0:64, h, q0:q1], start=False, stop=True),
                                r=["KB%d" % hb, "QTb"], w=["pSa%d" % sbi])
                            pi = npt % 4; npt += 1
                            P.act(lambda e, sbi=sbi, pi=pi, w_=w_: e.activation(out=PT[pi][:, 0:w_], in_=pS[sbi][:, 0:w_], func=AF.Exp),
                                  r=["pSa%d" % sbi], w=["PT%d" % pi])
                            cur = (kt, pi)
                        else:
                            cur = None
                        if pend is not None:
                            k0, p0 = pend
                            P.pe(lambda e, k0=k0, p0=p0, hb=hb, w_=w_, nkt=nkt: e.matmul(
                                pO[:, 0:w_], lhsT=VH[hb][:, k0, :], rhs=PT[p0][:, 0:w_], start=(k0 == 0), stop=(k0 == nkt - 1)),
                                r=["VH%d" % hb, "PT%d" % p0], w=["pO"])
                            P.pe(lambda e, k0=k0, p0=p0, w_=w_, nkt=nkt: e.matmul(
                                pDen[:, 0:w_], lhsT=K.ones_bf[:], rhs=PT[p0][:, 0:w_], start=(k0 == 0), stop=(k0 == nkt - 1)),
                                r=["ones_bf", "PT%d" % p0], w=["pDen"])
                        pend = cur
                    P.dve(lambda e, w_=w_: e.reciprocal(out=rec[:, 0:w_], in_=pDen[:, 0:w_]), r=["pDen"], w=["rec"])
                    P.dve(lambda e, w_=w_, h=h, q0=q0, q1=q1: e.tensor_tensor(out=OT[:, h, q0:q1], in0=pO[:, 0:w_], in1=rec[:, 0:w_], op=ALU.mult),
                          r=["pO", "rec"], w=["OT"])
            P.emit()
        with ExitStack() as es:
            def sb(name, shape, dt=F32):
                return es.enter_context(nc.sbuf_tensor(name, list(shape), dt))

            def ps(name, shape, dt=F32):
                return es.enter_context(nc.psum_tensor(name, list(shape), dt))
            Wo = sb("Wo", [128, KC, D], BF)
            oi = [sb("oi%d" % i, [128, 1], I32) for i in range(2)]
            s5g = [sb("s5g%d" % i, [128, 512], BF) for i in range(2)]
            s5T = [sb("s5T%d" % i, [128, 4, 128], BF) for i in range(2)]
            xt = [sb("xo%d" % i, [128, D]) for i in range(2)]; tmp = sb("tmpo", [128, D]); h1 = [sb("h1_%d" % i, [128, D]) for i in range(2)]
            pT = ps("pTo", [128, 4, 128], BF); pOut = [ps("pOut%d" % i, [128, D]) for i in range(2)]
            P.dma(POOL, Wo[:], I["a_w_out"].rearrange("(kc p) n -> p kc n", p=128), w=["Wo"])
            for t in range(NTO):
                b = t % 2; typ = 1 if t < 2 else 0
                rows = slice(t * 128, (t + 1) * 128)
                P.dma(SP, xt[b][:], I["xo"][rows, :], w=["xo%d" % b])
                P.dma(SP, oi[b][:], I["oidx"][rows, :], w=["oi%d" % b])
                P.add(POOL, lambda e, b=b: e.indirect_dma_start(out=s5g[b][:], out_offset=None, in_=S["S5"][:, :],
                                                                in_offset=bass.IndirectOffsetOnAxis(ap=oi[b][:, 0:1], axis=0)),
                      r=["oi%d" % b, "S5sc"], w=["s5g%d" % b], dma=True)
                for fc in range(4):
                    P.pe(lambda e, b=b, fc=fc: e.transpose(pT[:, fc, :], s5g[b][:, fc * 128:(fc + 1) * 128], K.ident[:]),
                         r=["s5g%d" % b, "ident"], w=["pTo"])
                P.act(lambda e, b=b: e.activation(out=s5T[b][:], in_=pT[:], func=AF.Copy), r=["pTo"], w=["s5T%d" % b])
                for half in range(2):
                    hs = slice(half * 512, (half + 1) * 512)
                    for ch in range(8):
                        lhs = s5T[b][:, ch, :] if ch < 4 else OT[:, ch - 4, rows]
                        P.pe(lambda e, b=b, ch=ch, hs=hs, lhs=lhs: e.matmul(pOut[b][:, hs], lhsT=lhs, rhs=Wo[:, ch, hs], start=(ch == 0), stop=(ch == 7)),
                             r=["s5T%d" % b, "OT", "Wo"], w=["pOut%d" % b])
                P.dve(lambda e, b=b, typ=typ: e.tensor_tensor(out=tmp[:], in0=pOut[b][:], in1=modv[:, typ, 2 * D:3 * D], op=ALU.mult),
                      r=["pOut%d" % b, "modv3"], w=["tmpo"])
                P.pool(lambda e, b=b: e.tensor_tensor(out=h1[b][:], in0=tmp[:], in1=xt[b][:], op=ALU.add), r=["tmpo", "xo%d" % b], w=["h1_%d" % b])
                P.dma(ACT, S["H1"][rows, :], h1[b][:], r=["h1_%d" % b], w=["H1sc"])
            P.emit()


def phase_ffn(nc, P, I, S, K, layer, Hin, Hout, n_ctx_tiles, ntiles):
    from contextlib import ExitStack
    tag = "f%d" % layer
    NBT = 3
    WCH = [(0, 6), (6, 12), (12, 17), (17, 22)]
    with ExitStack() as es:
        def sb(name, shape, dt=F32):
            return es.enter_context(nc.sbuf_tensor(tag + name, list(shape), dt))

        def ps(name, shape, dt=F32):
            return es.enter_context(nc.psum_tensor(tag + name, list(shape), dt))
        Wg = sb("Wg", [128, KC, HID], BF); Wu = sb("Wu", [128, KC, HID], BF); Wd = sb("Wd", [128, HC, D], BF)
        modv = sb("modv", [128, 2, 3 * D])
        xn = [sb("xn%d" % i, [128, D]) for i in range(2)]; xr = sb("xr", [128, D]); tmp = sb("tmp", [128, D]); a_bf = sb("a_bf", [128, D], BF)
        aT4 = [sb("aT4_%d" % i, [128, KC, NBT * 128], BF) for i in range(2)]; actT = sb("actT", [128, HC, NBT * 128], BF)
        sg = [sb("sg%d" % i, [128, NBT * 128]) for i in range(2)]
        st = sb("st", [128, 4])
        pTa = ps("pTa", [128, 8, 128], BF)
        pG = [ps("pG%d" % i, [128, 512]) for i in range(2)]; pU = [ps("pU%d" % i, [128, 512]) for i in range(2)]
        pD = ps("pD", [128, D])
        P.dma(SP, modv[:], S["mod"][layer][:, :, 3 * D:6 * D], r=["modsc%d" % layer], w=[tag + "modv"])
        wgv = I["ffn_w_gate"][layer].rearrange("(kc p) n -> p kc n", p=128)
        wuv = I["ffn_w_up"][layer].rearrange("(kc p) n -> p kc n", p=128)
        wdv = I["ffn_w_down"][layer].rearrange("(j p) n -> p j n", p=128)
        for ci, (j0, j1) in enumerate(WCH):
            cs_ = slice(j0 * 128, j1 * 128)
            P.dma(POOL, Wg[:, :, cs_], wgv[:, :, cs_], w=[tag + "Wg%d" % ci])
            P.dma(POOL, Wu[:, :, cs_], wuv[:, :, cs_], w=[tag + "Wu%d" % ci])
        for ci, (j0, j1) in enumerate(WCH):
            P.dma(POOL, Wd[:, j0:j1, :], wdv[:, j0:j1, :], w=[tag + "Wd%d" % ci])
        wch_of = {}
        for ci, (j0, j1) in enumerate(WCH):
            for j in range(j0, j1):
                wch_of[j] = ci
        batches = []
        t0 = 0
        while t0 < ntiles:
            nb = min(NBT, ntiles - t0); batches.append((t0, nb)); t0 += nb

        def norm_tile(bi, ti):
            t0, nb = batches[bi]
            pb = bi % 2
            t = t0 + ti; typ = 1 if t < n_ctx_tiles else 0
            rows = slice(t * 128, (t + 1) * 128)
            xb_ = t % 2
            xk = tag + "xn%d" % xb_
            P.dma(SP, xn[xb_][:], Hin[rows, :], w=[xk])
            norm_mod_T(P, K, xn[xb_][:], xk, modv[:, typ, D:2 * D], modv[:, typ, 0:D], tag + "modv",
                       tmp[:], st[:, 0:1], st[:, 1:2], tmp[:], a_bf[:], pTa, aT4[pb][:, :, ti * 128:(ti + 1) * 128], tag,
                       tag + "aT4_%d" % pb)
        for ti in range(batches[0][1]):
            norm_tile(0, ti)
        for bi, (t0, nb) in enumerate(batches):
            pb = bi % 2
            aT = aT4[pb]; aTk = tag + "aT4_%d" % pb
            ncol = nb * 128
            nxt = list(range(batches[bi + 1][1])) if bi + 1 < len(batches) else []
            for j in range(HC):
                b = j % 2
                js = slice(j * 128, (j + 1) * 128)
                ci = wch_of[j]
                for kc in range(KC):
                    P.pe(lambda e, b=b, kc=kc, js=js, ncol=ncol, aT=aT: e.matmul(pG[b][:, 0:ncol], lhsT=Wg[:, kc, js], rhs=aT[:, kc, 0:ncol],
                                                                                 start=(kc == 0), stop=(kc == KC - 1)),
                         r=[tag + "Wg%d" % ci, aTk], w=[tag + "pG%d" % b])
                for kc in range(KC):
                    P.pe(lambda e, b=b, kc=kc, js=js, ncol=ncol, aT=aT: e.matmul(pU[b][:, 0:ncol], lhsT=Wu[:, kc, js], rhs=aT[:, kc, 0:ncol],
                                                                                 start=(kc == 0), stop=(kc == KC - 1)),
                         r=[tag + "Wu%d" % ci, aTk], w=[tag + "pU%d" % b])
                P.act(lambda e, b=b, ncol=ncol: e.activation(out=sg[b][:, 0:ncol], in_=pG[b][:, 0:ncol], func=AF.Silu),
                      r=[tag + "pG%d" % b], w=[tag + "sg%d" % b])
                P.dve(lambda e, b=b, j=j, ncol=ncol: e.tensor_tensor(out=actT[:, j, 0:ncol], in0=sg[b][:, 0:ncol], in1=pU[b][:, 0:ncol], op=ALU.mult),
                      r=[tag + "sg%d" % b, tag + "pU%d" % b], w=[tag + "actT"])
                if nxt and j in (2, 9, 16):
                    norm_tile(bi + 1, nxt.pop(0))
            while nxt:
                norm_tile(bi + 1, nxt.pop(0))
            for ti in range(nb):
                t = t0 + ti; typ = 1 if t < n_ctx_tiles else 0
                rows = slice(t * 128, (t + 1) * 128)
                P.dma(SP, xr[:], Hin[rows, :], w=[tag + "xr"])
                for half in range(2):
                    hs = slice(half * 512, (half + 1) * 512)
                    for j in range(HC):
                        P.pe(lambda e, j=j, ti=ti, hs=hs: e.matmul(pD[:, hs], lhsT=actT[:, j, ti * 128:(ti + 1) * 128], rhs=Wd[:, j, hs],
                                                                   start=(j == 0), stop=(j == HC - 1)),
                             r=[tag + "actT", tag + "Wd%d" % wch_of[j]], w=[tag + "pD"])
                P.dve(lambda e, typ=typ: e.tensor_tensor(out=tmp[:], in0=pD[:], in1=modv[:, typ, 2 * D:3 * D], op=ALU.mult),
                      r=[tag + "pD", tag + "modv"], w=[tag + "tmp"])
                P.pool(lambda e: e.tensor_tensor(out=xr[:], in0=xr[:], in1=tmp[:], op=ALU.add),
                       r=[tag + "tmp", tag + "xr"], w=[tag + "xr"])
                P.dma(ACT, Hout[rows, :], xr[:], r=[tag + "xr"], w=[tag + "hout"])
        P.emit()


def phase4_win(nc, P, I, S, K):
    from contextlib import ExitStack
    with ExitStack() as es:
        def sb(name, shape, dt=F32):
            return es.enter_context(nc.sbuf_tensor("w_" + name, list(shape), dt))

        def ps(name, shape, dt=F32):
            return es.enter_context(nc.psum_tensor("w_" + name, list(shape), dt))

        def two(name, shape, dt=F32):
            return [sb(name + str(i), shape, dt) for i in range(2)]
        Wi = sb("Wi", [128, KC, 1536], BF); Wo = sb("Wo", [128, 16, D], BF)
        modv = sb("modv", [128, 2, 3 * D])
        gq = sb("gq", [128, 64]); gk = sb("gk", [128, 64]); esk = sb("esk", [128, 16])
        mtmp = sb("mtmp", [128, 128]); maskP = sb("maskP", [128, 128], BF); maskN = sb("maskN", [128, 128], BF)
        KT1 = sb("KT1", [128, 4, NOWN], BF); V1 = sb("V1", [128, NTO, 256], BF)
        xt = [sb("xt%d" % i, [128, D]) for i in range(3)]; rp = [sb("rp%d" % i, [128, 2, 64]) for i in range(2)]
        tmp2 = two("tmp", [128, D]); a_bf2 = two("a_bf", [128, D], BF)
        aT = [sb("aT%d" % i, [128, KC, 128], BF) for i in range(2)]
        st2 = two("st", [128, 64])
        qf2 = two("qf", [128, 16, 64]); qsq2 = two("qsq", [128, 16, 64]); t12 = two("t1", [128, 16, 64])
        Q_bf2 = two("Q_bf", [128, 16, 64], BF); QT = [sb("QT%d" % i, [128, 16, 128], BF) for i in range(3)]
        kf2 = two("kf", [128, 4, 64]); ksq2 = two("ksq", [128, 4, 64]); K_bf2 = two("K_bf", [128, 4, 64], BF)
        PT = [sb("PT%d" % i, [128, 512], BF) for i in range(3)]
        o_bf2 = two("o_bf", [128, 16, 128], BF); den2 = two("den", [128, 512])
        pBig = ps("pBig", [128, 4, 512]); pKVx = ps("pKVx", [128, 512]); pT1 = ps("pT1", [128, 8, 128], BF)
        pS = [ps("pS%d" % i, [128, 512]) for i in range(2)]

        P.dma(POOL, Wi[:], I["c_w_in"].rearrange("(kc p) n -> p kc n", p=128), w=["w_Wi"])
        P.dma(POOL, Wo[0:64, :, :], I["c_w_out"].rearrange("(h d) n -> d h n", d=64), w=["w_Wo"])
        P.dma(SP, modv[:], S["mod"][1][:, :, 0:3 * D], r=["modsc1"], w=["w_modv"])
        P.dma(SP, gq[:], bcast(I["c_q_norm"][0:1, :], [128, 64]), w=["w_gq"])
        P.dma(SP, gk[:], bcast(I["c_k_norm"][0:1, :], [128, 64]), w=["w_gk"])
        P.dma(SP, esk[:], bcast(I["c_sink"][0:1, :], [128, 16]), w=["w_esk"])
        P.dve(lambda e: e.tensor_scalar(out=gq[:], in0=gq[:], scalar1=WIN_SCALE, scalar2=None, op0=ALU.mult), r=["w_gq"], w=["w_gq"])
        P.act(lambda e: e.activation(out=esk[:], in_=esk[:], func=AF.Exp), r=["w_esk"], w=["w_esk"])
        for (mk, pat, cm) in ((maskP, [[-1, 128]], 1), (maskN, [[1, 128]], -1)):
            P.pool(lambda e: e.memset(mtmp[:], 1.0), r=["w_mtmp"], w=["w_mtmp"])
            P.pool(lambda e, pat=pat, cm=cm: e.affine_select(out=mtmp[:], in_=mtmp[:], pattern=pat, compare_op=ALU.is_ge, fill=0.0,
                                                             base=0, channel_multiplier=cm), r=["w_mtmp"], w=["w_mtmp"])
            P.pool(lambda e, mk=mk: e.tensor_copy(out=mk[:], in_=mtmp[:]), r=["w_mtmp"], w=["w_mask"])
        cnt = {"ps": 0, "pt": 0}

        def qkv_tile(t):
            b = t % 2; xb = t % 3; typ = 1 if t < 2 else 0
            sfx = "_%d" % b
            tmp, a_bf, st, qf, qsq, t1, Q_bf, kf, ksq, K_bf = (tmp2[b], a_bf2[b], st2[b], qf2[b], qsq2[b], t12[b], Q_bf2[b],
                                                              kf2[b], ksq2[b], K_bf2[b])
            rows = slice(t * 128, (t + 1) * 128)
            xk = "w_xt%d" % xb
            P.dma(SP, xt[xb][:], S["H2"][rows, :], w=[xk])
            P.dma(SP, rp[b][:], I["ropeC_o"][rows, :, :], w=["w_rp%d" % b])
            norm_mod_T(P, K, xt[xb][:], xk, modv[:, typ, D:2 * D], modv[:, typ, 0:D], "w_modv",
                       tmp[:], st[:, 0:1], st[:, 1:2], tmp[:], a_bf[:], pT1, aT[b][:], "w_" + sfx, "w_aT%d" % b, pTkey="w_pT")
            for kc in range(KC):
                P.pe(lambda e, kc=kc, b=b: e.matmul(pKVx[:], lhsT=aT[b][:, kc, :], rhs=Wi[:, kc, 1024:1536],
                                                    start=(kc == 0), stop=(kc == KC - 1)), r=["w_aT%d" % b, "w_Wi"], w=["w_pKV"])
            if t >= 2:
                for cb in range(2):
                    for kc in range(KC):
                        P.pe(lambda e, cb=cb, kc=kc, b=b: e.matmul(pBig[:, cb, :], lhsT=aT[b][:, kc, :], rhs=Wi[:, kc, cb * 512:(cb + 1) * 512],
                                                                   start=(kc == 0), stop=(kc == KC - 1)), r=["w_aT%d" % b, "w_Wi"], w=["w_pA"])
            P.act(lambda e: e.activation(out=kf[:], in_=pKVx[:, 0:256].rearrange("p (h d) -> p h d", h=4), func=AF.Copy),
                  r=["w_pKV"], w=["w_kf" + sfx])
            P.act(lambda e, t=t: e.activation(out=V1[:, t, :], in_=pKVx[:, 256:512], func=AF.Copy), r=["w_pKV"], w=["w_V1_%d" % t])
            if t >= 2:
                P.act(lambda e: e.activation(out=qf[:], in_=pBig[:, 0:2, :].rearrange("p a (h d) -> p (a h) d", d=64), func=AF.Copy),
                      r=["w_pA"], w=["w_qf" + sfx])
            P.dve(lambda e: e.tensor_tensor(out=ksq[:], in0=kf[:], in1=kf[:], op=ALU.mult), r=["w_kf" + sfx], w=["w_ksq" + sfx])
            P.dve(lambda e: e.reduce_sum(out=st[:, 4:8], in_=ksq[:], axis=AX.X), r=["w_ksq" + sfx], w=["w_kssq" + sfx])
            rstd_from_ssq(P, st[:, 4:8], st[:, 8:12], 4, 64, "w_k", sfx)
            P.dve(lambda e: e.tensor_tensor(out=kf[:], in0=kf[:], in1=bcast(st[:, 8:12].unsqueeze(2), [128, 4, 64]), op=ALU.mult),
                  r=["w_kf" + sfx, "w_krstd" + sfx], w=["w_kf" + sfx])
            P.pool(lambda e: e.tensor_tensor(out=kf[:], in0=kf[:], in1=bcast(gk[:].unsqueeze(1), [128, 4, 64]), op=ALU.mult),
                   r=["w_kf" + sfx, "w_gk"], w=["w_kf" + sfx])
            rope_inplace(P, P.pool, P.dve, kf[:], rp[b], t1[:, 0:4, :], 4, ["w_rp%d" % b], "w_kf" + sfx, "w_t1" + sfx)
            P.act(lambda e: e.activation(out=K_bf[:], in_=kf[:], func=AF.Copy), r=["w_kf" + sfx], w=["w_K_bf" + sfx])
            for h in range(4):
                P.pe(lambda e, h=h: e.transpose(pT1[0:64, h, :], K_bf[:, h, :], K.ident[:]), r=["w_K_bf" + sfx, "ident"], w=["w_pT"])
            P.dve(lambda e, rows=rows: e.tensor_copy(out=KT1[0:64, :, rows], in_=pT1[0:64, 0:4, :]), r=["w_pT"], w=["w_KT1_%d" % t])
            if t < 2:
                return
            P.dve(lambda e: e.tensor_tensor(out=qsq[:], in0=qf[:], in1=qf[:], op=ALU.mult), r=["w_qf" + sfx], w=["w_qsq" + sfx])
            P.dve(lambda e: e.reduce_sum(out=st[:, 16:32], in_=qsq[:], axis=AX.X), r=["w_qsq" + sfx], w=["w_qssq" + sfx])
            rstd_from_ssq(P, st[:, 16:32], st[:, 32:48], 16, 64, "w_q", sfx)
            P.dve(lambda e: e.tensor_tensor(out=qf[:], in0=qf[:], in1=bcast(st[:, 32:48].unsqueeze(2), [128, 16, 64]), op=ALU.mult),
                  r=["w_qf" + sfx, "w_qrstd" + sfx], w=["w_qf" + sfx])
            P.pool(lambda e: e.tensor_tensor(out=qf[:], in0=qf[:], in1=bcast(gq[:].unsqueeze(1), [128, 16, 64]), op=ALU.mult),
                   r=["w_qf" + sfx, "w_gq"], w=["w_qf" + sfx])
            rope_inplace(P, P.pool, P.dve, qf[:], rp[b], t1[:], 16, ["w_rp%d" % b], "w_qf" + sfx, "w_t1" + sfx)
            P.act(lambda e: e.activation(out=Q_bf[:], in_=qf[:], func=AF.Copy), r=["w_qf" + sfx], w=["w_Q_bf" + sfx])
            for hh in range(2):
                for h in range(8):
                    P.pe(lambda e, h=h, hh=hh: e.transpose(pT1[0:64, h, :], Q_bf[:, 8 * hh + h, :], K.ident[:]),
                         r=["w_Q_bf" + sfx, "ident"], w=["w_pT"])
                P.dve(lambda e, t=t, hh=hh: e.tensor_copy(out=QT[t % 3][0:64, 8 * hh:8 * hh + 8, :], in_=pT1[0:64, :, :]),
                      r=["w_pT"], w=["w_QT%d" % (t % 3)])

        def attn_tile(t):
            i = t - 2
            b = t % 2
            sfx = "_%d" % b
            o_bf, den, tmp = o_bf2[b], den2[b], tmp2[b]
            qt = QT[t % 3]; qk = "w_QT%d" % (t % 3)
            keyt = [(0, None), (1, None)]
            if i >= 1:
                keyt.append((t - 1, maskP))
            keyt.append((t, None))
            if i <= NTO - 4:
                keyt.append((t + 1, maskN))
            nk = len(keyt)
            pO = pBig[0:64, 2, :]; pDen = pBig[0:64, 3, :]
            for kh in range(4):
                pend = None
                for n_ in range(nk + 1):
                    cur = None
                    if n_ < nk:
                        kt, mk = keyt[n_]
                        sbi = cnt["ps"] % 2; cnt["ps"] += 1
                        pi = cnt["pt"] % 3; cnt["pt"] += 1
                        P.pe(lambda e, sbi=sbi, kt=kt, kh=kh: e.matmul(pS[sbi][:], lhsT=KT1[0:64, kh, kt * 128:(kt + 1) * 128],
                                                                        rhs=qt[0:64, 4 * kh:4 * kh + 4, :], start=True, stop=True),
                             r=["w_KT1_%d" % kt, qk], w=["w_pS%d" % sbi])
                        P.act(lambda e, sbi=sbi, pi=pi: e.activation(out=PT[pi][:], in_=pS[sbi][:], func=AF.Exp),
                              r=["w_pS%d" % sbi], w=["w_PT%d" % pi])
                        if mk is not None:
                            P.dve(lambda e, pi=pi, mk=mk: e.tensor_tensor(out=PT[pi][:].rearrange("p (g q) -> p g q", g=4),
                                                                          in0=PT[pi][:].rearrange("p (g q) -> p g q", g=4),
                                                                          in1=bcast(mk[:].unsqueeze(1), [128, 4, 128]), op=ALU.mult),
                                  r=["w_PT%d" % pi, "w_mask"], w=["w_PT%d" % pi])
                        cur = (n_, kt, pi)
                    if pend is not None:
                        n0, k0, p0 = pend
                        P.pe(lambda e, n0=n0, k0=k0, p0=p0, kh=kh: e.matmul(pO, lhsT=V1[:, k0, kh * 64:(kh + 1) * 64], rhs=PT[p0][:],
                                                                            start=(n0 == 0), stop=(n0 == nk - 1)),
                             r=["w_V1_%d" % k0, "w_PT%d" % p0], w=["w_pB"])
                        P.pe(lambda e, n0=n0, p0=p0: e.matmul(pDen, lhsT=K.ones_bf[:, 0:64], rhs=PT[p0][:], start=(n0 == 0), stop=(n0 == nk - 1)),
                             r=["ones_bf", "w_PT%d" % p0], w=["w_pC"])
                    pend = cur
                P.dve(lambda e, kh=kh: e.tensor_tensor(out=den[0:64, :].rearrange("p (g q) -> p g q", g=4),
                                                       in0=pDen.rearrange("p (g q) -> p g q", g=4),
                                                       in1=bcast(esk[0:64, 4 * kh:4 * kh + 4].unsqueeze(2), [64, 4, 128]), op=ALU.add),
                      r=["w_pC", "w_esk"], w=["w_den" + sfx])
                P.dve(lambda e: e.reciprocal(out=den[0:64, :], in_=den[0:64, :]), r=["w_den" + sfx], w=["w_den" + sfx])
                P.dve(lambda e, kh=kh: e.tensor_tensor(out=o_bf[0:64, 4 * kh:4 * kh + 4, :].rearrange("p g q -> p (g q)"), in0=pO,
                                                       in1=den[0:64, :], op=ALU.mult), r=["w_pB", "w_den" + sfx], w=["w_o_bf" + sfx])
            for half in range(2):
                for h in range(16):
                    P.pe(lambda e, half=half, h=h: e.matmul(pBig[:, half, :], lhsT=o_bf[0:64, h, :], rhs=Wo[0:64, h, half * 512:(half + 1) * 512],
                                                            start=(h == 0), stop=(h == 15)), r=["w_o_bf" + sfx, "w_Wo"], w=["w_pA"])
            hb = i % 2
            P.dve(lambda e: e.tensor_tensor(out=tmp[:], in0=pBig[:, 0:2, :].rearrange("p a c -> p (a c)"), in1=modv[:, 0, 2 * D:3 * D], op=ALU.mult),
                  r=["w_pA", "w_modv"], w=["w_" + sfx + "tmp"])
            P.pool(lambda e, t=t: e.tensor_tensor(out=tmp[:], in0=tmp[:], in1=xt[t % 3][:], op=ALU.add),
                   r=["w_" + sfx + "tmp", "w_xt%d" % (t % 3)], w=["w_" + sfx + "tmp"])
            P.dma(ACT, S["H3"][i * 128:(i + 1) * 128, :], tmp[:], r=["w_" + sfx + "tmp"], w=["H3sc"])

        order = []
        for t in range(NTO):
            order.append(("q", t))
            if t - 1 >= 2:
                order.append(("a", t - 1))
        order.append(("a", NTO - 1))
        if P4_STAGED:
            tiles = [P.capture((lambda t=t: qkv_tile(t)) if k == "q" else (lambda t=t: attn_tile(t))) for (k, t) in order]
            nst = sorted(len(x) for x in tiles)[len(tiles) // 2]
            K.p4_info = (nst, [len(x) for x in tiles[:8]])
            P.run_staged(tiles, skew=max(1, nst // P4_DIV))
        else:
            for (k, t) in order:
                (qkv_tile if k == "q" else attn_tile)(t)
        P.emit()
```
